# Optimizing a Trainium2 kernel written in Bass

```python
import math, functools
import jax, jax.numpy as jnp
from jax import lax
import numpy as np

D_MODEL = 1024
BATCH = 8
SEQ = 8192
DEPTH = 2

GRID_W = 64
CTX_LEN = 256
N_EVEN = (DEPTH + 1) // 2
N_ODD = DEPTH // 2
N_MOD = 6
NORM_EPS = 1e-6

RWKV_HEADS = 8
RWKV_HEAD_DIM = 64
RWKV_WIDTH = RWKV_HEADS * RWKV_HEAD_DIM
DECAY_LORA = 64
ICLR_LORA = 64
GATE_LORA = 128
RWKV_GN_EPS = 64e-5
RWKV_SPLITS = (RWKV_WIDTH, RWKV_WIDTH, RWKV_WIDTH, DECAY_LORA, DECAY_LORA, ICLR_LORA, ICLR_LORA, GATE_LORA)
RWKV_COLS = sum(RWKV_SPLITS)

SSD_HEADS = 8
SSD_HEAD_DIM = 64
SSD_WIDTH = SSD_HEADS * SSD_HEAD_DIM
SSD_GROUPS = 2
SSD_STATE = 128
SSD_CONV = 3
SSD_CHUNK = 128
SSD_XBC = SSD_WIDTH + 2 * SSD_GROUPS * SSD_STATE
SSD_SPLITS = (SSD_WIDTH, SSD_XBC, SSD_HEADS, SSD_HEADS)
SSD_COLS = sum(SSD_SPLITS)

EVEN_IN = RWKV_COLS + SSD_COLS
MIX_WIDTH = RWKV_WIDTH + SSD_WIDTH

RET_HEADS = 8
RET_QK_DIM = 128
RET_V_DIM = 256
RET_CHUNK = 128
RET_V = RET_HEADS * RET_V_DIM
RET_SPLITS = (RET_HEADS * RET_QK_DIM, RET_HEADS * RET_QK_DIM, RET_V, RET_V)
RET_IN = sum(RET_SPLITS)
ROPE_BASE = 10000.0

D_FF = 2816
FFN_CONV = 3

kernel_name = 'hybrid_rwkv7_ssd_retention_dit'


def split_last(x, sizes):
    return jnp.split(x, np.cumsum(sizes)[:-1].tolist(), axis=-1)


def to_heads(t, n_heads):
    return t.reshape(t.shape[:-1] + (n_heads, t.shape[-1] // n_heads))


def rms_norm(x, g=None, eps=NORM_EPS):
    xf = x.astype(jnp.float32)
    y = xf * lax.rsqrt(jnp.mean(xf * xf, axis=-1, keepdims=True) + eps)
    if g is not None:
        y = y * g.astype(jnp.float32)
    return y.astype(x.dtype)


def modulate(h, shift, scale):
    return h * (1 + scale) + shift


def depthwise_conv_seq(x, w, b):
    k = w.shape[0]
    y = lax.conv_general_dilated(x, w[:, None, :], (1,), [(k // 2, k // 2)],
                                 dimension_numbers=('NWC', 'WIO', 'NWC'), feature_group_count=x.shape[-1])
    return y + b


def depthwise_conv_grid(x, w, b):
    bsz, t, ch = x.shape
    rows = t // GRID_W
    k = w.shape[0]
    y = lax.conv_general_dilated(x.reshape(bsz, rows, GRID_W, ch), w[:, :, None, :], (1, 1),
                                 [(k // 2, k // 2), (k // 2, k // 2)],
                                 dimension_numbers=('NHWC', 'HWIO', 'NHWC'), feature_group_count=ch)
    return y.reshape(bsz, t, ch) + b


def token_shift_bidir(u, mu_prev, mu_next):
    prev = jnp.pad(u, ((0, 0), (1, 0), (0, 0)))[:, :-1]
    nxt = jnp.pad(u, ((0, 0), (0, 1), (0, 0)))[:, 1:]
    return u + mu_prev * (prev - u) + mu_next * (nxt - u)


def rope_2d(x):
    t, dk = x.shape[1], x.shape[-1]
    n = dk // 4
    pos = jnp.arange(t)
    row = (pos // GRID_W).astype(jnp.float32)
    col = (pos % GRID_W).astype(jnp.float32)
    inv = ROPE_BASE ** (-jnp.arange(n, dtype=jnp.float32) / n)
    ang = jnp.concatenate([row[:, None] * inv, col[:, None] * inv], axis=-1)[:, None, :]
    cos, sin = jnp.cos(ang).astype(x.dtype), jnp.sin(ang).astype(x.dtype)
    x1, x2 = jnp.split(x, 2, axis=-1)
    return jnp.concatenate([x1 * cos - x2 * sin, x2 * cos + x1 * sin], axis=-1)


def run_bidirectional(scan_f, scan_b, ctx_f, ctx_b, lat_f, lat_b, state0):
    flip = lambda ts: tuple(jnp.flip(t, 1) for t in ts)
    yc_f, sc_f = scan_f(*ctx_f, state0)
    yl_f, _ = scan_f(*lat_f, sc_f)
    yc_b, sc_b = scan_b(*flip(ctx_b), state0)
    yl_b, _ = scan_b(*flip(lat_b), sc_b)
    return yc_f + jnp.flip(yc_b, 1), yl_f + jnp.flip(yl_b, 1)


def rwkv7_scan(r, decay, k, v, a, b, s0):
    def step(s, inp):
        r_t, w_t, k_t, v_t, a_t, b_t = inp
        sa = jnp.einsum('bhvk,bhk->bhv', s, a_t)
        s = s * w_t[:, :, None, :] + sa[..., None] * b_t[:, :, None, :] + v_t[..., None] * k_t[:, :, None, :]
        return s, jnp.einsum('bhvk,bhk->bhv', s, r_t)
    xs = tuple(jnp.moveaxis(t.astype(jnp.float32), 1, 0) for t in (r, decay, k, v, a, b))
    s, ys = lax.scan(step, s0.astype(jnp.float32), xs)
    return jnp.moveaxis(ys, 0, 1), s


def ssd_scan(x, dt, bm, cm, h0, A):
    f32 = jnp.float32
    bsz, t, nh, hp = x.shape
    ng, ns = bm.shape[2], bm.shape[3]
    hg = nh // ng
    L = SSD_CHUNK
    nc = t // L
    x = x.astype(f32).reshape(bsz, nc, L, ng, hg, hp)
    dt = dt.astype(f32).reshape(bsz, nc, L, ng, hg)
    bm = bm.astype(f32).reshape(bsz, nc, L, ng, ns)
    cm = cm.astype(f32).reshape(bsz, nc, L, ng, ns)
    cs = jnp.cumsum(dt * A.astype(f32).reshape(ng, hg), axis=2)
    xdt = x * dt[..., None]
    causal = jnp.tril(jnp.ones((L, L), bool))[:, :, None, None]
    seg = cs[:, :, :, None] - cs[:, :, None, :]
    decay = jnp.exp(jnp.where(causal, seg, -jnp.inf))
    cb = jnp.einsum('bclgn,bcsgn->bclsg', cm, bm)
    y = jnp.einsum('bclsg,bclsgh,bcsghp->bclghp', cb, decay, xdt)
    states = jnp.einsum('bcsgn,bcsgh,bcsghp->bcghpn', bm, jnp.exp(cs[:, :, -1:] - cs), xdt)
    def step(h, inp):
        s_c, a_c = inp
        return a_c[..., None, None] * h + s_c, h
    h_t, h_prev = lax.scan(step, h0.astype(f32).reshape(bsz, ng, hg, hp, ns),
                           (jnp.moveaxis(states, 1, 0), jnp.moveaxis(jnp.exp(cs[:, :, -1]), 1, 0)))
    h_prev = jnp.moveaxis(h_prev, 0, 1)
    y = y + jnp.einsum('bclgn,bcghpn,bclgh->bclghp', cm, h_prev, jnp.exp(cs))
    return y.reshape(bsz, t, nh, hp), h_t.reshape(bsz, nh, hp, ns)


def retention_scan(q, k, v, s0, log_gamma):
    f32 = jnp.float32
    bsz, t, nh, dk = q.shape
    dv = v.shape[-1]
    L = RET_CHUNK
    nc = t // L
    q = q.astype(f32).reshape(bsz, nc, L, nh, dk)
    k = k.astype(f32).reshape(bsz, nc, L, nh, dk)
    v = v.astype(f32).reshape(bsz, nc, L, nh, dv)
    lg = log_gamma.astype(f32)
    pos = jnp.arange(L, dtype=f32)
    rel = pos[:, None] - pos[None, :]
    inner = jnp.where(rel[..., None] >= 0, jnp.exp(jnp.maximum(rel, 0.0)[..., None] * lg), 0.0)
    scores = jnp.einsum('bclhd,bcshd->bclsh', q, k) * inner
    y = jnp.einsum('bclsh,bcshe->bclhe', scores, v)
    k_end = k * jnp.exp((L - 1 - pos)[:, None] * lg)[:, :, None]
    states = jnp.einsum('bcshd,bcshe->bchde', k_end, v)
    chunk_decay = jnp.exp(L * lg)[:, None, None]
    def step(s, s_c):
        return chunk_decay * s + s_c, s
    s_t, s_prev = lax.scan(step, s0.astype(f32), jnp.moveaxis(states, 1, 0))
    s_prev = jnp.moveaxis(s_prev, 0, 1)
    q_dec = q * jnp.exp((pos + 1)[:, None] * lg)[:, :, None]
    y = y + jnp.einsum('bclhd,bchde->bclhe', q_dec, s_prev)
    return y.reshape(bsz, t, nh, dv), s_t


def even_features(h, p):
    f32 = jnp.float32
    proj = h @ p['w_in']
    rw = token_shift_bidir(proj[..., :RWKV_COLS], p['mu_prev'], p['mu_next'])
    r, k, v, wd_f, wd_b, ad_f, ad_b, gd = split_last(rw, RWKV_SPLITS)
    rh, vh, kh = to_heads(r, RWKV_HEADS), to_heads(v, RWKV_HEADS), to_heads(k, RWKV_HEADS)
    kk = to_heads(k * p['k_k'], RWKV_HEADS).astype(f32)
    kk = kk / jnp.maximum(jnp.sqrt(jnp.sum(kk * kk, axis=-1, keepdims=True)), 1e-12)
    def direction(wd, ad, w0, w2, a0, a2):
        logw = (-jax.nn.softplus(-(w0 + jnp.tanh(wd) @ w2)) - 0.5).astype(f32)
        iclr = jax.nn.sigmoid(a0 + ad @ a2)
        k_dir = to_heads(k * (1 + (iclr - 1) * p['k_a']), RWKV_HEADS)
        iclr = to_heads(iclr, RWKV_HEADS)
        return (rh, to_heads(jnp.exp(-jnp.exp(logw)), RWKV_HEADS), k_dir, vh, -kk, kk * iclr)
    rw_f = direction(wd_f, ad_f, p['w0_f'], p['w2_f'], p['a0_f'], p['a2_f'])
    rw_b = direction(wd_b, ad_b, p['w0_b'], p['w2_b'], p['a0_b'], p['a2_b'])
    bonus = jnp.sum(rh * kh * p['r_k'], axis=-1, keepdims=True) * vh
    gate = jax.nn.sigmoid(gd) @ p['g2']
    z, xbc, dtr_f, dtr_b = split_last(proj[..., RWKV_COLS:], SSD_SPLITS)
    xbc = jax.nn.silu(depthwise_conv_seq(xbc, p['conv_w'], p['conv_b']))
    xs, bm, cm = split_last(xbc, (SSD_WIDTH, SSD_GROUPS * SSD_STATE, SSD_GROUPS * SSD_STATE))
    xs = to_heads(xs, SSD_HEADS)
    bm = to_heads(bm, SSD_GROUPS)
    cm = to_heads(cm, SSD_GROUPS)
    dt_f = jax.nn.softplus(dtr_f + p['dt_bias_f'])
    dt_b = jax.nn.softplus(dtr_b + p['dt_bias_b'])
    return {'rwkv_f': rw_f, 'rwkv_b': rw_b, 'bonus': bonus, 'gate': gate, 'z': z, 'xs': xs,
            'ssd_f': (xs, dt_f, bm, cm), 'ssd_b': (xs, dt_b, bm, cm)}


def even_mixer(hc, hx, p, need_ctx):
    f32 = jnp.float32
    fc, fx = even_features(hc, p), even_features(hx, p)
    bsz = hx.shape[0]
    s0 = jnp.zeros((bsz, RWKV_HEADS, RWKV_HEAD_DIM, RWKV_HEAD_DIM), f32)
    rk_c, rk_x = run_bidirectional(rwkv7_scan, rwkv7_scan, fc['rwkv_f'], fc['rwkv_b'],
                                   fx['rwkv_f'], fx['rwkv_b'], s0)
    h0 = jnp.zeros((bsz, SSD_HEADS, SSD_HEAD_DIM, SSD_STATE), f32)
    scan_f = functools.partial(ssd_scan, A=-jnp.exp(p['A_log_f'].astype(f32)))
    scan_b = functools.partial(ssd_scan, A=-jnp.exp(p['A_log_b'].astype(f32)))
    sd_c, sd_x = run_bidirectional(scan_f, scan_b, fc['ssd_f'], fc['ssd_b'], fx['ssd_f'], fx['ssd_b'], h0)

    def finish(f, y_rk, y_sd, dtype):
        mean = jnp.mean(y_rk, axis=-1, keepdims=True)
        var = jnp.var(y_rk, axis=-1, keepdims=True)
        y_rk = (y_rk - mean) * lax.rsqrt(var + RWKV_GN_EPS)
        y_rk = y_rk * p['ln_w'].reshape(RWKV_HEADS, RWKV_HEAD_DIM) + p['ln_b'].reshape(RWKV_HEADS, RWKV_HEAD_DIM)
        y_rk = (y_rk + f['bonus']).reshape(y_rk.shape[:-2] + (RWKV_WIDTH,)) * f['gate']
        y_sd = y_sd + p['D'][:, None] * f['xs']
        y_sd = rms_norm(y_sd.reshape(y_sd.shape[:-2] + (SSD_WIDTH,)) * jax.nn.silu(f['z']), p['norm_w'])
        return jnp.concatenate([y_rk, y_sd], axis=-1).astype(dtype) @ p['w_out']

    yx = finish(fx, rk_x, sd_x, hx.dtype)
    yc = finish(fc, rk_c, sd_c, hc.dtype) if need_ctx else None
    return yc, yx


def odd_features(h, p, on_grid):
    q, k, v, g = split_last(h @ p['w_in'], RET_SPLITS)
    q = to_heads(q, RET_HEADS)
    k = to_heads(k, RET_HEADS) * (RET_QK_DIM ** -0.5)
    if on_grid:
        q, k = rope_2d(q), rope_2d(k)
    return (q, k, to_heads(v, RET_HEADS)), g


def odd_mixer(hc, hx, p, need_ctx):
    f32 = jnp.float32
    (sc, gc), (sx, gx) = odd_features(hc, p, False), odd_features(hx, p, True)
    lg_f = jnp.log1p(-jnp.exp2(-p['log2_f'].astype(f32)))
    lg_b = jnp.log1p(-jnp.exp2(-p['log2_b'].astype(f32)))
    s0 = jnp.zeros((hx.shape[0], RET_HEADS, RET_QK_DIM, RET_V_DIM), f32)
    yc, yx = run_bidirectional(functools.partial(retention_scan, log_gamma=lg_f),
                               functools.partial(retention_scan, log_gamma=lg_b), sc, sc, sx, sx, s0)

    def finish(y, g, dtype):
        y = rms_norm(y).reshape(y.shape[:-2] + (RET_V,))
        return (jax.nn.silu(g) * y).astype(dtype) @ p['w_out']

    out_x = finish(yx, gx, hx.dtype)
    out_c = finish(yc, gc, hc.dtype) if need_ctx else None
    return out_c, out_x


def conv_ffn(h, w_up, conv_w, conv_b, w_down, on_grid):
    gate, val = jnp.split(h @ w_up, 2, axis=-1)
    if on_grid:
        gate = depthwise_conv_grid(gate, conv_w, conv_b)
    else:
        gate = depthwise_conv_seq(gate, conv_w[FFN_CONV // 2], conv_b)
    return (jax.nn.gelu(gate) * val) @ w_down


def setup_inputs(seed: int = 0) -> dict:
    key = jax.random.key(seed)
    keys = iter(jax.random.split(key, 64))
    f32 = jnp.float32
    def nrm(shape, scale):
        return scale * jax.random.normal(next(keys), shape, f32)
    def uni(shape, lo, hi):
        return jax.random.uniform(next(keys), shape, f32, lo, hi)
    def dt_bias(shape):
        dt = jnp.exp(uni(shape, math.log(1e-3), math.log(1e-1)))
        return dt + jnp.log(-jnp.expm1(-dt))
    D = D_MODEL
    E, O = N_EVEN, N_ODD
    ret_base = 5.0 + jnp.arange(RET_HEADS, dtype=f32)[None, :]
    return {
        'x': nrm((BATCH, SEQ, D), 1.0),
        'c': nrm((BATCH, D), 1.0),
        'ctx': nrm((BATCH, CTX_LEN, D), 1.0),
        'c_ctx': nrm((D,), 1.0),
        'mod_w': nrm((DEPTH, D, N_MOD * D), 0.3 * D ** -0.5),
        'mod_b': nrm((DEPTH, N_MOD * D), 0.02),
        'norm1_g': 1.0 + nrm((DEPTH, D), 0.02),
        'norm2_g': 1.0 + nrm((DEPTH, D), 0.02),
        'ffn_w_up': nrm((DEPTH, D, 2 * D_FF), D ** -0.5),
        'ffn_conv_w': nrm((DEPTH, FFN_CONV, FFN_CONV, D_FF), 1.0 / FFN_CONV),
        'ffn_conv_b': nrm((DEPTH, D_FF), 0.02),
        'ffn_w_down': nrm((DEPTH, D_FF, D), D_FF ** -0.5),
        'ev_w_in': nrm((E, D, EVEN_IN), D ** -0.5),
        'ev_mu_prev': uni((E, RWKV_COLS), 0.0, 0.5),
        'ev_mu_next': uni((E, RWKV_COLS), 0.0, 0.5),
        'rk_w0_f': uni((E, RWKV_WIDTH), -6.0, -1.0),
        'rk_w0_b': uni((E, RWKV_WIDTH), -6.0, -1.0),
        'rk_w2_f': nrm((E, DECAY_LORA, RWKV_WIDTH), 0.5 * DECAY_LORA ** -0.5),
        'rk_w2_b': nrm((E, DECAY_LORA, RWKV_WIDTH), 0.5 * DECAY_LORA ** -0.5),
        'rk_a0_f': nrm((E, RWKV_WIDTH), 0.1),
        'rk_a0_b': nrm((E, RWKV_WIDTH), 0.1),
        'rk_a2_f': nrm((E, ICLR_LORA, RWKV_WIDTH), 0.5 * ICLR_LORA ** -0.5),
        'rk_a2_b': nrm((E, ICLR_LORA, RWKV_WIDTH), 0.5 * ICLR_LORA ** -0.5),
        'rk_g2': nrm((E, GATE_LORA, RWKV_WIDTH), GATE_LORA ** -0.5),
        'rk_k_k': 0.85 + nrm((E, RWKV_WIDTH), 0.05),
        'rk_k_a': 1.0 + nrm((E, RWKV_WIDTH), 0.05),
        'rk_r_k': nrm((E, RWKV_HEADS, RWKV_HEAD_DIM), 0.1),
        'rk_ln_w': 1.0 + nrm((E, RWKV_WIDTH), 0.02),
        'rk_ln_b': nrm((E, RWKV_WIDTH), 0.02),
        'ssd_conv_w': nrm((E, SSD_CONV, SSD_XBC), SSD_CONV ** -0.5),
        'ssd_conv_b': nrm((E, SSD_XBC), 0.02),
        'ssd_dt_bias_f': dt_bias((E, SSD_HEADS)),
        'ssd_dt_bias_b': dt_bias((E, SSD_HEADS)),
        'ssd_a_log_f': jnp.log(uni((E, SSD_HEADS), 1.0, 16.0)),
        'ssd_a_log_b': jnp.log(uni((E, SSD_HEADS), 1.0, 16.0)),
        'ssd_d': 1.0 + nrm((E, SSD_HEADS), 0.1),
        'ssd_norm_w': 1.0 + nrm((E, SSD_WIDTH), 0.02),
        'ev_w_out': nrm((E, MIX_WIDTH, D), MIX_WIDTH ** -0.5),
        'ret_w_in': nrm((O, D, RET_IN), D ** -0.5),
        'ret_log2_f': ret_base + uni((O, RET_HEADS), -0.3, 0.3),
        'ret_log2_b': ret_base + uni((O, RET_HEADS), -0.3, 0.3),
        'ret_w_out': nrm((O, RET_V, D), RET_V ** -0.5),
        'final_norm_g': 1.0 + nrm((D,), 0.02),
    }


def reference(x, c, ctx, c_ctx, mod_w, mod_b, norm1_g, norm2_g, ffn_w_up, ffn_conv_w, ffn_conv_b,
              ffn_w_down, ev_w_in, ev_mu_prev, ev_mu_next, rk_w0_f, rk_w0_b, rk_w2_f, rk_w2_b, rk_a0_f,
              rk_a0_b, rk_a2_f, rk_a2_b, rk_g2, rk_k_k, rk_k_a, rk_r_k, rk_ln_w, rk_ln_b, ssd_conv_w,
              ssd_conv_b, ssd_dt_bias_f, ssd_dt_bias_b, ssd_a_log_f, ssd_a_log_b, ssd_d, ssd_norm_w,
              ev_w_out, ret_w_in, ret_log2_f, ret_log2_b, ret_w_out, final_norm_g):
    cond_x = jax.nn.silu(c)[:, None, :]
    cond_c = jax.nn.silu(c_ctx)[None, None, :]
    for i in range(DEPTH):
        need_ctx = i < DEPTH - 1
        mx = jnp.split(cond_x @ mod_w[i] + mod_b[i], N_MOD, axis=-1)
        mc = jnp.split(cond_c @ mod_w[i] + mod_b[i], N_MOD, axis=-1)
        hx = modulate(rms_norm(x, norm1_g[i]), mx[0], mx[1])
        hc = modulate(rms_norm(ctx, norm1_g[i]), mc[0], mc[1])
        j = i // 2
        if i % 2 == 0:
            p = {'w_in': ev_w_in[j], 'mu_prev': ev_mu_prev[j], 'mu_next': ev_mu_next[j],
                 'w0_f': rk_w0_f[j], 'w0_b': rk_w0_b[j], 'w2_f': rk_w2_f[j], 'w2_b': rk_w2_b[j],
                 'a0_f': rk_a0_f[j], 'a0_b': rk_a0_b[j], 'a2_f': rk_a2_f[j], 'a2_b': rk_a2_b[j],
                 'g2': rk_g2[j], 'k_k': rk_k_k[j], 'k_a': rk_k_a[j], 'r_k': rk_r_k[j],
                 'ln_w': rk_ln_w[j], 'ln_b': rk_ln_b[j], 'conv_w': ssd_conv_w[j], 'conv_b': ssd_conv_b[j],
                 'dt_bias_f': ssd_dt_bias_f[j], 'dt_bias_b': ssd_dt_bias_b[j],
                 'A_log_f': ssd_a_log_f[j], 'A_log_b': ssd_a_log_b[j], 'D': ssd_d[j],
                 'norm_w': ssd_norm_w[j], 'w_out': ev_w_out[j]}
            yc, yx = even_mixer(hc, hx, p, need_ctx)
        else:
            p = {'w_in': ret_w_in[j], 'log2_f': ret_log2_f[j], 'log2_b': ret_log2_b[j], 'w_out': ret_w_out[j]}
            yc, yx = odd_mixer(hc, hx, p, need_ctx)
        x = x + mx[2] * yx
        x = x + mx[5] * conv_ffn(modulate(rms_norm(x, norm2_g[i]), mx[3], mx[4]),
                                 ffn_w_up[i], ffn_conv_w[i], ffn_conv_b[i], ffn_w_down[i], True)
        if need_ctx:
            ctx = ctx + mc[2] * yc
            ctx = ctx + mc[5] * conv_ffn(modulate(rms_norm(ctx, norm2_g[i]), mc[3], mc[4]),
                                         ffn_w_up[i], ffn_conv_w[i], ffn_conv_b[i], ffn_w_down[i], False)
    return rms_norm(x, final_norm_g)
```

```python
import contextlib
import numpy as np
import concourse.bass as bass
import concourse.mybir as mybir
from concourse.bass_utils import run_bass_kernel_spmd

F32 = mybir.dt.float32
BF16 = mybir.dt.bfloat16
AF = mybir.ActivationFunctionType
ALU = mybir.AluOpType
AX = mybir.AxisListType

D = 1024
TC = 256
GW = 64
NDMA = 40
SELF_SKIP = ("pe", "sp", "pool")
READ_ARGS = ("in_", "in0", "in1", "lhsT", "rhs", "scalar1", "scalar2", "scalar", "bias", "scale",
             "data0", "data1", "initial", "identity", "mask", "on_true", "on_false")
WRITE_ARGS = ("out", "accum_out", "ap")


class PB:
    ENG = ("pe", "act", "dve", "pool", "sp")

    def __init__(self):
        self.nc = nc = bass.Bass("TRN2", target_bir_lowering=False)
        self.es = contextlib.ExitStack()
        self.eng = dict(pe=nc.tensor, act=nc.scalar, dve=nc.vector, pool=nc.gpsimd, sp=nc.sync)
        self.sems = {}
        for e in self.ENG:
            self.sems[e] = self.es.enter_context(nc.semaphore("sem_" + e))
        for k in range(NDMA):
            self.sems[("d", k)] = self.es.enter_context(nc.semaphore(f"semd{k}"))
        self.cnt = {e: 0 for e in self.ENG}
        self.duse = [0] * NDMA
        self.dq = {"sp": list(range(0, 28)), "pool": list(range(28, 36)), "act": list(range(36, 40))}
        self.dqi = {"sp": 0, "pool": 0, "act": 0}
        self.waited = {e: {} for e in self.ENG}
        self.lastw = {}
        self.rds = {}
        self.uid = 0
        self.stack = [self.es]
        self.psb = [self.es.enter_context(nc.psum_tensor(f"psb{i}", [128, 512], F32)) for i in range(8)]
        self.psi = 0
        self.ninstr = 0

    def sb(self, name, shape, dt=F32):
        self.uid += 1
        return self.stack[-1].enter_context(self.nc.sbuf_tensor(f"{name}_{self.uid}", list(shape), dt))

    def pool_of(self, name, shape, dt, n):
        return Ring([self.sb(name, shape, dt) for _ in range(n)])

    @contextlib.contextmanager
    def phase(self):
        st = contextlib.ExitStack()
        self.stack.append(st)
        try:
            yield
        finally:
            self.barrier()
            self.stack.pop()
            st.close()

    def ps(self):
        t = self.psb[self.psi]
        self.psi = (self.psi + 1) % 8
        return t

    def _wait(self, e, tok):
        sk, v = tok
        if sk == e and e in SELF_SKIP:
            return
        if self.waited[e].get(sk, 0) >= v:
            return
        self.eng[e].wait_ge(self.sems[sk], v)
        self.waited[e][sk] = v
        self.ninstr += 1

    def _deps(self, e, rd, wr):
        for k in rd:
            t = self.lastw.get(k)
            if t:
                self._wait(e, t)
            if k.startswith("psb"):
                for sk, v in self.rds.get(k, {}).items():
                    self._wait(e, (sk, v))
        for k in wr:
            t = self.lastw.get(k)
            if t:
                self._wait(e, t)
            for sk, v in self.rds.get(k, {}).items():
                self._wait(e, (sk, v))

    def _mark(self, tok, rd, wr):
        sk, v = tok
        for k in rd:
            d = self.rds.setdefault(k, {})
            d[sk] = max(d.get(sk, 0), v)
        for k in wr:
            self.lastw[k] = tok
            self.rds[k] = {}

    @staticmethod
    def _keys(kw, names):
        out = []
        for k in names:
            v = kw.get(k)
            if isinstance(v, bass.AP) and str(v.space) != "DRAM":
                out.append(v.tensor.name)
        return out

    def op(self, e, name, rd=None, wr=None, xrd=(), xwr=(), **kw):
        if rd is None:
            rd = self._keys(kw, READ_ARGS)
        if wr is None:
            wr = self._keys(kw, WRITE_ARGS)
        rd = list(rd) + list(xrd)
        wr = list(wr) + list(xwr)
        self._deps(e, rd, wr)
        ins = getattr(self.eng[e], name)(**kw)
        self.cnt[e] += 1
        ins.then_inc(self.sems[e], 1)
        self._mark((e, self.cnt[e]), rd, wr)
        self.ninstr += 1
        return ins

    def dma(self, q, out, in_, rd=None, wr=None, **kw):
        if rd is None:
            rd = self._keys({"in_": in_}, ("in_",))
        if wr is None:
            wr = self._keys({"out": out}, ("out",))
        self._deps(q, rd, wr)
        k = self.dq[q][self.dqi[q]]
        self.dqi[q] = (self.dqi[q] + 1) % len(self.dq[q])
        if self.duse[k] > 0:
            self._wait(q, (("d", k), 16 * self.duse[k]))
        self.eng[q].dma_start(out=out, in_=in_, **kw).then_inc(self.sems[("d", k)], 16)
        self.duse[k] += 1
        self._mark((("d", k), 16 * self.duse[k]), rd, wr)
        self.ninstr += 1

    def barrier(self):
        toks = [(e, self.cnt[e]) for e in self.ENG if self.cnt[e] > 0]
        toks += [(("d", k), 16 * self.duse[k]) for k in range(NDMA) if self.duse[k] > 0]
        for e in self.ENG:
            for t in toks:
                self._wait(e, t)

    def mm(self, out, lhsT, rhs, start=True, stop=True, **kw):
        return self.op("pe", "matmul", out=out, lhsT=lhsT, rhs=rhs, start=start, stop=stop, **kw)

    def tr(self, out, in_, ident):
        return self.op("pe", "transpose", out=out, in_=in_, identity=ident)

    def act(self, out, in_, func, e="act", **kw):
        return self.op(e, "activation", out=out, in_=in_, func=func, **kw)

    def tt(self, out, in0, in1, op, e="dve"):
        return self.op(e, "tensor_tensor", out=out, in0=in0, in1=in1, op=op)

    def ts(self, out, in0, s1, op0, s2=None, op1=None, e="dve", **kw):
        if op1 is None:
            return self.op(e, "tensor_scalar", out=out, in0=in0, scalar1=s1, scalar2=None, op0=op0, **kw)
        return self.op(e, "tensor_scalar", out=out, in0=in0, scalar1=s1, scalar2=s2, op0=op0, op1=op1, **kw)

    def stt(self, out, in0, scalar, in1, op0, op1):
        return self.op("dve", "scalar_tensor_tensor", out=out, in0=in0, scalar=scalar, in1=in1, op0=op0, op1=op1)

    def cp(self, out, in_, e="dve"):
        if e == "act":
            return self.op("act", "activation", out=out, in_=in_, func=AF.Copy)
        return self.op(e, "tensor_copy", out=out, in_=in_)


class Ring:
    def __init__(self, items):
        self.items = items
        self.i = 0

    def next(self):
        t = self.items[self.i]
        self.i = (self.i + 1) % len(self.items)
        return t


def bc(ap, shape):
    return ap.broadcast_to(list(shape))


CI = dict(ident=0, ones=128, blk64=256, le=384, lt=512, ge=640, gt=768, negf=896, negb=1024, cmask=1152,
          relf=1664, relb=1792, lp1=1920, rl=2048)
NCONST = 2048 + 128


def make_consts():
    c = np.zeros((128, NCONST), np.float32)
    i = np.arange(128)
    s, t = i[:, None], i[None, :]
    c[:, 0:128] = np.eye(128)
    c[:, 128:256] = 1.0
    c[:, 256:384] = (s // 64 == t // 64)
    c[:, 384:512] = (s <= t)
    c[:, 512:640] = (s < t)
    c[:, 640:768] = (s >= t)
    c[:, 768:896] = (s > t)
    c[:, 896:1024] = np.where(s <= t, 0.0, -30000.0)
    c[:, 1024:1152] = np.where(s >= t, 0.0, -30000.0)
    cm = np.ones(512, np.float32)
    cm[::128] = 0.0
    c[:, 1152:1664] = cm[None, :]
    c[:, 1664:1792] = np.maximum(t - s, 0)
    c[:, 1792:1920] = np.maximum(s - t, 0)
    c[:, 1920:2048] = t + 1 + 0 * s
    c[:, 2048:2176] = 128 - t + 0 * s
    return c


def make_rope(T):
    n = 32
    pos = np.arange(T)
    row = (pos // GW).astype(np.float32)
    col = (pos % GW).astype(np.float32)
    inv = (np.float32(10000.0) ** (-np.arange(n, dtype=np.float32) / n)).astype(np.float32)
    ang = np.concatenate([row[:, None] * inv, col[:, None] * inv], -1).astype(np.float32)
    return np.concatenate([np.cos(ang), np.sin(ang)], -1).astype(np.float32)


class Ctx:
    pass


def token_tiles(T, w=512):
    tiles = [(g, min(w, TC - g)) for g in range(0, TC, w)]
    tiles += [(TC + g, min(w, T - g)) for g in range(0, T, w)]
    return tiles


def build(T, dbg=()):
    P = PB()
    nc = P.nc
    C = Ctx()
    C.P, C.T, C.TT = P, T, T + TC
    TT = C.TT
    C.dbg = dbg

    def din(name, shape, dt=F32):
        return nc.dram_tensor(name, list(shape), dt, kind="ExternalInput")

    def dscr(name, shape, dt=F32):
        kind = "ExternalOutput" if name in dbg else "Internal"
        return nc.dram_tensor(name, list(shape), dt, kind=kind)

    I = C.I = {}
    I["x"] = din("x", [T, D])
    I["ctx"] = din("ctx", [TC, D])
    I["cT"] = din("cT", [128, 16])
    I["consts"] = din("consts", [128, NCONST])
    I["rope"] = din("rope", [T, 128])
    I["pvec"] = din("pvec", [128, NPV])
    for nm, shp in WSHAPES.items():
        I[nm] = din(nm, shp)
    C.out = nc.dram_tensor("out", [T, D], F32, kind="ExternalOutput")
    S = C.S = {}
    S["modbc"] = dscr("modbc", [2, 2, 6 * D])
    S["xcur"] = dscr("xcur", [TT, D])
    S["hT"] = dscr("hT", [1 + (T + 511) // 512, 128, 8, 512], BF16)
    S["F0"] = dscr("F0", [28 * 128, TT])
    for d in "fb":
        for nm in ("AT", "BT", "KT", "RT"):
            S[nm + d] = dscr(nm + d, [128, 4, TT], BF16)
        S["BTM" + d] = dscr("BTM" + d, [TT, 512], BF16)
        S["KTM" + d] = dscr("KTM" + d, [TT, 512], BF16)
        S["LAMC" + d] = dscr("LAMC" + d, [128, 4, TT // 128])
    S["VTM"] = dscr("VTM", [TT, 512], BF16)
    S["BONUS"] = dscr("BONUS", [TT, 512])
    S["GATE"] = dscr("GATE", [TT, 512])
    S["YRK"] = dscr("YRK", [TT, 512])
    S["MIX"] = dscr("MIX", [TT, 2048], BF16)
    S["XSTM"] = dscr("XSTM", [TT, 512])
    S["ZTM"] = dscr("ZTM", [TT, 512])
    S["BMTM"] = dscr("BMTM", [TT, 256], BF16)
    S["DTTM"] = dscr("DTTM", [TT, 32])
    S["BCT"] = dscr("BCT", [128, 4, TT], BF16)
    S["YSD"] = dscr("YSD", [TT, 512])
    S["ACTT"] = dscr("ACTT", [128, 22, TT], BF16)
    S["QT"] = dscr("QT", [128, 8, TT], BF16)
    S["KT1"] = dscr("KT1", [128, 8, TT], BF16)
    S["KTM1"] = dscr("KTM1", [TT, 1024], BF16)
    S["VTM1"] = dscr("VTM1", [TT, 2048], BF16)
    S["GS1"] = dscr("GS1", [TT, 2048], BF16)
    S["YRET"] = dscr("YRET", [TT, 2048])

    C.cst = P.sb("cst", [128, NCONST])
    P.dma("sp", C.cst[:], I["consts"].ap())
    C.pv = P.sb("pvec", [128, NPV])
    P.dma("sp", C.pv[:], I["pvec"].ap())
    C.identb = P.sb("identb", [128, 128], BF16)
    P.cp(C.identb[:], C.cst[:, 0:128])
    C.cmaskb = None

    phase_mod(C)
    for layer in range(2):
        C.layer = layer
        phase_norm(C, layer, 0, first=(layer == 0))
        if layer == 0:
            phase_proj0(C)
            if "STOP_B0" in dbg:
                break
            if "SKIP_RWKV" not in dbg:
                phase_rwkv_feat(C)
            if "STOP_C0" in dbg:
                break
            phase_ssd_prep(C)
            phase_scans0(C)
            if "STOP_E0" in dbg:
                break
            phase_outproj(C, 0, 1024, "ev_w_out", True, False)
            if "STOP_F0" in dbg:
                break
            if "STOP_G0a" in dbg:
                break
            phase_ffn_up(C, 0, False)
            if "STOP_G0b" in dbg:
                break
            phase_ffn_down(C, 0, False, False)
            if "STOP_G0" in dbg:
                break
        else:
            phase_ret_proj(C)
            if "STOP_R1" in dbg:
                break
            phase_ret_scan(C)
            if "STOP_S1" in dbg:
                break
            phase_outproj(C, 1, 2048, "ret_w_out", False, True)
            if "STOP_F1" in dbg:
                break
            phase_ffn_up(C, 1, True)
            phase_ffn_down(C, 1, True, True)
    P.barrier()
    P.es.close()
    return nc


WSHAPES = dict(
    mod_w=[2, D, 6 * D], mod_b=[2, 6 * D], norm1_g=[2, D], norm2_g=[2, D],
    ffn_w_up=[2, D, 5632], ffn_conv_w=[2, 3, 3, 2816], ffn_conv_b=[2, 2816], ffn_w_down=[2, 2816, D],
    ev_w_in=[1, D, 3472], ev_mu_prev=[1, 1920], ev_mu_next=[1, 1920],
    rk_w0_f=[1, 512], rk_w0_b=[1, 512], rk_w2_f=[1, 64, 512], rk_w2_b=[1, 64, 512],
    rk_a0_f=[1, 512], rk_a0_b=[1, 512], rk_a2_f=[1, 64, 512], rk_a2_b=[1, 64, 512],
    rk_g2=[1, 128, 512], rk_k_k=[1, 512], rk_k_a=[1, 512], rk_r_k=[1, 8, 64],
    rk_ln_w=[1, 512], rk_ln_b=[1, 512], ssd_conv_w=[1, 3, 1024], ssd_conv_b=[1, 1024],
    ssd_dt_bias_f=[1, 8], ssd_dt_bias_b=[1, 8], ssd_a_log_f=[1, 8], ssd_a_log_b=[1, 8],
    ssd_d=[1, 8], ssd_norm_w=[1, 512], ev_w_out=[1, D, D], ret_w_in=[1, D, 6144],
    ret_log2_f=[1, 8], ret_log2_b=[1, 8], ret_w_out=[1, 2048, D], final_norm_g=[D],
)


def hT_view(C, g0, n):
    if g0 < TC:
        ti, off = 0, g0
    else:
        ti, off = 1 + (g0 - TC) // 512, (g0 - TC) % 512
    return C.S["hT"].ap()[ti][:, :, off:off + n]


def cs_(C, name):
    o = CI[name]
    return C.cst[:, o:o + 128]


def phase_mod(C):
    P, I, S = C.P, C.I, C.S
    with P.phase():
        cT = P.sb("cT", [128, 16])
        P.dma("sp", cT[:], I["cT"].ap())
        cs = P.sb("csil", [128, 16])
        P.act(cs[:], cT[:], AF.Silu)
        L = P.sb("Lbc", [128, 8, 128])
        for j in range(8):
            P.ts(L[:, j, 0:64], cs_(C, "ones")[:, 0:64], cs[:, j:j + 1], ALU.mult)
            P.ts(L[:, j, 64:128], cs_(C, "ones")[:, 0:64], cs[:, 8 + j:9 + j], ALU.mult)
        wring = P.pool_of("modw", [128, 8, 512], F32, 2)
        bring = P.pool_of("modb", [128, 512], F32, 2)
        oring = P.pool_of("modo", [128, 512], F32, 2)
        for l in range(2):
            wv = I["mod_w"].ap()[l].rearrange("(kc p) n -> p kc n", p=128)
            for n in range(12):
                w = wring.next()
                P.dma("sp", w[:], wv[:, :, n * 512:(n + 1) * 512])
                b = bring.next()
                P.dma("sp", b[:], I["mod_b"].ap()[l, n * 512:(n + 1) * 512].partition_broadcast(128))
                ps = P.ps()
                for kc in range(8):
                    P.mm(ps[:], L[:, kc, :], w[:, kc, :], start=(kc == 0), stop=(kc == 7))
                o = oring.next()
                P.tt(o[:], ps[:], b[:], ALU.add)
                for ty in range(2):
                    P.dma("pool", S["modbc"].ap()[l, ty:ty + 1, n * 512:(n + 1) * 512], o[64 * ty:64 * ty + 1, :])


def load_mod(C, layer, ty, idx, out):
    C.P.dma("sp", out, C.S["modbc"].ap()[layer, ty, idx * D:(idx + 1) * D].partition_broadcast(128))


def phase_norm(C, layer, which, first=False, lat_only=False):
    P, I, S, T, TT = C.P, C.I, C.S, C.T, C.TT
    gname = "norm1_g" if which == 0 else "norm2_g"
    with P.phase():
        G, SH = {}, {}
        g = P.sb("ng", [128, D])
        P.dma("sp", g[:], I[gname].ap()[layer].partition_broadcast(128))
        for ty in range(2):
            G[ty] = P.sb("G", [128, D])
            SH[ty] = P.sb("SH", [128, D])
            load_mod(C, layer, ty, 3 * which + 1, G[ty][:])
            load_mod(C, layer, ty, 3 * which + 0, SH[ty][:])
            P.stt(G[ty][:], G[ty][:], 1.0, g[:], ALU.add, ALU.mult)
        xr = P.pool_of("nx", [128, D], F32, 3)
        junk = P.sb("njunk", [128, D], BF16)
        hr = P.pool_of("nh", [128, D], BF16, 2)
        hTr = P.pool_of("nhT", [128, 8, 128], BF16, 3)
        st = P.pool_of("nst", [128, 4], F32, 3)
        for i in range(TT // 128):
            g0 = i * 128
            ty = 1 if g0 < TC else 0
            if lat_only and ty == 1:
                continue
            if first:
                src = I["ctx"].ap()[g0:g0 + 128, :] if ty == 1 else I["x"].ap()[g0 - TC:g0 - TC + 128, :]
            else:
                src = S["xcur"].ap()[g0:g0 + 128, :]
            x = xr.next()
            P.dma("sp", x[:], src)
            s = st.next()
            P.act(junk[:], x[:], AF.Square, accum_out=s[:, 0:1])
            P.act(s[:, 1:2], s[:, 0:1], AF.Sqrt, scale=1.0 / D, bias=1e-6)
            P.op("dve", "reciprocal", out=s[:, 2:3], in_=s[:, 1:2])
            P.stt(x[:], x[:], s[:, 2:3], G[ty][:], ALU.mult, ALU.mult)
            h = hr.next()
            P.tt(h[:], x[:], SH[ty][:], ALU.add)
            ps = P.ps()
            psb = ps[:].bitcast(BF16)
            for kc in range(8):
                P.tr(psb[:, kc * 128:(kc + 1) * 128], h[:, kc * 128:(kc + 1) * 128], C.identb[:])
            hT = hTr.next()
            P.cp(hT[:].rearrange("p a b -> p (a b)"), psb, e="act")
            P.dma("pool", hT_view(C, g0, 128), hT[:])


PV = {}
_o = 0
for _n, _w in (("mu_prev", 15), ("mu_next", 15), ("conv_w", 24), ("conv_b", 8), ("k_k", 4), ("k_a", 4), ("r_k", 4),
               ("w0_f", 4), ("w0_b", 4), ("a0_f", 4), ("a0_b", 4), ("dt_bias", 1), ("a_log", 1), ("ssd_d", 1),
               ("ffn_cw0", 198), ("ffn_cb0", 22), ("ffn_cw1", 198), ("ffn_cb1", 22)):
    PV[_n] = (_o, _w)
    _o += _w
NPV = _o


def fm(v):
    v = np.asarray(v, np.float32).reshape(-1)
    return np.ascontiguousarray(v.reshape(-1, 128).T)


def make_pvec(inp):
    pv = np.zeros((128, NPV), np.float32)

    def put(name, arr):
        o, w = PV[name]
        assert arr.shape[1] == w, (name, arr.shape, w)
        pv[:arr.shape[0], o:o + w] = arr
    put("mu_prev", fm(inp["ev_mu_prev"][0]))
    put("mu_next", fm(inp["ev_mu_next"][0]))
    put("conv_w", np.concatenate([fm(inp["ssd_conv_w"][0, j]) for j in range(3)], 1))
    put("conv_b", fm(inp["ssd_conv_b"][0]))
    for n, k in (("k_k", "rk_k_k"), ("k_a", "rk_k_a"), ("r_k", "rk_r_k"), ("w0_f", "rk_w0_f"), ("w0_b", "rk_w0_b"),
                 ("a0_f", "rk_a0_f"), ("a0_b", "rk_a0_b")):
        put(n, fm(inp[k][0]))
    put("dt_bias", np.concatenate([inp["ssd_dt_bias_f"][0], inp["ssd_dt_bias_b"][0]])[:, None].astype(np.float32))
    put("a_log", np.concatenate([inp["ssd_a_log_f"][0], inp["ssd_a_log_b"][0]])[:, None].astype(np.float32))
    put("ssd_d", np.concatenate([inp["ssd_d"][0], inp["ssd_d"][0]])[:, None].astype(np.float32))
    for l in range(2):
        cw = inp["ffn_conv_w"][l].reshape(9, 2816)
        put(f"ffn_cw{l}", np.concatenate([fm(cw[j]) for j in range(9)], 1))
        put(f"ffn_cb{l}", fm(inp["ffn_conv_b"][l]))
    return pv


def pv_(C, name, j=0, n=1):
    o, w = PV[name]
    return C.pv[:, o + j:o + j + n]


def phase_proj0(C):
    P, I, S, T, TT = C.P, C.I, C.S, C.T, C.TT
    PL = min(4096, max(T // 2, 128))
    parts = [(0, TC, 0, TC)]
    for a in range(0, T, PL):
        oA, oB = TC + a, TC + min(a + PL, T)
        parts.append((max(oA - 1, TC), min(oB + 1, TC + T), oA, oB))
    ML = PL + 2
    with P.phase():
        c0 = P.sb("c0", [128, 15])
        P.tt(c0[:], pv_(C, "mu_prev", 0, 15), pv_(C, "mu_next", 0, 15), ALU.add)
        P.ts(c0[:], c0[:], -1.0, ALU.mult, 1.0, ALU.add)
        hp_ = P.sb("phT", [128, 8, ML], BF16)
        ur = P.pool_of("urow", [128, ML + 2], F32, 2)
        t1r = P.pool_of("t1row", [128, ML + 2], F32, 2)
        wfr = P.pool_of("wf", [128, 8, 128], F32, 2)
        wbr = P.pool_of("wb", [128, 8, 128], BF16, 2)
        wv = I["ev_w_in"].ap()[0].rearrange("(kc p) n -> p kc n", p=128)

        def loadw(j):
            ncol = 128 if j < 27 else 16
            wf, wb = wfr.next(), wbr.next()
            P.dma("sp", wf[:, :, 0:ncol], wv[:, :, j * 128:j * 128 + ncol])
            P.cp(wb[:, :, 0:ncol], wf[:, :, 0:ncol], e="pool")
            return wb

        for (gA, gB, oA, oB) in parts:
            NL, oL, o0 = gB - gA, oB - oA, oA - gA
            g = gA
            while g < gB:
                tile_end = TC if g < TC else TC + ((g - TC) // 512 + 1) * 512
                n = min(gB, tile_end) - g
                if n == 1:
                    P.dma("sp", hp_[:, :, g - gA:g - gA + n], hT_view(C, g, n), allow_slow_non_contiguous=True)
                else:
                    P.dma("sp", hp_[:, :, g - gA:g - gA + n], hT_view(C, g, n))
                g += n
            for u in ur.items:
                P.op("dve", "memset", ap=u[:, 0:1], constant=0.0)
                P.op("dve", "memset", ap=u[:, 1 + NL:2 + NL], constant=0.0)
            wnext = loadw(0)
            for j in range(28):
                ncol = 128 if j < 27 else 16
                wb = wnext
                if j + 1 < 28:
                    wnext = loadw(j + 1)
                u = ur.next()
                for t0 in range(0, NL, 512):
                    n = min(512, NL - t0)
                    ps = P.ps()
                    for kc in range(8):
                        P.mm(ps[0:ncol, 0:n], wb[:, kc, 0:ncol], hp_[:, kc, t0:t0 + n], start=(kc == 0), stop=(kc == 7))
                    P.cp(u[0:ncol, 1 + t0:1 + t0 + n], ps[0:ncol, 0:n], e="act")
                s0 = 1 + o0
                dst = S["F0"].ap()[j * 128:j * 128 + ncol, oA:oB]
                if j < 15:
                    t1 = t1r.next()
                    P.act(t1[:, s0:s0 + oL], u[:, s0:s0 + oL], AF.Identity, scale=c0[:, j:j + 1])
                    P.stt(t1[:, s0:s0 + oL], u[:, s0 - 1:s0 - 1 + oL], pv_(C, "mu_prev", j), t1[:, s0:s0 + oL], ALU.mult, ALU.add)
                    P.stt(t1[:, s0:s0 + oL], u[:, s0 + 1:s0 + 1 + oL], pv_(C, "mu_next", j), t1[:, s0:s0 + oL], ALU.mult, ALU.add)
                    P.dma("pool", dst, t1[:, s0:s0 + oL])
                elif 19 <= j < 27:
                    c = j - 19
                    t1 = t1r.next()
                    P.act(t1[:, s0:s0 + oL], u[:, s0:s0 + oL], AF.Identity, scale=pv_(C, "conv_w", 8 + c), bias=pv_(C, "conv_b", c))
                    P.stt(t1[:, s0:s0 + oL], u[:, s0 - 1:s0 - 1 + oL], pv_(C, "conv_w", c), t1[:, s0:s0 + oL], ALU.mult, ALU.add)
                    P.stt(t1[:, s0:s0 + oL], u[:, s0 + 1:s0 + 1 + oL], pv_(C, "conv_w", 16 + c), t1[:, s0:s0 + oL], ALU.mult, ALU.add)
                    P.act(t1[:, s0:s0 + oL], t1[:, s0:s0 + oL], AF.Silu)
                    P.dma("pool", dst, t1[:, s0:s0 + oL])
                else:
                    P.dma("pool", dst, u[0:ncol, s0:s0 + oL])


def make_in_maps(inp, T, ncores):
    consts = make_consts()
    rope = make_rope(T)
    pvec = make_pvec(inp)
    shared = {k: np.ascontiguousarray(np.asarray(inp[k], np.float32)) for k in WSHAPES}
    maps = []
    for b in range(ncores):
        m = dict(shared)
        m["x"] = np.ascontiguousarray(inp["x"][b], dtype=np.float32)
        m["ctx"] = np.ascontiguousarray(inp["ctx"][b], dtype=np.float32)
        m["cT"] = np.ascontiguousarray(np.concatenate([fm(inp["c"][b]), fm(inp["c_ctx"])], 1))
        m["consts"] = consts
        m["rope"] = rope
        m["pvec"] = pvec
        maps.append(m)
    return maps


def kernel(**inputs):
    B, T, _ = inputs["x"].shape
    nc = build(T)
    maps = make_in_maps(inputs, T, B)
    res = run_bass_kernel_spmd(nc, maps, core_ids=list(range(B)))
    return np.stack([np.asarray(r["out"], np.float32) for r in res.results], 0)


LDK = 0.6065306597126334


def f0rows(C, r0, nchunk, g0, n):
    return C.S["F0"].ap()[r0:r0 + 128 * nchunk, g0:g0 + n].rearrange("(c p) t -> p c t", p=128)


def phase_rwkv_feat(C):
    P, I, S, T, TT = C.P, C.I, C.S, C.T, C.TT
    W = 256
    tiles = token_tiles(T, W)
    identf = cs_(C, "ident")
    blk = cs_(C, "blk64")
    with P.phase():
        wtmp = P.sb("lw32", [128, 3, 512])
        P.dma("sp", wtmp[0:64, 0, :], I["rk_w2_f"].ap()[0])
        P.dma("sp", wtmp[64:128, 0, :], I["rk_w2_b"].ap()[0])
        P.dma("sp", wtmp[0:64, 1, :], I["rk_a2_f"].ap()[0])
        P.dma("sp", wtmp[64:128, 1, :], I["rk_a2_b"].ap()[0])
        P.dma("sp", wtmp[:, 2, :], I["rk_g2"].ap()[0])
        wl = P.sb("lwb", [128, 3, 512], BF16)
        P.cp(wl[:], wtmp[:])
        omka = P.sb("omka", [128, 4])
        P.ts(omka[:], pv_(C, "k_a", 0, 4), -1.0, ALU.mult, 1.0, ALU.add)
        cmask = C.cst[:, CI["cmask"]:CI["cmask"] + W]
        Rr = P.pool_of("fR", [128, 4, W], F32, 2)
        Kr = P.pool_of("fK", [128, 4, W], F32, 2)
        Vr = P.pool_of("fV", [128, 4, W], F32, 2)
        KKr = P.pool_of("fKK", [128, 4, W], F32, 2)
        WAr = P.pool_of("fWA", [128, 2, W], F32, 2)
        GDr = P.pool_of("fGD", [128, W], F32, 2)
        tmp = P.pool_of("ftmp", [128, 4, W], F32, 12)
        t1r = P.pool_of("ft1", [128, W], F32, 4)
        b16 = P.pool_of("fb16", [128, 4, W], BF16, 10)
        s16 = P.pool_of("fs16", [128, W], BF16, 6)
        stf = P.pool_of("fstf", [128, 512], F32, 4)
        stb = P.pool_of("fstb", [128, 512], BF16, 6)
        lamr = P.pool_of("flam", [128, 4, W // 128], F32, 4)
        for (g0, n) in tiles:
            nch = n // 128
            R, K, V, KK, WA, GD = Rr.next(), Kr.next(), Vr.next(), KKr.next(), WAr.next(), GDr.next()
            P.dma("sp", R[:, :, 0:n], f0rows(C, 0, 4, g0, n))
            P.dma("sp", K[:, :, 0:n], f0rows(C, 512, 4, g0, n))
            P.dma("sp", V[:, :, 0:n], f0rows(C, 1024, 4, g0, n))
            P.dma("sp", WA[:, :, 0:n], f0rows(C, 1536, 2, g0, n))
            P.dma("sp", GD[:, 0:n], S["F0"].ap()[1792:1920, g0:g0 + n])
            thw, adb, sgd = s16.next(), s16.next(), s16.next()
            P.act(thw[:, 0:n], WA[:, 0, 0:n], AF.Tanh)
            P.cp(adb[:, 0:n], WA[:, 1, 0:n], e="act")
            P.act(sgd[:, 0:n], GD[:, 0:n], AF.Sigmoid)
            kq, bon, gt, sq4, rk4 = tmp.next(), tmp.next(), tmp.next(), tmp.next(), tmp.next()
            for c in range(4):
                P.ts(kq[:, c, 0:n], K[:, c, 0:n], pv_(C, "k_k", c), ALU.mult)
            P.act(sq4[:, :, 0:n], kq[:, :, 0:n], AF.Square)
            pss = []
            for c in range(4):
                ps = P.ps()
                P.mm(ps[:, 0:n], blk, sq4[:, c, 0:n])
                pss.append(ps)
            for c in range(4):
                P.stt(rk4[:, c, 0:n], R[:, c, 0:n], pv_(C, "r_k", c), K[:, c, 0:n], ALU.mult, ALU.mult)
            for c in range(4):
                P.act(sq4[:, c, 0:n], pss[c][:, 0:n], AF.Sqrt)
            pss = []
            for c in range(4):
                ps = P.ps()
                P.mm(ps[:, 0:n], blk, rk4[:, c, 0:n])
                pss.append(ps)
            P.ts(sq4[:, :, 0:n], sq4[:, :, 0:n], 1e-12, ALU.max)
            P.op("dve", "reciprocal", out=sq4[:, :, 0:n], in_=sq4[:, :, 0:n])
            P.tt(KK[:, :, 0:n], kq[:, :, 0:n], sq4[:, :, 0:n], ALU.mult)
            for c in range(4):
                P.tt(bon[:, c, 0:n], pss[c][:, 0:n], V[:, c, 0:n], ALU.mult)
            pss = []
            for c in range(4):
                ps = P.ps()
                P.mm(ps[:, 0:n], wl[:, 2, c * 128:(c + 1) * 128], sgd[:, 0:n])
                pss.append(ps)
            for c in range(4):
                P.cp(gt[:, c, 0:n], pss[c][:, 0:n], e="act")
            vb = b16.next()
            P.cp(vb[:, :, 0:n], V[:, :, 0:n], e="act")
            for k in range(nch):
                gk = g0 + k * 128
                for (src, dst) in ((bon, S["BONUS"]), (gt, S["GATE"])):
                    ps = P.ps()
                    for c in range(4):
                        P.tr(ps[:, c * 128:(c + 1) * 128], src[:, c, k * 128:(k + 1) * 128], identf)
                    st = stf.next()
                    P.cp(st[:], ps[:], e="act")
                    P.dma("pool", dst.ap()[gk:gk + 128, :], st[:])
                ps = P.ps()
                psb = ps[:].bitcast(BF16)
                for c in range(4):
                    P.tr(psb[:, c * 128:(c + 1) * 128], vb[:, c, k * 128:(k + 1) * 128], C.identb[:])
                st = stb.next()
                P.cp(st[:], psb[:, 0:512], e="act")
                P.dma("pool", S["VTM"].ap()[gk:gk + 128, :], st[:])
            for d, dn in enumerate("fb"):
                dp = slice(0, 64) if d == 0 else slice(64, 128)
                sig, icl = tmp.next(), tmp.next()
                for (wi, src, dst, bn) in ((0, thw, sig, "w0_"), (1, adb, icl, "a0_")):
                    pss = []
                    for c in range(4):
                        ps = P.ps()
                        P.mm(ps[:, 0:n], wl[dp, wi, c * 128:(c + 1) * 128], src[dp, 0:n])
                        pss.append(ps)
                    for c in range(4):
                        P.act(dst[:, c, 0:n], pss[c][:, 0:n], AF.Sigmoid, bias=pv_(C, bn + dn, c))
                cs = tmp.next()
                for c in range(4):
                    P.op("dve", "tensor_tensor_scan", out=cs[:, c, 0:n], data0=cmask[:, 0:n], data1=sig[:, c, 0:n],
                         initial=0.0, op0=ALU.mult, op1=ALU.add)
                v4 = lambda t: t[:, :, 0:n].rearrange("p c (k t) -> p c k t", t=128)
                tot = v4(cs)[:, :, :, 127:128]
                lam = lamr.next()
                P.act(lam[:, :, 0:nch], v4(cs)[:, :, :, 127], AF.Exp, scale=-LDK)
                P.dma("pool", S["LAMC" + dn].ap()[:, :, g0 // 128:g0 // 128 + nch], lam[:, :, 0:nch])
                if d == 1:
                    cs2 = tmp.next()
                    P.tt(cs2[:, :, 0:n], sig[:, :, 0:n], cs[:, :, 0:n], ALU.subtract)
                    P.tt(v4(cs2), v4(cs2), bc(tot, [128, 4, nch, 128]), ALU.add)
                    cs = cs2
                ex = tmp.next()
                P.tt(ex[:, :, 0:n], cs[:, :, 0:n], sig[:, :, 0:n], ALU.subtract)
                P.act(ex[:, :, 0:n], ex[:, :, 0:n], AF.Exp, scale=-LDK)
                ef, ei = tmp.next(), tmp.next()
                P.act(ef[:, :, 0:n], cs[:, :, 0:n], AF.Exp, scale=-LDK)
                P.act(ei[:, :, 0:n], cs[:, :, 0:n], AF.Exp, scale=LDK)
                kd = tmp.next()
                for c in range(4):
                    P.ts(kd[:, c, 0:n], icl[:, c, 0:n], pv_(C, "k_a", c), ALU.mult, omka[:, c:c + 1], ALU.add)
                P.tt(kd[:, :, 0:n], kd[:, :, 0:n], K[:, :, 0:n], ALU.mult)
                at, bt, kt, rt = b16.next(), b16.next(), b16.next(), b16.next()
                P.stt(at[:, :, 0:n], KK[:, :, 0:n], -1.0, ex[:, :, 0:n], ALU.mult, ALU.mult)
                P.tt(icl[:, :, 0:n], icl[:, :, 0:n], KK[:, :, 0:n], ALU.mult)
                P.tt(bt[:, :, 0:n], icl[:, :, 0:n], ei[:, :, 0:n], ALU.mult)
                P.tt(kt[:, :, 0:n], kd[:, :, 0:n], ei[:, :, 0:n], ALU.mult)
                P.tt(rt[:, :, 0:n], R[:, :, 0:n], ef[:, :, 0:n], ALU.mult)
                for (src, nm) in ((at, "AT"), (bt, "BT"), (kt, "KT"), (rt, "RT")):
                    P.dma("pool", S[nm + dn].ap()[:, :, g0:g0 + n], src[:, :, 0:n])
                for k in range(nch):
                    gk = g0 + k * 128
                    for (src, nm) in ((bt, "BTM"), (kt, "KTM")):
                        ps = P.ps()
                        psb = ps[:].bitcast(BF16)
                        for c in range(4):
                            P.tr(psb[:, c * 128:(c + 1) * 128], src[:, c, k * 128:(k + 1) * 128], C.identb[:])
                        st = stb.next()
                        P.cp(st[:], psb[:, 0:512], e="act")
                        P.dma("pool", S[nm + dn].ap()[gk:gk + 128, :], st[:])


def bcast_row(C, out, vec_ap):
    C.P.dma("sp", out, vec_ap.partition_broadcast(128))


def gen_rwkv_scan(C):
    P, I, S, T, TT = C.P, C.I, C.S, C.T, C.TT
    NCH = TT // 128
    NCC = TC // 128
    hp = lambda h: slice(0, 64) if h % 2 == 0 else slice(64, 128)
    mk = {}
    for nm in ("le", "lt", "ge", "gt"):
        mk[nm] = P.sb("mk" + nm, [128, 4, 128])
        for i in range(4):
            P.cp(mk[nm][:, i, :], cs_(C, nm))
    id4 = P.sb("id4", [128, 4, 128], F32)
    for i in range(4):
        P.cp(id4[:, i, :], cs_(C, "ident"))
    lnw = P.sb("lnw", [128, 512])
    lnb = P.sb("lnb", [128, 512])
    bcast_row(C, lnw[:], I["rk_ln_w"].ap()[0])
    bcast_row(C, lnb[:], I["rk_ln_b"].ap()[0])
    ST = P.sb("ST", [128, 4, 64])
    STz = P.sb("STz", [128, 8, 64], BF16)
    lam = P.sb("lamall", [128, 4, NCH])
    fr = {nm: P.pool_of("s" + nm, [128, 4, 128], BF16, 2) for nm in ("at", "bt", "kt", "rt")}
    tr_ = {nm: P.pool_of("s" + nm, [128, 512], BF16, 2) for nm in ("btm", "ktm", "v")}
    Pr = P.pool_of("sP", [128, 4, 128], F32, 4)
    PTr = P.pool_of("sPT", [128, 4, 128], F32, 4)
    TTr = P.pool_of("sTT", [128, 4, 128], F32, 4)
    TTfin = P.pool_of("sTTf", [128, 4, 128], F32, 4)
    Mr = {nm: P.pool_of("s" + nm, [128, 4, 128], BF16, 4) for nm in ("lak", "nrb", "nrk")}
    zr = P.pool_of("sz", [128, 512], F32, 2)
    ur = P.pool_of("su", [128, 512], BF16, 2)
    yr = P.pool_of("sy", [128, 512], F32, 5)
    wk = P.pool_of("swk", [128, 512], F32, 5)
    sm = P.pool_of("ssm", [128, 32], F32, 3)
    ob = P.pool_of("sob", [128, 512], BF16, 2)

    def prep(d, ci):
        dn = "fb"[d]
        g0 = ci * 128
        t = {}
        for nm, key in (("at", "AT"), ("bt", "BT"), ("kt", "KT"), ("rt", "RT")):
            t[nm] = fr[nm].next()
            P.dma("sp", t[nm][:], S[key + dn].ap()[:, :, g0:g0 + 128])
        for nm, key in (("btm", "BTM" + dn), ("ktm", "KTM" + dn), ("v", "VTM")):
            t[nm] = tr_[nm].next()
            P.dma("sp", t[nm][:], S[key].ap()[g0:g0 + 128, :])
        m_l, m_lT, m_n = (("gt", "lt", "le") if d == 0 else ("lt", "gt", "ge"))
        t["TT"], t["lak"], t["nrb"], t["nrk"] = [], [], [], []

        def prod(half, lhs, rhs, mask, ring):
            ps = P.ps()
            for i in range(4):
                P.mm(ps[:, i * 128:(i + 1) * 128], t[lhs][hp(half), i, :], t[rhs][hp(half), i, :])
            o = ring.next()
            P.tt(o[:].rearrange("p a b -> p (a b)"), ps[:], mk[mask][:].rearrange("p a b -> p (a b)"), ALU.mult)
            return o
        Pm, PTm, TTm = [None, None], [None, None], [None, None]
        for half in range(2):
            Pm[half] = prod(half, "at", "bt", m_l, Pr)
            PTm[half] = prod(half, "bt", "at", m_lT, PTr)
            TTm[half] = TTr.next()
            P.tt(TTm[half][:], PTm[half][:], id4[:], ALU.add)
            t["lak"].append(prod(half, "kt", "at", m_lT, Mr["lak"]))
            t["nrb"].append(prod(half, "bt", "rt", m_n, Mr["nrb"]))
            t["nrk"].append(prod(half, "kt", "rt", m_n, Mr["nrk"]))
        for lvl in range(1, 7):
            Pn = [None, None]
            for half in range(2):
                ps = P.ps()
                for i in range(4):
                    P.mm(ps[:, i * 128:(i + 1) * 128], PTm[half][:, i, :], Pm[half][:, i, :])
                Pn[half] = Pr.next()
                P.cp(Pn[half][:].rearrange("p a b -> p (a b)"), ps[:], e="act")
            for half in range(2):
                ps = P.ps()
                for i in range(4):
                    P.mm(ps[:, i * 128:(i + 1) * 128], Pn[half][:, i, :], TTm[half][:, i, :])
                TTn = TTr.next() if lvl < 6 else TTfin.next()
                P.tt(TTn[:].rearrange("p a b -> p (a b)"), ps[:], TTm[half][:].rearrange("p a b -> p (a b)"), ALU.add)
                TTm[half] = TTn
                if lvl < 6:
                    ps = P.ps()
                    for i in range(4):
                        P.tr(ps[:, i * 128:(i + 1) * 128], Pn[half][:, i, :], cs_(C, "ident"))
                    PTn = PTr.next()
                    P.cp(PTn[:].rearrange("p a b -> p (a b)"), ps[:], e="act")
                    PTm[half] = PTn
                Pm[half] = Pn[half]
        t["TT"] = TTm
        return t
        for half in range(2):
            def prod(lhs, rhs, mask, ring, addI=False):
                ps = P.ps()
                for i in range(4):
                    P.mm(ps[:, i * 128:(i + 1) * 128], t[lhs][hp(half), i, :], t[rhs][hp(half), i, :])
                o = ring.next()
                P.tt(o[:].rearrange("p a b -> p (a b)"), ps[:], mk[mask][:].rearrange("p a b -> p (a b)"), ALU.mult)
                return o
            Pm = prod("at", "bt", m_l, Pr)
            PTm = prod("bt", "at", m_lT, PTr)
            TTm = TTr.next()
            P.tt(TTm[:], PTm[:], id4[:], ALU.add)
            t["lak"].append(prod("kt", "at", m_lT, Mr["lak"]))
            t["nrb"].append(prod("bt", "rt", m_n, Mr["nrb"]))
            t["nrk"].append(prod("kt", "rt", m_n, Mr["nrk"]))
            if "PREP_PROD" in C.dbg:
                continue
            for lvl in range(1, 7):
                ps = P.ps()
                for i in range(4):
                    P.mm(ps[:, i * 128:(i + 1) * 128], PTm[:, i, :], Pm[:, i, :])
                Pn = Pr.next()
                P.cp(Pn[:].rearrange("p a b -> p (a b)"), ps[:], e="act")
                if lvl < 6:
                    ps = P.ps()
                    for i in range(4):
                        P.mm(ps[:, i * 128:(i + 1) * 128], Pm[:, i, :], PTm[:, i, :])
                    PTn = PTr.next()
                    P.cp(PTn[:].rearrange("p a b -> p (a b)"), ps[:], e="act")
                else:
                    PTn = None
                ps = P.ps()
                for i in range(4):
                    P.mm(ps[:, i * 128:(i + 1) * 128], Pn[:, i, :], TTm[:, i, :])
                TTn = TTr.next() if lvl < 6 else TTfin.next()
                P.tt(TTn[:].rearrange("p a b -> p (a b)"), ps[:], TTm[:].rearrange("p a b -> p (a b)"), ALU.add)
                Pm, PTm, TTm = Pn, PTn, TTn
            t["TT"].append(TTm)
        return t

    def state(d, ci, t):
        g0 = ci * 128
        hc = lambda h: slice(h * 64, (h + 1) * 64)
        psZ = P.ps()
        for h in range(8):
            P.mm(psZ[:, hc(h)], t["at"][:, h // 2, :], STz[:, h, :], start=True, stop=False)
            P.mm(psZ[:, hc(h)], t["lak"][h % 2][:, h // 2, :], t["v"][:, hc(h)], start=False, stop=True)
        zb = zr.next()
        P.cp(zb[:], psZ[:], e="act")
        psU = P.ps()
        for h in range(8):
            P.mm(psU[:, hc(h)], t["TT"][h % 2][:, h // 2, :], zb[:, hc(h)])
        ub = ur.next()
        P.cp(ub[:], psU[:], e="act")
        psS = P.ps()
        for p in range(4):
            pc = slice(p * 128, (p + 1) * 128)
            P.mm(psS[:, pc], t["btm"][:, pc], ub[:, pc], start=True, stop=False)
            P.mm(psS[:, pc], t["ktm"][:, pc], t["v"][:, pc], start=False, stop=True)
        psY = P.ps()
        for h in range(8):
            P.mm(psY[:, hc(h)], t["rt"][:, h // 2, :], STz[:, h, :], start=True, stop=False)
            P.mm(psY[:, hc(h)], t["nrb"][h % 2][:, h // 2, :], ub[:, hc(h)], start=False, stop=False)
            P.mm(psY[:, hc(h)], t["nrk"][h % 2][:, h // 2, :], t["v"][:, hc(h)], start=False, stop=True)
        for par in range(2):
            rows = slice(par * 64, par * 64 + 64)
            blk_ = psS[rows, :].rearrange("p (a b) -> p a b", b=128)[:, :, par * 64:(par + 1) * 64]
            P.tt(ST[rows, :, :], ST[rows, :, :], blk_, ALU.add)
        P.tt(ST[:], ST[:], bc(lam[:, :, ci:ci + 1], [128, 4, 64]), ALU.mult)
        for par in range(2):
            rows = slice(par * 64, par * 64 + 64)
            P.cp(STz[rows, par::2, :], ST[rows, :, :], e="act")
        y = yr.next()
        P.cp(y[:], psY[:], e="act")
        return lambda: finish(d, g0, y)

    def finish(d, g0, y):
        if d == 1:
            P.dma("pool", S["YRK"].ap()[g0:g0 + 128, :], y[:])
            return
        yb = yr.next()
        P.dma("sp", yb[:], S["YRK"].ap()[g0:g0 + 128, :])
        bon, gat = wk.next(), wk.next()
        P.dma("sp", bon[:], S["BONUS"].ap()[g0:g0 + 128, :])
        P.dma("sp", gat[:], S["GATE"].ap()[g0:g0 + 128, :])
        P.tt(y[:], y[:], yb[:], ALU.add)
        if "YRKSUM" in C.dbg:
            P.dma("pool", S["YRK"].ap()[g0:g0 + 128, :], y[:])
        s = sm.next()
        y3 = y[:].rearrange("p (a b) -> p a b", b=64)
        P.op("dve", "tensor_reduce", out=s[:, 0:8], in_=y3, axis=AX.X, op=ALU.add)
        P.ts(s[:, 0:8], s[:, 0:8], 1.0 / 64, ALU.mult)
        P.tt(y3, y3, bc(s[:, 0:8].unsqueeze(2), [128, 8, 64]), ALU.subtract)
        sq = wk.next()
        P.tt(sq[:], y[:], y[:], ALU.mult)
        P.op("dve", "tensor_reduce", out=s[:, 8:16], in_=sq[:].rearrange("p (a b) -> p a b", b=64), axis=AX.X, op=ALU.add)
        P.act(s[:, 16:24], s[:, 8:16], AF.Sqrt, scale=1.0 / 64, bias=64e-5)
        P.op("dve", "reciprocal", out=s[:, 24:32], in_=s[:, 16:24])
        P.tt(y3, y3, bc(s[:, 24:32].unsqueeze(2), [128, 8, 64]), ALU.mult)
        P.tt(y[:], y[:], lnw[:], ALU.mult)
        P.tt(y[:], y[:], lnb[:], ALU.add)
        P.tt(y[:], y[:], bon[:], ALU.add)
        o = ob.next()
        P.tt(o[:], y[:], gat[:], ALU.mult)
        P.dma("pool", S["MIX"].ap()[g0:g0 + 128, 0:512], o[:])

    yield
    for d in (1, 0):
        if d == 0:
            P.barrier()
        order = list(range(NCH)) if d == 0 else (list(range(NCC - 1, -1, -1)) + list(range(NCH - 1, NCC - 1, -1)))
        P.dma("sp", lam[:], S["LAMC" + "fb"[d]].ap())
        P.op("dve", "memset", ap=ST[:], constant=0.0)
        P.op("dve", "memset", ap=STz[:], constant=0.0)
        nxt = prep(d, order[0])
        fin = None
        for i, ci in enumerate(order):
            cur = nxt
            if i + 1 < len(order):
                nxt = prep(d, order[i + 1])
            f2 = state(d, ci, cur)
            if fin is not None:
                fin()
            fin = f2
            yield
        fin()


def phase_ssd_prep(C):
    P, I, S, T, TT = C.P, C.I, C.S, C.T, C.TT
    W = 512
    tiles = token_tiles(T, W)
    identf = cs_(C, "ident")
    with P.phase():
        negA = P.sb("negA", [16, 1])
        P.act(negA[:], pv_(C, "a_log")[0:16, :], AF.Exp)
        P.ts(negA[:], negA[:], -1.0, ALU.mult)
        Zr = P.pool_of("pZ", [128, 4, W], F32, 2)
        Xr = P.pool_of("pX", [128, 4, W], F32, 2)
        Br = P.pool_of("pB", [128, 4, W], F32, 2)
        Dr = P.pool_of("pD", [16, W], F32, 2)
        D2 = P.pool_of("pD2", [16, 2, W], F32, 2)
        bcb = P.pool_of("pbc", [128, 4, W], BF16, 2)
        stf = P.pool_of("pstf", [128, 512], F32, 4)
        stb = P.pool_of("pstb", [128, 256], BF16, 3)
        std = P.pool_of("pstd", [128, 32], F32, 3)
        for (g0, n) in tiles:
            nch = n // 128
            Z, X, B, Dt, dd = Zr.next(), Xr.next(), Br.next(), Dr.next(), D2.next()
            P.dma("sp", Z[:, :, 0:n], f0rows(C, 1920, 4, g0, n))
            P.dma("sp", X[:, :, 0:n], f0rows(C, 2432, 4, g0, n))
            P.dma("sp", B[:, :, 0:n], f0rows(C, 2944, 4, g0, n))
            P.dma("sp", Dt[:, 0:n], S["F0"].ap()[3456:3472, g0:g0 + n])
            P.act(Dt[:, 0:n], Dt[:, 0:n], AF.Exp, bias=pv_(C, "dt_bias")[0:16, :])
            P.act(dd[:, 0, 0:n], Dt[:, 0:n], AF.Ln, bias=1.0)
            P.ts(dd[:, 1, 0:n], dd[:, 0, 0:n], negA[:, 0:1], ALU.mult)
            bb = bcb.next()
            P.cp(bb[:, :, 0:n], B[:, :, 0:n], e="act")
            P.dma("pool", S["BCT"].ap()[:, :, g0:g0 + n], bb[:, :, 0:n])
            for k in range(nch):
                gk = g0 + k * 128
                ks = slice(k * 128, (k + 1) * 128)
                for (src, dst) in ((X, "XSTM"), (Z, "ZTM")):
                    ps = P.ps()
                    for c in range(4):
                        P.tr(ps[:, c * 128:(c + 1) * 128], src[:, c, ks], identf)
                    st = stf.next()
                    P.cp(st[:], ps[:], e="act")
                    P.dma("pool", S[dst].ap()[gk:gk + 128, :], st[:])
                ps = P.ps()
                psb = ps[:].bitcast(BF16)
                for c in range(2):
                    P.tr(psb[:, c * 128:(c + 1) * 128], bb[:, c, ks], C.identb[:])
                st = stb.next()
                P.cp(st[:], psb[:, 0:256], e="act")
                P.dma("pool", S["BMTM"].ap()[gk:gk + 128, :], st[:])
                ps = P.ps()
                for j in range(2):
                    P.tr(ps[:, j * 16:(j + 1) * 16], dd[:, j, ks], identf[0:16, 0:16])
                st = std.next()
                P.cp(st[:], ps[:, 0:32], e="act")
                P.dma("pool", S["DTTM"].ap()[gk:gk + 128, :], st[:])


def gen_ssd_scan(C):
    P, I, S, T, TT = C.P, C.I, C.S, C.T, C.TT
    NCH = TT // 128
    NCC = TC // 128
    id8 = P.sb("id8", [128, 8, 128])
    neg8 = {d: P.sb("neg8", [128, 8, 128]) for d in (0, 1)}
    for i in range(8):
        P.cp(id8[:, i, :], cs_(C, "ident"))
        P.cp(neg8[0][:, i, :], cs_(C, "negf"), e="act")
        P.cp(neg8[1][:, i, :], cs_(C, "negb"), e="act")
    ones = cs_(C, "ones")
    Dv = P.sb("Dv", [128, 8])
    P.dma("sp", Dv[:], I["ssd_d"].ap()[0].partition_broadcast(128))
    nw = P.sb("nw", [128, 512])
    bcast_row(C, nw[:], I["ssd_norm_w"].ap()[0])
    H = P.sb("H", [128, 8, 64])
    Hb = P.sb("Hb", [128, 8, 64], BF16)
    xsr = P.pool_of("dxs", [128, 512], F32, 4)
    bmr = P.pool_of("dbm", [128, 256], BF16, 3)
    bcr = P.pool_of("dbc", [128, 4, 128], BF16, 3)
    dtr = P.pool_of("ddt", [128, 32], F32, 3)
    smr = P.pool_of("dsm", [128, 48], F32, 6)
    dgr = P.pool_of("ddg", [128, 8, 128], F32, 2)
    segr = P.pool_of("dseg", [128, 8, 128], F32, 2)
    cbr = P.pool_of("dcb", [128, 2, 128], F32, 2)
    wtr = P.pool_of("dwt", [128, 8, 128], BF16, 2)
    xdr = P.pool_of("dxd", [128, 8, 64], BF16, 3)
    xer = P.pool_of("dxe", [128, 8, 64], BF16, 3)
    wk = P.pool_of("dwk", [128, 512], F32, 11)
    ob = P.pool_of("dob", [128, 512], BF16, 2)
    junk = P.sb("djunk", [128, 512], BF16)

    def prep(d, ci):
        g0 = ci * 128
        do = 8 * d
        lend = 127 if d == 0 else 0
        t = {}
        xs, bm, bct, dtt, sm = xsr.next(), bmr.next(), bcr.next(), dtr.next(), smr.next()
        P.dma("sp", xs[:], S["XSTM"].ap()[g0:g0 + 128, :])
        P.dma("sp", bm[:], S["BMTM"].ap()[g0:g0 + 128, :])
        P.dma("sp", bct[:], S["BCT"].ap()[:, :, g0:g0 + 128])
        P.dma("sp", dtt[:], S["DTTM"].ap()[g0:g0 + 128, :])
        dt, dtA = dtt[:, do:do + 8], dtt[:, 16 + do:16 + do + 8]
        if "SS0" in C.dbg:
            return t
        ps = P.ps()
        P.mm(ps[:, 0:8], cs_(C, "le" if d == 0 else "ge"), dtA)
        cs = sm[:, 0:8]
        P.cp(cs, ps[:, 0:8])
        if "SS0b" in C.dbg:
            return t
        dg = dgr.next()
        P.tt(dg[:], id8[:], bc(cs.unsqueeze(2), [128, 8, 128]), ALU.mult)
        if "SS0c" in C.dbg:
            return t
        seg = segr.next()
        P.tt(seg[:], neg8[d][:], bc(cs.unsqueeze(2), [128, 8, 128]), ALU.subtract)
        eend = sm[:, 16:24]
        if "SS1" in C.dbg:
            return t
        for hf in range(2):
            psr = P.ps()
            P.mm(psr[:], ones, dg[:, hf * 4:(hf + 1) * 4, :].rearrange("p a b -> p (a b)"))
            P.tt(seg[:, hf * 4:(hf + 1) * 4, :].rearrange("p a b -> p (a b)"), seg[:, hf * 4:(hf + 1) * 4, :].rearrange("p a b -> p (a b)"), psr[:], ALU.add)
            P.act(eend[:, hf * 4:(hf + 1) * 4], psr[:].rearrange("p (a b) -> p a b", b=128)[:, :, lend], AF.Exp)
        P.act(seg[:].rearrange("p a b -> p (a b)"), seg[:].rearrange("p a b -> p (a b)"), AF.Exp)
        if "SS2" in C.dbg:
            return t
        ecs = sm[:, 8:16]
        P.act(ecs, cs, AF.Exp)
        psc = P.ps()
        for g in range(2):
            P.mm(psc[:, g * 128:(g + 1) * 128], bct[:, g, :], bct[:, 2 + g, :])
        cb = cbr.next()
        P.cp(cb[:].rearrange("p a b -> p (a b)"), psc[:, 0:256], e="act")
        wt = wtr.next()
        for g in range(2):
            P.tt(wt[:, g * 4:(g + 1) * 4, :], seg[:, g * 4:(g + 1) * 4, :], bc(cb[:, g:g + 1, :], [128, 4, 128]), ALU.mult)
        if "SS3" in C.dbg:
            return t
        xdt = xdr.next()
        xs3 = xs[:].rearrange("p (a b) -> p a b", b=64)
        P.tt(xdt[:], xs3, bc(dt.unsqueeze(2), [128, 8, 64]), ALU.mult)
        sc = sm[:, 24:32]
        P.tt(sc, dt, seg[:, :, lend], ALU.mult)
        xdd = xer.next()
        P.tt(xdd[:], xs3, bc(sc.unsqueeze(2), [128, 8, 64]), ALU.mult)
        psY = P.ps()
        for h in range(8):
            P.mm(psY[:, h * 64:(h + 1) * 64], wt[:, h, :], xdt[:, h, :])
        yi = wk.next()
        P.cp(yi[:], psY[:], e="act")
        t.update(xs=xs, bm=bm, bct=bct, ecs=ecs, eend=eend, xdd=xdd, yi=yi)
        return t

    def state(d, ci, t):
        g0 = ci * 128
        psI = P.ps()
        for g in range(2):
            P.mm(psI[:, g * 256:(g + 1) * 256], t["bct"][:, 2 + g, :], Hb[:, g * 4:(g + 1) * 4, :].rearrange("p a b -> p (a b)"))
        psH = P.ps()
        for g in range(2):
            P.mm(psH[:, g * 256:(g + 1) * 256], t["bm"][:, g * 128:(g + 1) * 128], t["xdd"][:, g * 4:(g + 1) * 4, :].rearrange("p a b -> p (a b)"))
        P.tt(H[:], H[:], bc(t["eend"].unsqueeze(2), [128, 8, 64]), ALU.mult)
        P.tt(H[:], H[:], psH[:].rearrange("p (a b) -> p a b", b=64), ALU.add)
        P.cp(Hb[:], H[:], e="act")
        y = wk.next()
        P.cp(y[:], psI[:], e="act")
        return lambda: finish(d, g0, t, y)

    def finish(d, g0, t, y):
        y3 = y[:].rearrange("p (a b) -> p a b", b=64)
        P.tt(y3, y3, bc(t["ecs"].unsqueeze(2), [128, 8, 64]), ALU.mult)
        P.tt(y[:], y[:], t["yi"][:], ALU.add)
        if d == 1:
            P.dma("pool", S["YSD"].ap()[g0:g0 + 128, :], y[:])
            return
        yb, z = wk.next(), wk.next()
        P.dma("sp", yb[:], S["YSD"].ap()[g0:g0 + 128, :])
        P.dma("sp", z[:], S["ZTM"].ap()[g0:g0 + 128, :])
        P.tt(y[:], y[:], yb[:], ALU.add)
        if "YSDSUM" in C.dbg:
            P.dma("pool", S["YSD"].ap()[g0:g0 + 128, :], y[:])
        P.tt(yb[:].rearrange("p (a b) -> p a b", b=64), t["xs"][:].rearrange("p (a b) -> p a b", b=64),
             bc(Dv[:].unsqueeze(2), [128, 8, 64]), ALU.mult)
        P.tt(y[:], y[:], yb[:], ALU.add)
        P.act(z[:], z[:], AF.Silu)
        P.tt(y[:], y[:], z[:], ALU.mult)
        s = smr.next()
        P.act(junk[:], y[:], AF.Square, accum_out=s[:, 0:1])
        P.act(s[:, 1:2], s[:, 0:1], AF.Sqrt, scale=1.0 / 512, bias=1e-6)
        P.op("dve", "reciprocal", out=s[:, 2:3], in_=s[:, 1:2])
        o = ob.next()
        P.stt(o[:], y[:], s[:, 2:3], nw[:], ALU.mult, ALU.mult)
        P.dma("pool", S["MIX"].ap()[g0:g0 + 128, 512:1024], o[:])

    yield
    for d in (1, 0):
        if d == 0:
            P.barrier()
        order = list(range(NCH)) if d == 0 else (list(range(NCC - 1, -1, -1)) + list(range(NCH - 1, NCC - 1, -1)))
        P.op("dve", "memset", ap=H[:], constant=0.0)
        P.op("dve", "memset", ap=Hb[:], constant=0.0)
        nxt = prep(d, order[0])
        fin = None
        for i, ci in enumerate(order):
            cur = nxt
            if i + 1 < len(order):
                nxt = prep(d, order[i + 1])
            f2 = state(d, ci, cur)
            if fin is not None:
                fin()
            fin = f2
            yield
        fin()


def load_weight_bf16(C, dst, src_view, nk, ncols):
    P = C.P
    ring = P.pool_of("wst", [128, ncols], F32, 2)
    for k in range(nk):
        st = ring.next()
        P.dma("sp", st[:], src_view[:, k, :])
        P.cp(dst[:, k, :], st[:], e="pool")


def xsrc(C, first, g0):
    if first:
        return C.I["ctx"].ap()[g0:g0 + 128, :] if g0 < TC else C.I["x"].ap()[g0 - TC:g0 - TC + 128, :]
    return C.S["xcur"].ap()[g0:g0 + 128, :]


def phase_outproj(C, layer, mixcols, wname, first, lat_only):
    P, I, S, T, TT = C.P, C.I, C.S, C.T, C.TT
    nk = mixcols // 128
    with P.phase():
        w = P.sb("wout", [128, nk, D], BF16)
        load_weight_bf16(C, w, I[wname].ap()[0].rearrange("(kc p) n -> p kc n", p=128), nk, D)
        GA = {}
        for ty in range(2):
            GA[ty] = P.sb("GA", [128, D])
            load_mod(C, layer, ty, 2, GA[ty][:])
        G2, SH2 = {}, {}
        g2 = P.sb("ng2", [128, D])
        P.dma("sp", g2[:], I["norm2_g"].ap()[layer].partition_broadcast(128))
        for ty in range(2):
            G2[ty] = P.sb("G2", [128, D])
            SH2[ty] = P.sb("SH2", [128, D])
            load_mod(C, layer, ty, 4, G2[ty][:])
            load_mod(C, layer, ty, 3, SH2[ty][:])
            P.stt(G2[ty][:], G2[ty][:], 1.0, g2[:], ALU.add, ALU.mult)
        njunk = P.sb("onjunk", [128, D], BF16)
        xnr = P.pool_of("oxn", [128, D], F32, 2)
        hr = P.pool_of("onh", [128, D], BF16, 2)
        hTr = P.pool_of("onhT", [128, 8, 128], BF16, 3)
        nst = P.pool_of("onst", [128, 4], F32, 3)
        mr = P.pool_of("omix", [128, mixcols], BF16, 2)
        mtr = P.pool_of("omixT", [128, nk, 128], BF16, 2)
        xr = P.pool_of("ox", [128, D], F32, 3)
        tr_ = P.pool_of("otmp", [128, 512], F32, 3)
        for i in range(TT // 128):
            g0 = i * 128
            ty = 1 if g0 < TC else 0
            if lat_only and ty == 1:
                continue
            m = mr.next()
            P.dma("sp", m[:], S["MIX"].ap()[g0:g0 + 128, 0:mixcols])
            x = xr.next()
            P.dma("sp", x[:], xsrc(C, first, g0))
            mT = mtr.next()
            for b in range(nk // 8):
                ps = P.ps()
                psb = ps[:].bitcast(BF16)
                for kc in range(8):
                    P.tr(psb[:, kc * 128:(kc + 1) * 128], m[:, (b * 8 + kc) * 128:(b * 8 + kc + 1) * 128], C.identb[:])
                P.cp(mT[:, b * 8:(b + 1) * 8, :].rearrange("p a b -> p (a b)"), psb, e="act")
            for hf in range(2):
                ps = P.ps()
                for kc in range(nk):
                    P.mm(ps[:], mT[:, kc, :], w[:, kc, hf * 512:(hf + 1) * 512], start=(kc == 0), stop=(kc == nk - 1))
                tmp = tr_.next()
                P.tt(tmp[:], ps[:], GA[ty][:, hf * 512:(hf + 1) * 512], ALU.mult)
                P.tt(x[:, hf * 512:(hf + 1) * 512], x[:, hf * 512:(hf + 1) * 512], tmp[:], ALU.add)
            P.dma("pool", S["xcur"].ap()[g0:g0 + 128, :], x[:])
            sn = nst.next()
            P.act(njunk[:], x[:], AF.Square, accum_out=sn[:, 0:1])
            P.act(sn[:, 1:2], sn[:, 0:1], AF.Sqrt, scale=1.0 / D, bias=1e-6)
            P.op("dve", "reciprocal", out=sn[:, 2:3], in_=sn[:, 1:2])
            xn = xnr.next()
            P.stt(xn[:], x[:], sn[:, 2:3], G2[ty][:], ALU.mult, ALU.mult)
            h = hr.next()
            P.tt(h[:], xn[:], SH2[ty][:], ALU.add)
            ps = P.ps()
            psb = ps[:].bitcast(BF16)
            for kc in range(8):
                P.tr(psb[:, kc * 128:(kc + 1) * 128], h[:, kc * 128:(kc + 1) * 128], C.identb[:])
            hT = hTr.next()
            P.cp(hT[:].rearrange("p a b -> p (a b)"), psb, e="act")
            P.dma("pool", hT_view(C, g0, 128), hT[:])


GELU_C = 1.5957691216057308


def phase_ffn_up(C, layer, lat_only):
    P, I, S, T, TT = C.P, C.I, C.S, C.T, C.TT
    R = T // GW
    RP = min(64, max(R // 2, 1))
    parts = [] if lat_only else [("ctx", 0, 0)]
    parts += [("lat", r0, min(r0 + RP, R)) for r0 in range(0, R, RP)]
    ML = (RP + 2) * GW
    cwn, cbn = f"ffn_cw{layer}", f"ffn_cb{layer}"
    with P.phase():
        hp_ = P.sb("uhT", [128, 8, ML], BF16)
        gr = P.pool_of("grow", [128, ML + 2], F32, 2)
        for g_ in gr.items:
            P.op("pool", "memset", ap=g_[:], constant=0.0)
        acc = P.sb("gacc", [128, ML + 2])
        gb = P.sb("ggelu", [128, ML], BF16)
        vr = P.pool_of("vrow", [128, ML], BF16, 2)
        wfr = P.pool_of("uwf", [128, 8, 256], F32, 2)
        wbr = P.pool_of("uwb", [128, 8, 256], BF16, 2)
        wv = I["ffn_w_up"].ap()[layer].rearrange("(kc p) n -> p kc n", p=128)

        def loadw(j):
            wf, wb = wfr.next(), wbr.next()
            P.dma("sp", wf[:, :, 0:128], wv[:, :, j * 128:(j + 1) * 128])
            P.dma("sp", wf[:, :, 128:256], wv[:, :, 2816 + j * 128:2816 + (j + 1) * 128])
            P.cp(wb[:], wf[:], e="pool")
            return wb

        for (kind, R0, R1) in parts:
            if kind == "ctx":
                gA, gB = 0, TC
                oA, oB = 0, TC
            else:
                Rb0, Rb1 = max(R0 - 1, 0), min(R1 + 1, R)
                gA, gB = TC + Rb0 * GW, TC + Rb1 * GW
                oA, oB = TC + R0 * GW, TC + R1 * GW
            NL = gB - gA
            g = gA
            while g < gB:
                tile_end = TC if g < TC else TC + ((g - TC) // 512 + 1) * 512
                n = min(gB, tile_end) - g
                P.dma("sp", hp_[:, :, g - gA:g - gA + n], hT_view(C, g, n))
                g += n
            o0, oL = oA - gA, oB - oA

            def head(j, wb, vrow, grow):
                for t0 in range(0, NL, 512):
                    n = min(512, NL - t0)
                    psA, psB = P.ps(), P.ps()
                    for kc in range(8):
                        P.mm(psA[:, 0:n], wb[:, kc, 0:128], hp_[:, kc, t0:t0 + n], start=(kc == 0), stop=(kc == 7))
                    for kc in range(8):
                        P.mm(psB[:, 0:n], wb[:, kc, 128:256], hp_[:, kc, t0:t0 + n], start=(kc == 0), stop=(kc == 7))
                    P.cp(grow[:, 1 + t0:1 + t0 + n], psA[:, 0:n], e="act")
                    P.cp(vrow[:, t0:t0 + n], psB[:, 0:n], e="act")

            def conv(j, grow):
                cw = lambda tap: pv_(C, cwn, tap * 22 + j)
                P.act(acc[:, 1 + o0:1 + o0 + oL], grow[:, 1 + o0:1 + o0 + oL], AF.Identity, scale=cw(4), bias=pv_(C, cbn, j))
                if kind == "ctx":
                    P.stt(acc[:, 1:1 + TC], grow[:, 0:TC], cw(3), acc[:, 1:1 + TC], ALU.mult, ALU.add)
                    P.stt(acc[:, 1:1 + TC], grow[:, 2:2 + TC], cw(5), acc[:, 1:1 + TC], ALU.mult, ALU.add)
                else:
                    nb = Rb1 - Rb0
                    gin = grow[:, 1:1 + nb * GW].rearrange("p (r w) -> p r w", w=GW)
                    gac = acc[:, 1:1 + nb * GW].rearrange("p (r w) -> p r w", w=GW)
                    for di in (-1, 0, 1):
                        for dj in (-1, 0, 1):
                            if di == 0 and dj == 0:
                                continue
                            rl, rh = max(R0, -di) - Rb0, min(R1, R - di) - Rb0
                            c0, c1 = max(0, -dj), GW - max(0, dj)
                            P.stt(gac[:, rl:rh, c0:c1], gin[:, rl + di:rh + di, c0 + dj:c1 + dj], cw((di + 1) * 3 + dj + 1),
                                  gac[:, rl:rh, c0:c1], ALU.mult, ALU.add)

            def tail(j, vrow):
                P.act(gb[:, o0:o0 + oL], acc[:, 1 + o0:1 + o0 + oL], AF.Gelu_apprx_tanh)
                P.tt(vrow[:, o0:o0 + oL], gb[:, o0:o0 + oL], vrow[:, o0:o0 + oL], ALU.mult)
                P.dma("pool", S["ACTT"].ap()[:, j, oA:oB], vrow[:, o0:o0 + oL])

            wnext = loadw(0)
            pend = None
            for j in range(22):
                wb = wnext
                if j + 1 < 22:
                    wnext = loadw(j + 1)
                vrow, grow = vr.next(), gr.next()
                head(j, wb, vrow, grow)
                if pend is not None:
                    tail(*pend)
                conv(j, grow)
                pend = (j, vrow)
            tail(*pend)


def phase_ffn_down(C, layer, lat_only, final):
    P, I, S, T, TT = C.P, C.I, C.S, C.T, C.TT
    tiles = [t for t in token_tiles(T) if not (lat_only and t[0] < TC)]
    with P.phase():
        w = P.sb("wdn", [128, 22, D], BF16)
        load_weight_bf16(C, w, I["ffn_w_down"].ap()[layer].rearrange("(kc p) n -> p kc n", p=128), 22, D)
        GA = {}
        for ty in range(2):
            GA[ty] = P.sb("GA2", [128, D])
            load_mod(C, layer, ty, 5, GA[ty][:])
        if final:
            gf = P.sb("gfin", [128, D])
            bcast_row(C, gf[:], I["final_norm_g"].ap())
            junk = P.sb("fjunk", [128, D], BF16)
            st = P.pool_of("fst", [128, 4], F32, 3)
        ar = P.pool_of("dact", [128, 22, 512], BF16, 2)
        xr = P.pool_of("dx", [128, D], F32, 3)
        tr_ = P.pool_of("dtmp", [128, 512], F32, 3)
        for (g0, n) in tiles:
            a = ar.next()
            P.dma("sp", a[:, :, 0:n], S["ACTT"].ap()[:, :, g0:g0 + n])
            for sub in range(n // 128):
                gs = g0 + sub * 128
                ty = 1 if gs < TC else 0
                x = xr.next()
                P.dma("sp", x[:], S["xcur"].ap()[gs:gs + 128, :])
                for hf in range(2):
                    ps = P.ps()
                    for cc in range(22):
                        P.mm(ps[:], a[:, cc, sub * 128:(sub + 1) * 128], w[:, cc, hf * 512:(hf + 1) * 512],
                             start=(cc == 0), stop=(cc == 21))
                    tmp = tr_.next()
                    P.tt(tmp[:], ps[:], GA[ty][:, hf * 512:(hf + 1) * 512], ALU.mult)
                    P.tt(x[:, hf * 512:(hf + 1) * 512], x[:, hf * 512:(hf + 1) * 512], tmp[:], ALU.add)
                if final:
                    s = st.next()
                    P.act(junk[:], x[:], AF.Square, accum_out=s[:, 0:1])
                    P.act(s[:, 1:2], s[:, 0:1], AF.Sqrt, scale=1.0 / D, bias=1e-6)
                    P.op("dve", "reciprocal", out=s[:, 2:3], in_=s[:, 1:2])
                    P.stt(x[:], x[:], s[:, 2:3], gf[:], ALU.mult, ALU.mult)
                    P.dma("pool", C.out.ap()[gs - TC:gs - TC + 128, :], x[:])
                else:
                    P.dma("pool", S["xcur"].ap()[gs:gs + 128, :], x[:])


def phase_ret_proj(C):
    P, I, S, T, TT = C.P, C.I, C.S, C.T, C.TT
    tiles = token_tiles(T)
    with P.phase():
        w = P.sb("wret", [128, 8, 6144], BF16)
        wv = I["ret_w_in"].ap()[0].rearrange("(kc p) n -> p kc n", p=128)
        ring = P.pool_of("wst", [128, 8, 256], F32, 2)
        for nb in range(24):
            st = ring.next()
            P.dma("sp", st[:], wv[:, :, nb * 256:(nb + 1) * 256])
            P.cp(w[:, :, nb * 256:(nb + 1) * 256], st[:], e="pool")
        htr = P.pool_of("rht", [128, 8, 512], BF16, 2)
        qfr = P.pool_of("rqf", [128, 1024], F32, 3)
        qbr = P.pool_of("rqb", [128, 1024], BF16, 4)
        tmr = P.pool_of("rtm", [128, 8, 64], F32, 4)
        rpr = P.pool_of("rrp", [128, 128], F32, 2)
        vbr = P.pool_of("rvb", [128, 2048], BF16, 3)
        str_ = P.pool_of("rst", [128, 8, 128], BF16, 4)
        for (g0, n) in tiles:
            ht = htr.next()
            P.dma("sp", ht[:, :, 0:n], hT_view(C, g0, n))
            for sub in range(n // 128):
                gs = g0 + sub * 128
                lat = gs >= TC
                if lat:
                    rp = rpr.next()
                    P.dma("sp", rp[:], I["rope"].ap()[gs - TC:gs - TC + 128, :])
                    cos = bc(rp[:, 0:64].unsqueeze(1), [128, 8, 64])
                    sin = bc(rp[:, 64:128].unsqueeze(1), [128, 8, 64])

                def proj(nb0, dst, scale=None, func=None):
                    for b in range(dst.shape[1] // 512):
                        ps = P.ps()
                        for kc in range(8):
                            P.mm(ps[:], ht[:, kc, sub * 128:(sub + 1) * 128], w[:, kc, (nb0 + b) * 512:(nb0 + b + 1) * 512],
                                 start=(kc == 0), stop=(kc == 7))
                        o = dst[:, b * 512:(b + 1) * 512]
                        if func is not None:
                            P.act(o, ps[:], func)
                        elif scale is not None:
                            P.act(o, ps[:], AF.Copy, scale=scale)
                        else:
                            P.cp(o, ps[:], e="act")
                for (nb0, scale, tname, tmname) in ((0, None, "QT", None), (2, 128 ** -0.5, "KT1", "KTM1")):
                    qb = qbr.next()
                    if lat:
                        qf = qfr.next()
                        proj(nb0, qf[:], scale=scale)
                        q3 = qf[:].rearrange("p (h d) -> p h d", d=128)
                        o3 = qb[:].rearrange("p (h d) -> p h d", d=128)
                        x1, x2 = q3[:, :, 0:64], q3[:, :, 64:128]
                        t1, t2 = tmr.next(), tmr.next()
                        P.tt(t1[:], x1, cos, ALU.mult)
                        P.tt(t2[:], x2, sin, ALU.mult)
                        P.tt(o3[:, :, 0:64], t1[:], t2[:], ALU.subtract)
                        t3, t4 = tmr.next(), tmr.next()
                        P.tt(t3[:], x2, cos, ALU.mult)
                        P.tt(t4[:], x1, sin, ALU.mult)
                        P.tt(o3[:, :, 64:128], t3[:], t4[:], ALU.add)
                    else:
                        proj(nb0, qb[:], scale=scale)
                    ps = P.ps()
                    psb = ps[:].bitcast(BF16)
                    for h in range(8):
                        P.tr(psb[:, h * 128:(h + 1) * 128], qb[:, h * 128:(h + 1) * 128], C.identb[:])
                    st = str_.next()
                    P.cp(st[:].rearrange("p a b -> p (a b)"), psb, e="act")
                    P.dma("pool", S[tname].ap()[:, :, gs:gs + 128], st[:])
                    if tmname:
                        P.dma("pool", S[tmname].ap()[gs:gs + 128, :], qb[:])
                vb = vbr.next()
                proj(4, vb[:])
                P.dma("pool", S["VTM1"].ap()[gs:gs + 128, :], vb[:])
                if lat:
                    gb = vbr.next()
                    proj(8, gb[:], func=AF.Silu)
                    P.dma("pool", S["GS1"].ap()[gs:gs + 128, :], gb[:])


def phase_ret_scan(C):
    P, I, S, T, TT = C.P, C.I, C.S, C.T, C.TT
    NCH = TT // 128
    NCC = TC // 128
    LN2 = 0.6931471805599453
    with P.phase():
        Dm, GQ, GL = {}, {}, {}
        for d, dn in enumerate("fb"):
            lg = P.sb("lg", [128, 8])
            P.dma("sp", lg[:], I["ret_log2_" + dn].ap()[0].partition_broadcast(128))
            P.act(lg[:], lg[:], AF.Exp, scale=-LN2)
            P.ts(lg[:], lg[:], -1.0, ALU.mult, 1.0, ALU.add)
            P.act(lg[:], lg[:], AF.Ln)
            Dm[d] = P.sb("Dm", [128, 8, 128])
            GQ[d] = P.sb("GQ", [128, 8, 128])
            GL[d] = P.sb("GL", [128, 8])
            for h in range(8):
                P.act(Dm[d][:, h, :], cs_(C, "relf" if d == 0 else "relb"), AF.Exp, scale=lg[:, h:h + 1])
                P.tt(Dm[d][:, h, :], Dm[d][:, h, :], cs_(C, "le" if d == 0 else "ge"), ALU.mult)
                P.act(GQ[d][:, h, :], cs_(C, "lp1" if d == 0 else "rl"), AF.Exp, scale=lg[:, h:h + 1])
            P.act(GL[d][:], lg[:], AF.Exp, scale=128.0)
        St = P.sb("rS", [128, 8, 256])
        Sb = P.sb("rSb", [128, 8, 256], BF16)
        qr = P.pool_of("sq", [128, 8, 128], BF16, 2)
        kr = P.pool_of("sk", [128, 8, 128], BF16, 2)
        kmr = P.pool_of("skm", [128, 1024], BF16, 2)
        vr = P.pool_of("sv", [128, 2048], BF16, 2)
        wtr = P.pool_of("swt", [128, 8, 128], BF16, 2)
        qdr = P.pool_of("sqd", [128, 8, 128], BF16, 2)
        ker = P.pool_of("ske", [128, 8, 128], BF16, 2)
        yr = P.pool_of("sy", [128, 2048], F32, 2)
        ybr = P.pool_of("syb", [128, 2048], F32, 2)
        gr = P.pool_of("sg", [128, 2048], BF16, 2)
        ob = P.pool_of("so", [128, 2048], BF16, 2)
        sm = P.pool_of("ssm", [128, 32], F32, 3)

        def prep(d, ci):
            g0 = ci * 128
            q, k, km, v = qr.next(), kr.next(), kmr.next(), vr.next()
            P.dma("sp", q[:], S["QT"].ap()[:, :, g0:g0 + 128])
            P.dma("sp", k[:], S["KT1"].ap()[:, :, g0:g0 + 128])
            P.dma("sp", km[:], S["KTM1"].ap()[g0:g0 + 128, :])
            P.dma("sp", v[:], S["VTM1"].ap()[g0:g0 + 128, :])
            lend = 127 if d == 0 else 0
            t = dict(v=v)
            ke = ker.next()
            P.tt(ke[:], km[:].rearrange("p (h d) -> p h d", d=128), bc(Dm[d][:, :, lend:lend + 1], [128, 8, 128]), ALU.mult, e="pool")
            t["ke"] = ke
            if ci >= NCC:
                wt = wtr.next()
                for hf in range(2):
                    ps = P.ps()
                    for i in range(4):
                        h = hf * 4 + i
                        P.mm(ps[:, i * 128:(i + 1) * 128], k[:, h, :], q[:, h, :])
                    P.tt(wt[:, hf * 4:(hf + 1) * 4, :].rearrange("p a b -> p (a b)"), ps[:],
                         Dm[d][:, hf * 4:(hf + 1) * 4, :].rearrange("p a b -> p (a b)"), ALU.mult)
                qd = qdr.next()
                P.tt(qd[:], q[:], GQ[d][:], ALU.mult, e="pool")
                t.update(wt=wt, qd=qd)
            return t

        def state(d, ci, t):
            g0 = ci * 128
            v = t["v"]
            if ci >= NCC:
                y = yr.next()
                for b in range(4):
                    ps = P.ps()
                    for j in range(2):
                        h = 2 * b + j
                        P.mm(ps[:, j * 256:(j + 1) * 256], t["wt"][:, h, :], v[:, h * 256:(h + 1) * 256], start=True, stop=False)
                        P.mm(ps[:, j * 256:(j + 1) * 256], t["qd"][:, h, :], Sb[:, h, :], start=False, stop=True)
                    P.cp(y[:, b * 512:(b + 1) * 512], ps[:], e="act")
            for b in range(4):
                ps = P.ps()
                for j in range(2):
                    h = 2 * b + j
                    P.mm(ps[:, j * 256:(j + 1) * 256], t["ke"][:, h, :], v[:, h * 256:(h + 1) * 256])
                for j in range(2):
                    h = 2 * b + j
                    P.stt(St[:, h, :], St[:, h, :], GL[d][:, h:h + 1], ps[:, j * 256:(j + 1) * 256], ALU.mult, ALU.add)
            P.cp(Sb[:], St[:], e="act")
            if ci < NCC:
                return
            if d == 1:
                P.dma("pool", S["YRET"].ap()[g0:g0 + 128, :], y[:])
                return
            yb, gs = ybr.next(), gr.next()
            P.dma("sp", yb[:], S["YRET"].ap()[g0:g0 + 128, :])
            P.dma("sp", gs[:], S["GS1"].ap()[g0:g0 + 128, :])
            P.tt(y[:], y[:], yb[:], ALU.add)
            if "YRETSUM" in C.dbg:
                P.dma("pool", S["YRET"].ap()[g0:g0 + 128, :], y[:])
            P.tt(yb[:], y[:], y[:], ALU.mult)
            s = sm.next()
            P.op("dve", "tensor_reduce", out=s[:, 0:8], in_=yb[:].rearrange("p (a b) -> p a b", b=256), axis=AX.X, op=ALU.add)
            P.act(s[:, 8:16], s[:, 0:8], AF.Sqrt, scale=1.0 / 256, bias=1e-6)
            P.op("dve", "reciprocal", out=s[:, 16:24], in_=s[:, 8:16])
            y3 = y[:].rearrange("p (a b) -> p a b", b=256)
            P.tt(y3, y3, bc(s[:, 16:24].unsqueeze(2), [128, 8, 256]), ALU.mult)
            o = ob.next()
            P.tt(o[:], y[:], gs[:], ALU.mult)
            P.dma("pool", S["MIX"].ap()[g0:g0 + 128, :], o[:])

        for d in (1, 0):
            if d == 0:
                P.barrier()
            order = list(range(NCH)) if d == 0 else (list(range(NCC - 1, -1, -1)) + list(range(NCH - 1, NCC - 1, -1)))
            P.op("dve", "memset", ap=St[:], constant=0.0)
            P.op("dve", "memset", ap=Sb[:], constant=0.0)
            nxt = prep(d, order[0])
            for i, ci in enumerate(order):
                cur = nxt
                if i + 1 < len(order):
                    nxt = prep(d, order[i + 1])
                state(d, ci, cur)


def phase_scans0(C):
    P = C.P
    with P.phase():
        gens = [gen_rwkv_scan(C), gen_ssd_scan(C)]
        while gens:
            for g in list(gens):
                try:
                    next(g)
                except StopIteration:
                    gens.remove(g)
```

```python
import contextlib
import numpy as np
import concourse.bass as bass
import concourse.mybir as mybir
from concourse.bass_utils import run_bass_kernel_spmd

F32 = mybir.dt.float32
BF16 = mybir.dt.bfloat16
AF = mybir.ActivationFunctionType
ALU = mybir.AluOpType
AX = mybir.AxisListType

D = 1024
TC = 256
GW = 64
NDMA = 40
SELF_SKIP = ("pe", "sp", "pool")
READ_ARGS = ("in_", "in0", "in1", "lhsT", "rhs", "scalar1", "scalar2", "scalar", "bias", "scale",
             "data0", "data1", "initial", "identity", "mask", "on_true", "on_false")
WRITE_ARGS = ("out", "accum_out", "ap")


class PB:
    ENG = ("pe", "act", "dve", "pool", "sp")

    def __init__(self):
        self.nc = nc = bass.Bass("TRN2", target_bir_lowering=False)
        self.es = contextlib.ExitStack()
        self.eng = dict(pe=nc.tensor, act=nc.scalar, dve=nc.vector, pool=nc.gpsimd, sp=nc.sync)
        self.sems = {}
        for e in self.ENG:
            self.sems[e] = self.es.enter_context(nc.semaphore("sem_" + e))
        for k in range(NDMA):
            self.sems[("d", k)] = self.es.enter_context(nc.semaphore(f"semd{k}"))
        self.cnt = {e: 0 for e in self.ENG}
        self.duse = [0] * NDMA
        self.dq = {"sp": list(range(0, 28)), "pool": list(range(28, 36)), "act": list(range(36, 40))}
        self.dqi = {"sp": 0, "pool": 0, "act": 0}
        self.waited = {e: {} for e in self.ENG}
        self.lastw = {}
        self.rds = {}
        self.uid = 0
        self.stack = [self.es]
        self.psb = [self.es.enter_context(nc.psum_tensor(f"psb{i}", [128, 512], F32)) for i in range(8)]
        self.psi = 0
        self.ninstr = 0

    def sb(self, name, shape, dt=F32):
        self.uid += 1
        return self.stack[-1].enter_context(self.nc.sbuf_tensor(f"{name}_{self.uid}", list(shape), dt))

    def pool_of(self, name, shape, dt, n):
        return Ring([self.sb(name, shape, dt) for _ in range(n)])

    @contextlib.contextmanager
    def phase(self):
        st = contextlib.ExitStack()
        self.stack.append(st)
        try:
            yield
        finally:
            self.barrier()
            self.stack.pop()
            st.close()

    def ps(self):
        t = self.psb[self.psi]
        self.psi = (self.psi + 1) % 8
        return t

    def _wait(self, e, tok):
        sk, v = tok
        if sk == e and e in SELF_SKIP:
            return
        if self.waited[e].get(sk, 0) >= v:
            return
        self.eng[e].wait_ge(self.sems[sk], v)
        self.waited[e][sk] = v
        self.ninstr += 1

    def _deps(self, e, rd, wr):
        for k in rd:
            t = self.lastw.get(k)
            if t:
                self._wait(e, t)
            if k.startswith("psb"):
                for sk, v in self.rds.get(k, {}).items():
                    self._wait(e, (sk, v))
        for k in wr:
            t = self.lastw.get(k)
            if t:
                self._wait(e, t)
            for sk, v in self.rds.get(k, {}).items():
                self._wait(e, (sk, v))

    def _mark(self, tok, rd, wr):
        sk, v = tok
        for k in rd:
            d = self.rds.setdefault(k, {})
            d[sk] = max(d.get(sk, 0), v)
        for k in wr:
            self.lastw[k] = tok
            self.rds[k] = {}

    @staticmethod
    def _keys(kw, names):
        out = []
        for k in names:
            v = kw.get(k)
            if isinstance(v, bass.AP) and str(v.space) != "DRAM":
                out.append(v.tensor.name)
        return out

    def op(self, e, name, rd=None, wr=None, xrd=(), xwr=(), **kw):
        if rd is None:
            rd = self._keys(kw, READ_ARGS)
        if wr is None:
            wr = self._keys(kw, WRITE_ARGS)
        rd = list(rd) + list(xrd)
        wr = list(wr) + list(xwr)
        self._deps(e, rd, wr)
        ins = getattr(self.eng[e], name)(**kw)
        self.cnt[e] += 1
        ins.then_inc(self.sems[e], 1)
        self._mark((e, self.cnt[e]), rd, wr)
        self.ninstr += 1
        return ins

    def dma(self, q, out, in_, rd=None, wr=None, **kw):
        if rd is None:
            rd = self._keys({"in_": in_}, ("in_",))
        if wr is None:
            wr = self._keys({"out": out}, ("out",))
        self._deps(q, rd, wr)
        k = self.dq[q][self.dqi[q]]
        self.dqi[q] = (self.dqi[q] + 1) % len(self.dq[q])
        if self.duse[k] > 0:
            self._wait(q, (("d", k), 16 * self.duse[k]))
        self.eng[q].dma_start(out=out, in_=in_, **kw).then_inc(self.sems[("d", k)], 16)
        self.duse[k] += 1
        self._mark((("d", k), 16 * self.duse[k]), rd, wr)
        self.ninstr += 1

    def barrier(self):
        toks = [(e, self.cnt[e]) for e in self.ENG if self.cnt[e] > 0]
        toks += [(("d", k), 16 * self.duse[k]) for k in range(NDMA) if self.duse[k] > 0]
        for e in self.ENG:
            for t in toks:
                self._wait(e, t)

    def mm(self, out, lhsT, rhs, start=True, stop=True, **kw):
        return self.op("pe", "matmul", out=out, lhsT=lhsT, rhs=rhs, start=start, stop=stop, **kw)

    def tr(self, out, in_, ident):
        return self.op("pe", "transpose", out=out, in_=in_, identity=ident)

    def act(self, out, in_, func, e="act", **kw):
        return self.op(e, "activation", out=out, in_=in_, func=func, **kw)

    def tt(self, out, in0, in1, op, e="dve"):
        return self.op(e, "tensor_tensor", out=out, in0=in0, in1=in1, op=op)

    def ts(self, out, in0, s1, op0, s2=None, op1=None, e="dve", **kw):
        if op1 is None:
            return self.op(e, "tensor_scalar", out=out, in0=in0, scalar1=s1, scalar2=None, op0=op0, **kw)
        return self.op(e, "tensor_scalar", out=out, in0=in0, scalar1=s1, scalar2=s2, op0=op0, op1=op1, **kw)

    def stt(self, out, in0, scalar, in1, op0, op1):
        return self.op("dve", "scalar_tensor_tensor", out=out, in0=in0, scalar=scalar, in1=in1, op0=op0, op1=op1)

    def cp(self, out, in_, e="dve"):
        if e == "act":
            return self.op("act", "activation", out=out, in_=in_, func=AF.Copy)
        return self.op(e, "tensor_copy", out=out, in_=in_)


class Ring:
    def __init__(self, items):
        self.items = items
        self.i = 0

    def next(self):
        t = self.items[self.i]
        self.i = (self.i + 1) % len(self.items)
        return t


def bc(ap, shape):
    return ap.broadcast_to(list(shape))


CI = dict(ident=0, ones=128, blk64=256, le=384, lt=512, ge=640, gt=768, negf=896, negb=1024, cmask=1152,
          relf=1664, relb=1792, lp1=1920, rl=2048)
NCONST = 2048 + 128


def make_consts():
    c = np.zeros((128, NCONST), np.float32)
    i = np.arange(128)
    s, t = i[:, None], i[None, :]
    c[:, 0:128] = np.eye(128)
    c[:, 128:256] = 1.0
    c[:, 256:384] = (s // 64 == t // 64)
    c[:, 384:512] = (s <= t)
    c[:, 512:640] = (s < t)
    c[:, 640:768] = (s >= t)
    c[:, 768:896] = (s > t)
    c[:, 896:1024] = np.where(s <= t, 0.0, -30000.0)
    c[:, 1024:1152] = np.where(s >= t, 0.0, -30000.0)
    cm = np.ones(512, np.float32)
    cm[::128] = 0.0
    c[:, 1152:1664] = cm[None, :]
    c[:, 1664:1792] = np.maximum(t - s, 0)
    c[:, 1792:1920] = np.maximum(s - t, 0)
    c[:, 1920:2048] = t + 1 + 0 * s
    c[:, 2048:2176] = 128 - t + 0 * s
    return c


def make_rope(T):
    n = 32
    pos = np.arange(T)
    row = (pos // GW).astype(np.float32)
    col = (pos % GW).astype(np.float32)
    inv = (np.float32(10000.0) ** (-np.arange(n, dtype=np.float32) / n)).astype(np.float32)
    ang = np.concatenate([row[:, None] * inv, col[:, None] * inv], -1).astype(np.float32)
    return np.concatenate([np.cos(ang), np.sin(ang)], -1).astype(np.float32)


class Ctx:
    pass


def token_tiles(T, w=512):
    tiles = [(g, min(w, TC - g)) for g in range(0, TC, w)]
    tiles += [(TC + g, min(w, T - g)) for g in range(0, T, w)]
    return tiles


def build(T, dbg=()):
    P = PB()
    nc = P.nc
    C = Ctx()
    C.P, C.T, C.TT = P, T, T + TC
    TT = C.TT
    C.dbg = dbg

    def din(name, shape, dt=F32):
        return nc.dram_tensor(name, list(shape), dt, kind="ExternalInput")

    def dscr(name, shape, dt=F32):
        kind = "ExternalOutput" if name in dbg else "Internal"
        return nc.dram_tensor(name, list(shape), dt, kind=kind)

    I = C.I = {}
    I["x"] = din("x", [T, D])
    I["ctx"] = din("ctx", [TC, D])
    I["cT"] = din("cT", [128, 16])
    I["consts"] = din("consts", [128, NCONST])
    I["rope"] = din("rope", [T, 128])
    I["pvec"] = din("pvec", [128, NPV])
    for nm, shp in WSHAPES.items():
        I[nm] = din(nm, shp)
    C.out = nc.dram_tensor("out", [T, D], F32, kind="ExternalOutput")
    S = C.S = {}
    S["modbc"] = dscr("modbc", [2, 2, 6 * D])
    S["xcur"] = dscr("xcur", [TT, D])
    S["hT"] = dscr("hT", [1 + (T + 511) // 512, 128, 8, 512], BF16)
    S["F0"] = dscr("F0", [28 * 128, TT])
    for d in "fb":
        for nm in ("AT", "BT", "KT", "RT"):
            S[nm + d] = dscr(nm + d, [128, 4, TT], BF16)
        S["BTM" + d] = dscr("BTM" + d, [TT, 512], BF16)
        S["KTM" + d] = dscr("KTM" + d, [TT, 512], BF16)
        S["LAMC" + d] = dscr("LAMC" + d, [128, 4, TT // 128])
    S["VTM"] = dscr("VTM", [TT, 512], BF16)
    S["BONUS"] = dscr("BONUS", [TT, 512])
    S["GATE"] = dscr("GATE", [TT, 512])
    S["YRK"] = dscr("YRK", [TT, 512])
    S["MIX"] = dscr("MIX", [TT, 2048], BF16)
    S["XSTM"] = dscr("XSTM", [TT, 512])
    S["ZTM"] = dscr("ZTM", [TT, 512])
    S["BMTM"] = dscr("BMTM", [TT, 256], BF16)
    S["DTTM"] = dscr("DTTM", [TT, 32])
    S["BCT"] = dscr("BCT", [128, 4, TT], BF16)
    S["YSD"] = dscr("YSD", [TT, 512])
    S["ACTT"] = dscr("ACTT", [128, 22, TT], BF16)
    S["QT"] = dscr("QT", [128, 8, TT], BF16)
    S["KT1"] = dscr("KT1", [128, 8, TT], BF16)
    S["KTM1"] = dscr("KTM1", [TT, 1024], BF16)
    S["VTM1"] = dscr("VTM1", [TT, 2048], BF16)
    S["GS1"] = dscr("GS1", [TT, 2048], BF16)
    S["YRET"] = dscr("YRET", [TT, 2048])

    C.cst = P.sb("cst", [128, NCONST])
    P.dma("sp", C.cst[:], I["consts"].ap())
    C.pv = P.sb("pvec", [128, NPV])
    P.dma("sp", C.pv[:], I["pvec"].ap())
    C.identb = P.sb("identb", [128, 128], BF16)
    P.cp(C.identb[:], C.cst[:, 0:128])
    C.cmaskb = None

    phase_mod(C)
    for layer in range(2):
        C.layer = layer
        phase_norm(C, layer, 0, first=(layer == 0))
        if layer == 0:
            phase_proj0(C)
            if "STOP_B0" in dbg:
                break
            if "SKIP_RWKV" not in dbg:
                phase_rwkv_feat(C)
            if "STOP_C0" in dbg:
                break
            phase_ssd_prep(C)
            phase_scans0(C)
            if "STOP_E0" in dbg:
                break
            phase_outproj(C, 0, 1024, "ev_w_out", True, False)
            if "STOP_F0" in dbg:
                break
            if "STOP_G0a" in dbg:
                break
            phase_ffn_up(C, 0, False)
            if "STOP_G0b" in dbg:
                break
            phase_ffn_down(C, 0, False, False)
            if "STOP_G0" in dbg:
                break
        else:
            phase_ret_proj(C)
            if "STOP_R1" in dbg:
                break
            phase_ret_scan(C)
            if "STOP_S1" in dbg:
                break
            phase_outproj(C, 1, 2048, "ret_w_out", False, True)
            if "STOP_F1" in dbg:
                break
            phase_ffn_up(C, 1, True)
            phase_ffn_down(C, 1, True, True)
    P.barrier()
    P.es.close()
    return nc


WSHAPES = dict(
    mod_w=[2, D, 6 * D], mod_b=[2, 6 * D], norm1_g=[2, D], norm2_g=[2, D],
    ffn_w_up=[2, D, 5632], ffn_conv_w=[2, 3, 3, 2816], ffn_conv_b=[2, 2816], ffn_w_down=[2, 2816, D],
    ev_w_in=[1, D, 3472], ev_mu_prev=[1, 1920], ev_mu_next=[1, 1920],
    rk_w0_f=[1, 512], rk_w0_b=[1, 512], rk_w2_f=[1, 64, 512], rk_w2_b=[1, 64, 512],
    rk_a0_f=[1, 512], rk_a0_b=[1, 512], rk_a2_f=[1, 64, 512], rk_a2_b=[1, 64, 512],
    rk_g2=[1, 128, 512], rk_k_k=[1, 512], rk_k_a=[1, 512], rk_r_k=[1, 8, 64],
    rk_ln_w=[1, 512], rk_ln_b=[1, 512], ssd_conv_w=[1, 3, 1024], ssd_conv_b=[1, 1024],
    ssd_dt_bias_f=[1, 8], ssd_dt_bias_b=[1, 8], ssd_a_log_f=[1, 8], ssd_a_log_b=[1, 8],
    ssd_d=[1, 8], ssd_norm_w=[1, 512], ev_w_out=[1, D, D], ret_w_in=[1, D, 6144],
    ret_log2_f=[1, 8], ret_log2_b=[1, 8], ret_w_out=[1, 2048, D], final_norm_g=[D],
)


def hT_view(C, g0, n):
    if g0 < TC:
        ti, off = 0, g0
    else:
        ti, off = 1 + (g0 - TC) // 512, (g0 - TC) % 512
    return C.S["hT"].ap()[ti][:, :, off:off + n]


def cs_(C, name):
    o = CI[name]
    return C.cst[:, o:o + 128]


def phase_mod(C):
    P, I, S = C.P, C.I, C.S
    with P.phase():
        cT = P.sb("cT", [128, 16])
        P.dma("sp", cT[:], I["cT"].ap())
        cs = P.sb("csil", [128, 16])
        P.act(cs[:], cT[:], AF.Silu)
        L = P.sb("Lbc", [128, 8, 128])
        for j in range(8):
            P.ts(L[:, j, 0:64], cs_(C, "ones")[:, 0:64], cs[:, j:j + 1], ALU.mult)
            P.ts(L[:, j, 64:128], cs_(C, "ones")[:, 0:64], cs[:, 8 + j:9 + j], ALU.mult)
        wring = P.pool_of("modw", [128, 8, 512], F32, 2)
        bring = P.pool_of("modb", [128, 512], F32, 2)
        oring = P.pool_of("modo", [128, 512], F32, 2)
        for l in range(2):
            wv = I["mod_w"].ap()[l].rearrange("(kc p) n -> p kc n", p=128)
            for n in range(12):
                w = wring.next()
                P.dma("sp", w[:], wv[:, :, n * 512:(n + 1) * 512])
                b = bring.next()
                P.dma("sp", b[:], I["mod_b"].ap()[l, n * 512:(n + 1) * 512].partition_broadcast(128))
                ps = P.ps()
                for kc in range(8):
                    P.mm(ps[:], L[:, kc, :], w[:, kc, :], start=(kc == 0), stop=(kc == 7))
                o = oring.next()
                P.tt(o[:], ps[:], b[:], ALU.add)
                for ty in range(2):
                    P.dma("pool", S["modbc"].ap()[l, ty:ty + 1, n * 512:(n + 1) * 512], o[64 * ty:64 * ty + 1, :])


def load_mod(C, layer, ty, idx, out):
    C.P.dma("sp", out, C.S["modbc"].ap()[layer, ty, idx * D:(idx + 1) * D].partition_broadcast(128))


def phase_norm(C, layer, which, first=False, lat_only=False):
    P, I, S, T, TT = C.P, C.I, C.S, C.T, C.TT
    gname = "norm1_g" if which == 0 else "norm2_g"
    with P.phase():
        G, SH = {}, {}
        g = P.sb("ng", [128, D])
        P.dma("sp", g[:], I[gname].ap()[layer].partition_broadcast(128))
        for ty in range(2):
            G[ty] = P.sb("G", [128, D])
            SH[ty] = P.sb("SH", [128, D])
            load_mod(C, layer, ty, 3 * which + 1, G[ty][:])
            load_mod(C, layer, ty, 3 * which + 0, SH[ty][:])
            P.stt(G[ty][:], G[ty][:], 1.0, g[:], ALU.add, ALU.mult)
        xr = P.pool_of("nx", [128, D], F32, 3)
        junk = P.sb("njunk", [128, D], BF16)
        hr = P.pool_of("nh", [128, D], BF16, 2)
        hTr = P.pool_of("nhT", [128, 8, 128], BF16, 3)
        st = P.pool_of("nst", [128, 4], F32, 3)
        for i in range(TT // 128):
            g0 = i * 128
            ty = 1 if g0 < TC else 0
            if lat_only and ty == 1:
                continue
            if first:
                src = I["ctx"].ap()[g0:g0 + 128, :] if ty == 1 else I["x"].ap()[g0 - TC:g0 - TC + 128, :]
            else:
                src = S["xcur"].ap()[g0:g0 + 128, :]
            x = xr.next()
            P.dma("sp", x[:], src)
            s = st.next()
            P.act(junk[:], x[:], AF.Square, accum_out=s[:, 0:1])
            P.act(s[:, 1:2], s[:, 0:1], AF.Sqrt, scale=1.0 / D, bias=1e-6)
            P.op("dve", "reciprocal", out=s[:, 2:3], in_=s[:, 1:2])
            P.stt(x[:], x[:], s[:, 2:3], G[ty][:], ALU.mult, ALU.mult)
            h = hr.next()
            P.tt(h[:], x[:], SH[ty][:], ALU.add)
            ps = P.ps()
            psb = ps[:].bitcast(BF16)
            for kc in range(8):
                P.tr(psb[:, kc * 128:(kc + 1) * 128], h[:, kc * 128:(kc + 1) * 128], C.identb[:])
            hT = hTr.next()
            P.cp(hT[:].rearrange("p a b -> p (a b)"), psb, e="act")
            P.dma("pool", hT_view(C, g0, 128), hT[:])


PV = {}
_o = 0
for _n, _w in (("mu_prev", 15), ("mu_next", 15), ("conv_w", 24), ("conv_b", 8), ("k_k", 4), ("k_a", 4), ("r_k", 4),
               ("w0_f", 4), ("w0_b", 4), ("a0_f", 4), ("a0_b", 4), ("dt_bias", 1), ("a_log", 1), ("ssd_d", 1),
               ("ffn_cw0", 198), ("ffn_cb0", 22), ("ffn_cw1", 198), ("ffn_cb1", 22)):
    PV[_n] = (_o, _w)
    _o += _w
NPV = _o


def fm(v):
    v = np.asarray(v, np.float32).reshape(-1)
    return np.ascontiguousarray(v.reshape(-1, 128).T)


def make_pvec(inp):
    pv = np.zeros((128, NPV), np.float32)

    def put(name, arr):
        o, w = PV[name]
        assert arr.shape[1] == w, (name, arr.shape, w)
        pv[:arr.shape[0], o:o + w] = arr
    put("mu_prev", fm(inp["ev_mu_prev"][0]))
    put("mu_next", fm(inp["ev_mu_next"][0]))
    put("conv_w", np.concatenate([fm(inp["ssd_conv_w"][0, j]) for j in range(3)], 1))
    put("conv_b", fm(inp["ssd_conv_b"][0]))
    for n, k in (("k_k", "rk_k_k"), ("k_a", "rk_k_a"), ("r_k", "rk_r_k"), ("w0_f", "rk_w0_f"), ("w0_b", "rk_w0_b"),
                 ("a0_f", "rk_a0_f"), ("a0_b", "rk_a0_b")):
        put(n, fm(inp[k][0]))
    put("dt_bias", np.concatenate([inp["ssd_dt_bias_f"][0], inp["ssd_dt_bias_b"][0]])[:, None].astype(np.float32))
    put("a_log", np.concatenate([inp["ssd_a_log_f"][0], inp["ssd_a_log_b"][0]])[:, None].astype(np.float32))
    put("ssd_d", np.concatenate([inp["ssd_d"][0], inp["ssd_d"][0]])[:, None].astype(np.float32))
    for l in range(2):
        cw = inp["ffn_conv_w"][l].reshape(9, 2816)
        put(f"ffn_cw{l}", np.concatenate([fm(cw[j]) for j in range(9)], 1))
        put(f"ffn_cb{l}", fm(inp["ffn_conv_b"][l]))
    return pv


def pv_(C, name, j=0, n=1):
    o, w = PV[name]
    return C.pv[:, o + j:o + j + n]


def phase_proj0(C):
    P, I, S, T, TT = C.P, C.I, C.S, C.T, C.TT
    PL = min(4096, max(T // 2, 128))
    parts = [(0, TC, 0, TC)]
    for a in range(0, T, PL):
        oA, oB = TC + a, TC + min(a + PL, T)
        parts.append((max(oA - 1, TC), min(oB + 1, TC + T), oA, oB))
    ML = PL + 2
    with P.phase():
        c0 = P.sb("c0", [128, 15])
        P.tt(c0[:], pv_(C, "mu_prev", 0, 15), pv_(C, "mu_next", 0, 15), ALU.add)
        P.ts(c0[:], c0[:], -1.0, ALU.mult, 1.0, ALU.add)
        hp_ = P.sb("phT", [128, 8, ML], BF16)
        ur = P.pool_of("urow", [128, ML + 2], F32, 2)
        t1r = P.pool_of("t1row", [128, ML + 2], F32, 2)
        wfr = P.pool_of("wf", [128, 8, 128], F32, 2)
        wbr = P.pool_of("wb", [128, 8, 128], BF16, 2)
        wv = I["ev_w_in"].ap()[0].rearrange("(kc p) n -> p kc n", p=128)

        def loadw(j):
            ncol = 128 if j < 27 else 16
            wf, wb = wfr.next(), wbr.next()
            P.dma("sp", wf[:, :, 0:ncol], wv[:, :, j * 128:j * 128 + ncol])
            P.cp(wb[:, :, 0:ncol], wf[:, :, 0:ncol], e="pool")
            return wb

        for (gA, gB, oA, oB) in parts:
            NL, oL, o0 = gB - gA, oB - oA, oA - gA
            g = gA
            while g < gB:
                tile_end = TC if g < TC else TC + ((g - TC) // 512 + 1) * 512
                n = min(gB, tile_end) - g
                if n == 1:
                    P.dma("sp", hp_[:, :, g - gA:g - gA + n], hT_view(C, g, n), allow_slow_non_contiguous=True)
                else:
                    P.dma("sp", hp_[:, :, g - gA:g - gA + n], hT_view(C, g, n))
                g += n
            for u in ur.items:
                P.op("dve", "memset", ap=u[:, 0:1], constant=0.0)
                P.op("dve", "memset", ap=u[:, 1 + NL:2 + NL], constant=0.0)
            wnext = loadw(0)
            for j in range(28):
                ncol = 128 if j < 27 else 16
                wb = wnext
                if j + 1 < 28:
                    wnext = loadw(j + 1)
                u = ur.next()
                for t0 in range(0, NL, 512):
                    n = min(512, NL - t0)
                    ps = P.ps()
                    for kc in range(8):
                        P.mm(ps[0:ncol, 0:n], wb[:, kc, 0:ncol], hp_[:, kc, t0:t0 + n], start=(kc == 0), stop=(kc == 7))
                    P.cp(u[0:ncol, 1 + t0:1 + t0 + n], ps[0:ncol, 0:n], e="act")
                s0 = 1 + o0
                dst = S["F0"].ap()[j * 128:j * 128 + ncol, oA:oB]
                if j < 15:
                    t1 = t1r.next()
                    P.act(t1[:, s0:s0 + oL], u[:, s0:s0 + oL], AF.Identity, scale=c0[:, j:j + 1])
                    P.stt(t1[:, s0:s0 + oL], u[:, s0 - 1:s0 - 1 + oL], pv_(C, "mu_prev", j), t1[:, s0:s0 + oL], ALU.mult, ALU.add)
                    P.stt(t1[:, s0:s0 + oL], u[:, s0 + 1:s0 + 1 + oL], pv_(C, "mu_next", j), t1[:, s0:s0 + oL], ALU.mult, ALU.add)
                    P.dma("pool", dst, t1[:, s0:s0 + oL])
                elif 19 <= j < 27:
                    c = j - 19
                    t1 = t1r.next()
                    P.act(t1[:, s0:s0 + oL], u[:, s0:s0 + oL], AF.Identity, scale=pv_(C, "conv_w", 8 + c), bias=pv_(C, "conv_b", c))
                    P.stt(t1[:, s0:s0 + oL], u[:, s0 - 1:s0 - 1 + oL], pv_(C, "conv_w", c), t1[:, s0:s0 + oL], ALU.mult, ALU.add)
                    P.stt(t1[:, s0:s0 + oL], u[:, s0 + 1:s0 + 1 + oL], pv_(C, "conv_w", 16 + c), t1[:, s0:s0 + oL], ALU.mult, ALU.add)
                    P.act(t1[:, s0:s0 + oL], t1[:, s0:s0 + oL], AF.Silu)
                    P.dma("pool", dst, t1[:, s0:s0 + oL])
                else:
                    P.dma("pool", dst, u[0:ncol, s0:s0 + oL])


def make_in_maps(inp, T, ncores):
    consts = make_consts()
    rope = make_rope(T)
    pvec = make_pvec(inp)
    shared = {k: np.ascontiguousarray(np.asarray(inp[k], np.float32)) for k in WSHAPES}
    maps = []
    for b in range(ncores):
        m = dict(shared)
        m["x"] = np.ascontiguousarray(inp["x"][b], dtype=np.float32)
        m["ctx"] = np.ascontiguousarray(inp["ctx"][b], dtype=np.float32)
        m["cT"] = np.ascontiguousarray(np.concatenate([fm(inp["c"][b]), fm(inp["c_ctx"])], 1))
        m["consts"] = consts
        m["rope"] = rope
        m["pvec"] = pvec
        maps.append(m)
    return maps


def kernel(**inputs):
    B, T, _ = inputs["x"].shape
    nc = build(T)
    maps = make_in_maps(inputs, T, B)
    res = run_bass_kernel_spmd(nc, maps, core_ids=list(range(B)))
    return np.stack([np.asarray(r["out"], np.float32) for r in res.results], 0)


LDK = 0.6065306597126334


def f0rows(C, r0, nchunk, g0, n):
    return C.S["F0"].ap()[r0:r0 + 128 * nchunk, g0:g0 + n].rearrange("(c p) t -> p c t", p=128)


def phase_rwkv_feat(C):
    P, I, S, T, TT = C.P, C.I, C.S, C.T, C.TT
    W = 256
    tiles = token_tiles(T, W)
    identf = cs_(C, "ident")
    blk = cs_(C, "blk64")
    with P.phase():
        wtmp = P.sb("lw32", [128, 3, 512])
        P.dma("sp", wtmp[0:64, 0, :], I["rk_w2_f"].ap()[0])
        P.dma("sp", wtmp[64:128, 0, :], I["rk_w2_b"].ap()[0])
        P.dma("sp", wtmp[0:64, 1, :], I["rk_a2_f"].ap()[0])
        P.dma("sp", wtmp[64:128, 1, :], I["rk_a2_b"].ap()[0])
        P.dma("sp", wtmp[:, 2, :], I["rk_g2"].ap()[0])
        wl = P.sb("lwb", [128, 3, 512], BF16)
        P.cp(wl[:], wtmp[:])
        omka = P.sb("omka", [128, 4])
        P.ts(omka[:], pv_(C, "k_a", 0, 4), -1.0, ALU.mult, 1.0, ALU.add)
        cmask = C.cst[:, CI["cmask"]:CI["cmask"] + W]
        Rr = P.pool_of("fR", [128, 4, W], F32, 2)
        Kr = P.pool_of("fK", [128, 4, W], F32, 2)
        Vr = P.pool_of("fV", [128, 4, W], F32, 2)
        KKr = P.pool_of("fKK", [128, 4, W], F32, 2)
        WAr = P.pool_of("fWA", [128, 2, W], F32, 2)
        GDr = P.pool_of("fGD", [128, W], F32, 2)
        tmp = P.pool_of("ftmp", [128, 4, W], F32, 12)
        t1r = P.pool_of("ft1", [128, W], F32, 4)
        b16 = P.pool_of("fb16", [128, 4, W], BF16, 10)
        s16 = P.pool_of("fs16", [128, W], BF16, 6)
        stf = P.pool_of("fstf", [128, 512], F32, 4)
        stb = P.pool_of("fstb", [128, 512], BF16, 6)
        lamr = P.pool_of("flam", [128, 4, W // 128], F32, 4)
        for (g0, n) in tiles:
            nch = n // 128
            R, K, V, KK, WA, GD = Rr.next(), Kr.next(), Vr.next(), KKr.next(), WAr.next(), GDr.next()
            P.dma("sp", R[:, :, 0:n], f0rows(C, 0, 4, g0, n))
            P.dma("sp", K[:, :, 0:n], f0rows(C, 512, 4, g0, n))
            P.dma("sp", V[:, :, 0:n], f0rows(C, 1024, 4, g0, n))
            P.dma("sp", WA[:, :, 0:n], f0rows(C, 1536, 2, g0, n))
            P.dma("sp", GD[:, 0:n], S["F0"].ap()[1792:1920, g0:g0 + n])
            thw, adb, sgd = s16.next(), s16.next(), s16.next()
            P.act(thw[:, 0:n], WA[:, 0, 0:n], AF.Tanh)
            P.cp(adb[:, 0:n], WA[:, 1, 0:n], e="act")
            P.act(sgd[:, 0:n], GD[:, 0:n], AF.Sigmoid)
            kq, bon, gt, sq4, rk4 = tmp.next(), tmp.next(), tmp.next(), tmp.next(), tmp.next()
            for c in range(4):
                P.ts(kq[:, c, 0:n], K[:, c, 0:n], pv_(C, "k_k", c), ALU.mult)
            P.act(sq4[:, :, 0:n], kq[:, :, 0:n], AF.Square)
            pss = []
            for c in range(4):
                ps = P.ps()
                P.mm(ps[:, 0:n], blk, sq4[:, c, 0:n])
                pss.append(ps)
            for c in range(4):
                P.stt(rk4[:, c, 0:n], R[:, c, 0:n], pv_(C, "r_k", c), K[:, c, 0:n], ALU.mult, ALU.mult)
            for c in range(4):
                P.act(sq4[:, c, 0:n], pss[c][:, 0:n], AF.Sqrt)
            pss = []
            for c in range(4):
                ps = P.ps()
                P.mm(ps[:, 0:n], blk, rk4[:, c, 0:n])
                pss.append(ps)
            P.ts(sq4[:, :, 0:n], sq4[:, :, 0:n], 1e-12, ALU.max)
            P.op("dve", "reciprocal", out=sq4[:, :, 0:n], in_=sq4[:, :, 0:n])
            P.tt(KK[:, :, 0:n], kq[:, :, 0:n], sq4[:, :, 0:n], ALU.mult)
            for c in range(4):
                P.tt(bon[:, c, 0:n], pss[c][:, 0:n], V[:, c, 0:n], ALU.mult)
            pss = []
            for c in range(4):
                ps = P.ps()
                P.mm(ps[:, 0:n], wl[:, 2, c * 128:(c + 1) * 128], sgd[:, 0:n])
                pss.append(ps)
            for c in range(4):
                P.cp(gt[:, c, 0:n], pss[c][:, 0:n], e="act")
            vb = b16.next()
            P.cp(vb[:, :, 0:n], V[:, :, 0:n], e="act")
            for k in range(nch):
                gk = g0 + k * 128
                for (src, dst) in ((bon, S["BONUS"]), (gt, S["GATE"])):
                    ps = P.ps()
                    for c in range(4):
                        P.tr(ps[:, c * 128:(c + 1) * 128], src[:, c, k * 128:(k + 1) * 128], identf)
                    st = stf.next()
                    P.cp(st[:], ps[:], e="act")
                    P.dma("pool", dst.ap()[gk:gk + 128, :], st[:])
                ps = P.ps()
                psb = ps[:].bitcast(BF16)
                for c in range(4):
                    P.tr(psb[:, c * 128:(c + 1) * 128], vb[:, c, k * 128:(k + 1) * 128], C.identb[:])
                st = stb.next()
                P.cp(st[:], psb[:, 0:512], e="act")
                P.dma("pool", S["VTM"].ap()[gk:gk + 128, :], st[:])
            for d, dn in enumerate("fb"):
                dp = slice(0, 64) if d == 0 else slice(64, 128)
                sig, icl = tmp.next(), tmp.next()
                for (wi, src, dst, bn) in ((0, thw, sig, "w0_"), (1, adb, icl, "a0_")):
                    pss = []
                    for c in range(4):
                        ps = P.ps()
                        P.mm(ps[:, 0:n], wl[dp, wi, c * 128:(c + 1) * 128], src[dp, 0:n])
                        pss.append(ps)
                    for c in range(4):
                        P.act(dst[:, c, 0:n], pss[c][:, 0:n], AF.Sigmoid, bias=pv_(C, bn + dn, c))
                cs = tmp.next()
                for c in range(4):
                    P.op("dve", "tensor_tensor_scan", out=cs[:, c, 0:n], data0=cmask[:, 0:n], data1=sig[:, c, 0:n],
                         initial=0.0, op0=ALU.mult, op1=ALU.add)
                v4 = lambda t: t[:, :, 0:n].rearrange("p c (k t) -> p c k t", t=128)
                tot = v4(cs)[:, :, :, 127:128]
                lam = lamr.next()
                P.act(lam[:, :, 0:nch], v4(cs)[:, :, :, 127], AF.Exp, scale=-LDK)
                P.dma("pool", S["LAMC" + dn].ap()[:, :, g0 // 128:g0 // 128 + nch], lam[:, :, 0:nch])
                if d == 1:
                    cs2 = tmp.next()
                    P.tt(cs2[:, :, 0:n], sig[:, :, 0:n], cs[:, :, 0:n], ALU.subtract)
                    P.tt(v4(cs2), v4(cs2), bc(tot, [128, 4, nch, 128]), ALU.add)
                    cs = cs2
                ex = tmp.next()
                P.tt(ex[:, :, 0:n], cs[:, :, 0:n], sig[:, :, 0:n], ALU.subtract)
                P.act(ex[:, :, 0:n], ex[:, :, 0:n], AF.Exp, scale=-LDK)
                ef, ei = tmp.next(), tmp.next()
                P.act(ef[:, :, 0:n], cs[:, :, 0:n], AF.Exp, scale=-LDK)
                P.act(ei[:, :, 0:n], cs[:, :, 0:n], AF.Exp, scale=LDK)
                kd = tmp.next()
                for c in range(4):
                    P.ts(kd[:, c, 0:n], icl[:, c, 0:n], pv_(C, "k_a", c), ALU.mult, omka[:, c:c + 1], ALU.add)
                P.tt(kd[:, :, 0:n], kd[:, :, 0:n], K[:, :, 0:n], ALU.mult)
                at, bt, kt, rt = b16.next(), b16.next(), b16.next(), b16.next()
                P.stt(at[:, :, 0:n], KK[:, :, 0:n], -1.0, ex[:, :, 0:n], ALU.mult, ALU.mult)
                P.tt(icl[:, :, 0:n], icl[:, :, 0:n], KK[:, :, 0:n], ALU.mult)
                P.tt(bt[:, :, 0:n], icl[:, :, 0:n], ei[:, :, 0:n], ALU.mult)
                P.tt(kt[:, :, 0:n], kd[:, :, 0:n], ei[:, :, 0:n], ALU.mult)
                P.tt(rt[:, :, 0:n], R[:, :, 0:n], ef[:, :, 0:n], ALU.mult)
                for (src, nm) in ((at, "AT"), (bt, "BT"), (kt, "KT"), (rt, "RT")):
                    P.dma("pool", S[nm + dn].ap()[:, :, g0:g0 + n], src[:, :, 0:n])
                for k in range(nch):
                    gk = g0 + k * 128
                    for (src, nm) in ((bt, "BTM"), (kt, "KTM")):
                        ps = P.ps()
                        psb = ps[:].bitcast(BF16)
                        for c in range(4):
                            P.tr(psb[:, c * 128:(c + 1) * 128], src[:, c, k * 128:(k + 1) * 128], C.identb[:])
                        st = stb.next()
                        P.cp(st[:], psb[:, 0:512], e="act")
                        P.dma("pool", S[nm + dn].ap()[gk:gk + 128, :], st[:])


def bcast_row(C, out, vec_ap):
    C.P.dma("sp", out, vec_ap.partition_broadcast(128))


def gen_rwkv_scan(C):
    P, I, S, T, TT = C.P, C.I, C.S, C.T, C.TT
    NCH = TT // 128
    NCC = TC // 128
    hp = lambda h: slice(0, 64) if h % 2 == 0 else slice(64, 128)
    mk = {}
    for nm in ("le", "lt", "ge", "gt"):
        mk[nm] = P.sb("mk" + nm, [128, 4, 128])
        for i in range(4):
            P.cp(mk[nm][:, i, :], cs_(C, nm))
    id4 = P.sb("id4", [128, 4, 128], F32)
    for i in range(4):
        P.cp(id4[:, i, :], cs_(C, "ident"))
    lnw = P.sb("lnw", [128, 512])
    lnb = P.sb("lnb", [128, 512])
    bcast_row(C, lnw[:], I["rk_ln_w"].ap()[0])
    bcast_row(C, lnb[:], I["rk_ln_b"].ap()[0])
    ST = P.sb("ST", [128, 4, 64])
    STz = P.sb("STz", [128, 8, 64], BF16)
    lam = P.sb("lamall", [128, 4, NCH])
    fr = {nm: P.pool_of("s" + nm, [128, 4, 128], BF16, 2) for nm in ("at", "bt", "kt", "rt")}
    tr_ = {nm: P.pool_of("s" + nm, [128, 512], BF16, 2) for nm in ("btm", "ktm", "v")}
    Pr = P.pool_of("sP", [128, 4, 128], F32, 4)
    PTr = P.pool_of("sPT", [128, 4, 128], F32, 4)
    TTr = P.pool_of("sTT", [128, 4, 128], F32, 4)
    TTfin = P.pool_of("sTTf", [128, 4, 128], F32, 4)
    Mr = {nm: P.pool_of("s" + nm, [128, 4, 128], BF16, 4) for nm in ("lak", "nrb", "nrk")}
    zr = P.pool_of("sz", [128, 512], F32, 2)
    ur = P.pool_of("su", [128, 512], BF16, 2)
    yr = P.pool_of("sy", [128, 512], F32, 5)
    wk = P.pool_of("swk", [128, 512], F32, 5)
    sm = P.pool_of("ssm", [128, 32], F32, 3)
    ob = P.pool_of("sob", [128, 512], BF16, 2)

    def prep(d, ci):
        dn = "fb"[d]
        g0 = ci * 128
        t = {}
        for nm, key in (("at", "AT"), ("bt", "BT"), ("kt", "KT"), ("rt", "RT")):
            t[nm] = fr[nm].next()
            P.dma("sp", t[nm][:], S[key + dn].ap()[:, :, g0:g0 + 128])
        for nm, key in (("btm", "BTM" + dn), ("ktm", "KTM" + dn), ("v", "VTM")):
            t[nm] = tr_[nm].next()
            P.dma("sp", t[nm][:], S[key].ap()[g0:g0 + 128, :])
        m_l, m_lT, m_n = (("gt", "lt", "le") if d == 0 else ("lt", "gt", "ge"))
        t["TT"], t["lak"], t["nrb"], t["nrk"] = [], [], [], []

        def prod(half, lhs, rhs, mask, ring):
            ps = P.ps()
            for i in range(4):
                P.mm(ps[:, i * 128:(i + 1) * 128], t[lhs][hp(half), i, :], t[rhs][hp(half), i, :])
            o = ring.next()
            P.tt(o[:].rearrange("p a b -> p (a b)"), ps[:], mk[mask][:].rearrange("p a b -> p (a b)"), ALU.mult)
            return o
        Pm, PTm, TTm = [None, None], [None, None], [None, None]
        for half in range(2):
            Pm[half] = prod(half, "at", "bt", m_l, Pr)
            PTm[half] = prod(half, "bt", "at", m_lT, PTr)
            TTm[half] = TTr.next()
            P.tt(TTm[half][:], PTm[half][:], id4[:], ALU.add)
            t["lak"].append(prod(half, "kt", "at", m_lT, Mr["lak"]))
            t["nrb"].append(prod(half, "bt", "rt", m_n, Mr["nrb"]))
            t["nrk"].append(prod(half, "kt", "rt", m_n, Mr["nrk"]))
        for lvl in range(1, 7):
            Pn = [None, None]
            for half in range(2):
                ps = P.ps()
                for i in range(4):
                    P.mm(ps[:, i * 128:(i + 1) * 128], PTm[half][:, i, :], Pm[half][:, i, :])
                Pn[half] = Pr.next()
                P.cp(Pn[half][:].rearrange("p a b -> p (a b)"), ps[:], e="act")
            for half in range(2):
                ps = P.ps()
                for i in range(4):
                    P.mm(ps[:, i * 128:(i + 1) * 128], Pn[half][:, i, :], TTm[half][:, i, :])
                TTn = TTr.next() if lvl < 6 else TTfin.next()
                P.tt(TTn[:].rearrange("p a b -> p (a b)"), ps[:], TTm[half][:].rearrange("p a b -> p (a b)"), ALU.add)
                TTm[half] = TTn
                if lvl < 6:
                    ps = P.ps()
                    for i in range(4):
                        P.tr(ps[:, i * 128:(i + 1) * 128], Pn[half][:, i, :], cs_(C, "ident"))
                    PTn = PTr.next()
                    P.cp(PTn[:].rearrange("p a b -> p (a b)"), ps[:], e="act")
                    PTm[half] = PTn
                Pm[half] = Pn[half]
        t["TT"] = TTm
        return t
        for half in range(2):
            def prod(lhs, rhs, mask, ring, addI=False):
                ps = P.ps()
                for i in range(4):
                    P.mm(ps[:, i * 128:(i + 1) * 128], t[lhs][hp(half), i, :], t[rhs][hp(half), i, :])
                o = ring.next()
                P.tt(o[:].rearrange("p a b -> p (a b)"), ps[:], mk[mask][:].rearrange("p a b -> p (a b)"), ALU.mult)
                return o
            Pm = prod("at", "bt", m_l, Pr)
            PTm = prod("bt", "at", m_lT, PTr)
            TTm = TTr.next()
            P.tt(TTm[:], PTm[:], id4[:], ALU.add)
            t["lak"].append(prod("kt", "at", m_lT, Mr["lak"]))
            t["nrb"].append(prod("bt", "rt", m_n, Mr["nrb"]))
            t["nrk"].append(prod("kt", "rt", m_n, Mr["nrk"]))
            if "PREP_PROD" in C.dbg:
                continue
            for lvl in range(1, 7):
                ps = P.ps()
                for i in range(4):
                    P.mm(ps[:, i * 128:(i + 1) * 128], PTm[:, i, :], Pm[:, i, :])
                Pn = Pr.next()
                P.cp(Pn[:].rearrange("p a b -> p (a b)"), ps[:], e="act")
                if lvl < 6:
                    ps = P.ps()
                    for i in range(4):
                        P.mm(ps[:, i * 128:(i + 1) * 128], Pm[:, i, :], PTm[:, i, :])
                    PTn = PTr.next()
                    P.cp(PTn[:].rearrange("p a b -> p (a b)"), ps[:], e="act")
                else:
                    PTn = None
                ps = P.ps()
                for i in range(4):
                    P.mm(ps[:, i * 128:(i + 1) * 128], Pn[:, i, :], TTm[:, i, :])
                TTn = TTr.next() if lvl < 6 else TTfin.next()
                P.tt(TTn[:].rearrange("p a b -> p (a b)"), ps[:], TTm[:].rearrange("p a b -> p (a b)"), ALU.add)
                Pm, PTm, TTm = Pn, PTn, TTn
            t["TT"].append(TTm)
        return t

    def state(d, ci, t):
        g0 = ci * 128
        hc = lambda h: slice(h * 64, (h + 1) * 64)
        psZ = P.ps()
        for h in range(8):
            P.mm(psZ[:, hc(h)], t["at"][:, h // 2, :], STz[:, h, :], start=True, stop=False)
            P.mm(psZ[:, hc(h)], t["lak"][h % 2][:, h // 2, :], t["v"][:, hc(h)], start=False, stop=True)
        zb = zr.next()
        P.cp(zb[:], psZ[:], e="act")
        psU = P.ps()
        for h in range(8):
            P.mm(psU[:, hc(h)], t["TT"][h % 2][:, h // 2, :], zb[:, hc(h)])
        ub = ur.next()
        P.cp(ub[:], psU[:], e="act")
        psS = P.ps()
        for p in range(4):
            pc = slice(p * 128, (p + 1) * 128)
            P.mm(psS[:, pc], t["btm"][:, pc], ub[:, pc], start=True, stop=False)
            P.mm(psS[:, pc], t["ktm"][:, pc], t["v"][:, pc], start=False, stop=True)
        psY = P.ps()
        for h in range(8):
            P.mm(psY[:, hc(h)], t["rt"][:, h // 2, :], STz[:, h, :], start=True, stop=False)
            P.mm(psY[:, hc(h)], t["nrb"][h % 2][:, h // 2, :], ub[:, hc(h)], start=False, stop=False)
            P.mm(psY[:, hc(h)], t["nrk"][h % 2][:, h // 2, :], t["v"][:, hc(h)], start=False, stop=True)
        for par in range(2):
            rows = slice(par * 64, par * 64 + 64)
            blk_ = psS[rows, :].rearrange("p (a b) -> p a b", b=128)[:, :, par * 64:(par + 1) * 64]
            P.tt(ST[rows, :, :], ST[rows, :, :], blk_, ALU.add)
        P.tt(ST[:], ST[:], bc(lam[:, :, ci:ci + 1], [128, 4, 64]), ALU.mult)
        for par in range(2):
            rows = slice(par * 64, par * 64 + 64)
            P.cp(STz[rows, par::2, :], ST[rows, :, :], e="act")
        y = yr.next()
        P.cp(y[:], psY[:], e="act")
        return lambda: finish(d, g0, y)

    def finish(d, g0, y):
        if d == 1:
            P.dma("pool", S["YRK"].ap()[g0:g0 + 128, :], y[:])
            return
        yb = yr.next()
        P.dma("sp", yb[:], S["YRK"].ap()[g0:g0 + 128, :])
        bon, gat = wk.next(), wk.next()
        P.dma("sp", bon[:], S["BONUS"].ap()[g0:g0 + 128, :])
        P.dma("sp", gat[:], S["GATE"].ap()[g0:g0 + 128, :])
        P.tt(y[:], y[:], yb[:], ALU.add)
        if "YRKSUM" in C.dbg:
            P.dma("pool", S["YRK"].ap()[g0:g0 + 128, :], y[:])
        s = sm.next()
        y3 = y[:].rearrange("p (a b) -> p a b", b=64)
        P.op("dve", "tensor_reduce", out=s[:, 0:8], in_=y3, axis=AX.X, op=ALU.add)
        P.ts(s[:, 0:8], s[:, 0:8], 1.0 / 64, ALU.mult)
        P.tt(y3, y3, bc(s[:, 0:8].unsqueeze(2), [128, 8, 64]), ALU.subtract)
        sq = wk.next()
        P.tt(sq[:], y[:], y[:], ALU.mult)
        P.op("dve", "tensor_reduce", out=s[:, 8:16], in_=sq[:].rearrange("p (a b) -> p a b", b=64), axis=AX.X, op=ALU.add)
        P.act(s[:, 16:24], s[:, 8:16], AF.Sqrt, scale=1.0 / 64, bias=64e-5)
        P.op("dve", "reciprocal", out=s[:, 24:32], in_=s[:, 16:24])
        P.tt(y3, y3, bc(s[:, 24:32].unsqueeze(2), [128, 8, 64]), ALU.mult)
        P.tt(y[:], y[:], lnw[:], ALU.mult)
        P.tt(y[:], y[:], lnb[:], ALU.add)
        P.tt(y[:], y[:], bon[:], ALU.add)
        o = ob.next()
        P.tt(o[:], y[:], gat[:], ALU.mult)
        P.dma("pool", S["MIX"].ap()[g0:g0 + 128, 0:512], o[:])

    yield
    for d in (1, 0):
        if d == 0:
            P.barrier()
        order = list(range(NCH)) if d == 0 else (list(range(NCC - 1, -1, -1)) + list(range(NCH - 1, NCC - 1, -1)))
        P.dma("sp", lam[:], S["LAMC" + "fb"[d]].ap())
        P.op("dve", "memset", ap=ST[:], constant=0.0)
        P.op("dve", "memset", ap=STz[:], constant=0.0)
        nxt = prep(d, order[0])
        fin = None
        for i, ci in enumerate(order):
            cur = nxt
            if i + 1 < len(order):
                nxt = prep(d, order[i + 1])
            f2 = state(d, ci, cur)
            if fin is not None:
                fin()
            fin = f2
            yield
        fin()


def phase_ssd_prep(C):
    P, I, S, T, TT = C.P, C.I, C.S, C.T, C.TT
    W = 512
    tiles = token_tiles(T, W)
    identf = cs_(C, "ident")
    with P.phase():
        negA = P.sb("negA", [16, 1])
        P.act(negA[:], pv_(C, "a_log")[0:16, :], AF.Exp)
        P.ts(negA[:], negA[:], -1.0, ALU.mult)
        Zr = P.pool_of("pZ", [128, 4, W], F32, 2)
        Xr = P.pool_of("pX", [128, 4, W], F32, 2)
        Br = P.pool_of("pB", [128, 4, W], F32, 2)
        Dr = P.pool_of("pD", [16, W], F32, 2)
        D2 = P.pool_of("pD2", [16, 2, W], F32, 2)
        bcb = P.pool_of("pbc", [128, 4, W], BF16, 2)
        stf = P.pool_of("pstf", [128, 512], F32, 4)
        stb = P.pool_of("pstb", [128, 256], BF16, 3)
        std = P.pool_of("pstd", [128, 32], F32, 3)
        for (g0, n) in tiles:
            nch = n // 128
            Z, X, B, Dt, dd = Zr.next(), Xr.next(), Br.next(), Dr.next(), D2.next()
            P.dma("sp", Z[:, :, 0:n], f0rows(C, 1920, 4, g0, n))
            P.dma("sp", X[:, :, 0:n], f0rows(C, 2432, 4, g0, n))
            P.dma("sp", B[:, :, 0:n], f0rows(C, 2944, 4, g0, n))
            P.dma("sp", Dt[:, 0:n], S["F0"].ap()[3456:3472, g0:g0 + n])
            P.act(Dt[:, 0:n], Dt[:, 0:n], AF.Exp, bias=pv_(C, "dt_bias")[0:16, :])
            P.act(dd[:, 0, 0:n], Dt[:, 0:n], AF.Ln, bias=1.0)
            P.ts(dd[:, 1, 0:n], dd[:, 0, 0:n], negA[:, 0:1], ALU.mult)
            bb = bcb.next()
            P.cp(bb[:, :, 0:n], B[:, :, 0:n], e="act")
            P.dma("pool", S["BCT"].ap()[:, :, g0:g0 + n], bb[:, :, 0:n])
            for k in range(nch):
                gk = g0 + k * 128
                ks = slice(k * 128, (k + 1) * 128)
                for (src, dst) in ((X, "XSTM"), (Z, "ZTM")):
                    ps = P.ps()
                    for c in range(4):
                        P.tr(ps[:, c * 128:(c + 1) * 128], src[:, c, ks], identf)
                    st = stf.next()
                    P.cp(st[:], ps[:], e="act")
                    P.dma("pool", S[dst].ap()[gk:gk + 128, :], st[:])
                ps = P.ps()
                psb = ps[:].bitcast(BF16)
                for c in range(2):
                    P.tr(psb[:, c * 128:(c + 1) * 128], bb[:, c, ks], C.identb[:])
                st = stb.next()
                P.cp(st[:], psb[:, 0:256], e="act")
                P.dma("pool", S["BMTM"].ap()[gk:gk + 128, :], st[:])
                ps = P.ps()
                for j in range(2):
                    P.tr(ps[:, j * 16:(j + 1) * 16], dd[:, j, ks], identf[0:16, 0:16])
                st = std.next()
                P.cp(st[:], ps[:, 0:32], e="act")
                P.dma("pool", S["DTTM"].ap()[gk:gk + 128, :], st[:])


def gen_ssd_scan(C):
    P, I, S, T, TT = C.P, C.I, C.S, C.T, C.TT
    NCH = TT // 128
    NCC = TC // 128
    id8 = P.sb("id8", [128, 8, 128])
    neg8 = {d: P.sb("neg8", [128, 8, 128]) for d in (0, 1)}
    for i in range(8):
        P.cp(id8[:, i, :], cs_(C, "ident"))
        P.cp(neg8[0][:, i, :], cs_(C, "negf"), e="act")
        P.cp(neg8[1][:, i, :], cs_(C, "negb"), e="act")
    ones = cs_(C, "ones")
    Dv = P.sb("Dv", [128, 8])
    P.dma("sp", Dv[:], I["ssd_d"].ap()[0].partition_broadcast(128))
    nw = P.sb("nw", [128, 512])
    bcast_row(C, nw[:], I["ssd_norm_w"].ap()[0])
    H = P.sb("H", [128, 8, 64])
    Hb = P.sb("Hb", [128, 8, 64], BF16)
    xsr = P.pool_of("dxs", [128, 512], F32, 4)
    bmr = P.pool_of("dbm", [128, 256], BF16, 3)
    bcr = P.pool_of("dbc", [128, 4, 128], BF16, 3)
    dtr = P.pool_of("ddt", [128, 32], F32, 3)
    smr = P.pool_of("dsm", [128, 48], F32, 6)
    dgr = P.pool_of("ddg", [128, 8, 128], F32, 2)
    segr = P.pool_of("dseg", [128, 8, 128], F32, 2)
    cbr = P.pool_of("dcb", [128, 2, 128], F32, 2)
    wtr = P.pool_of("dwt", [128, 8, 128], BF16, 2)
    xdr = P.pool_of("dxd", [128, 8, 64], BF16, 3)
    xer = P.pool_of("dxe", [128, 8, 64], BF16, 3)
    wk = P.pool_of("dwk", [128, 512], F32, 11)
    ob = P.pool_of("dob", [128, 512], BF16, 2)
    junk = P.sb("djunk", [128, 512], BF16)

    def prep(d, ci):
        g0 = ci * 128
        do = 8 * d
        lend = 127 if d == 0 else 0
        t = {}
        xs, bm, bct, dtt, sm = xsr.next(), bmr.next(), bcr.next(), dtr.next(), smr.next()
        P.dma("sp", xs[:], S["XSTM"].ap()[g0:g0 + 128, :])
        P.dma("sp", bm[:], S["BMTM"].ap()[g0:g0 + 128, :])
        P.dma("sp", bct[:], S["BCT"].ap()[:, :, g0:g0 + 128])
        P.dma("sp", dtt[:], S["DTTM"].ap()[g0:g0 + 128, :])
        dt, dtA = dtt[:, do:do + 8], dtt[:, 16 + do:16 + do + 8]
        if "SS0" in C.dbg:
            return t
        ps = P.ps()
        P.mm(ps[:, 0:8], cs_(C, "le" if d == 0 else "ge"), dtA)
        cs = sm[:, 0:8]
        P.cp(cs, ps[:, 0:8])
        if "SS0b" in C.dbg:
            return t
        dg = dgr.next()
        P.tt(dg[:], id8[:], bc(cs.unsqueeze(2), [128, 8, 128]), ALU.mult)
        if "SS0c" in C.dbg:
            return t
        seg = segr.next()
        P.tt(seg[:], neg8[d][:], bc(cs.unsqueeze(2), [128, 8, 128]), ALU.subtract)
        eend = sm[:, 16:24]
        if "SS1" in C.dbg:
            return t
        for hf in range(2):
            psr = P.ps()
            P.mm(psr[:], ones, dg[:, hf * 4:(hf + 1) * 4, :].rearrange("p a b -> p (a b)"))
            P.tt(seg[:, hf * 4:(hf + 1) * 4, :].rearrange("p a b -> p (a b)"), seg[:, hf * 4:(hf + 1) * 4, :].rearrange("p a b -> p (a b)"), psr[:], ALU.add)
            P.act(eend[:, hf * 4:(hf + 1) * 4], psr[:].rearrange("p (a b) -> p a b", b=128)[:, :, lend], AF.Exp)
        P.act(seg[:].rearrange("p a b -> p (a b)"), seg[:].rearrange("p a b -> p (a b)"), AF.Exp)
        if "SS2" in C.dbg:
            return t
        ecs = sm[:, 8:16]
        P.act(ecs, cs, AF.Exp)
        psc = P.ps()
        for g in range(2):
            P.mm(psc[:, g * 128:(g + 1) * 128], bct[:, g, :], bct[:, 2 + g, :])
        cb = cbr.next()
        P.cp(cb[:].rearrange("p a b -> p (a b)"), psc[:, 0:256], e="act")
        wt = wtr.next()
        for g in range(2):
            P.tt(wt[:, g * 4:(g + 1) * 4, :], seg[:, g * 4:(g + 1) * 4, :], bc(cb[:, g:g + 1, :], [128, 4, 128]), ALU.mult)
        if "SS3" in C.dbg:
            return t
        xdt = xdr.next()
        xs3 = xs[:].rearrange("p (a b) -> p a b", b=64)
        P.tt(xdt[:], xs3, bc(dt.unsqueeze(2), [128, 8, 64]), ALU.mult)
        sc = sm[:, 24:32]
        P.tt(sc, dt, seg[:, :, lend], ALU.mult)
        xdd = xer.next()
        P.tt(xdd[:], xs3, bc(sc.unsqueeze(2), [128, 8, 64]), ALU.mult)
        psY = P.ps()
        for h in range(8):
            P.mm(psY[:, h * 64:(h + 1) * 64], wt[:, h, :], xdt[:, h, :])
        yi = wk.next()
        P.cp(yi[:], psY[:], e="act")
        t.update(xs=xs, bm=bm, bct=bct, ecs=ecs, eend=eend, xdd=xdd, yi=yi)
        return t

    def state(d, ci, t):
        g0 = ci * 128
        psI = P.ps()
        for g in range(2):
            P.mm(psI[:, g * 256:(g + 1) * 256], t["bct"][:, 2 + g, :], Hb[:, g * 4:(g + 1) * 4, :].rearrange("p a b -> p (a b)"))
        psH = P.ps()
        for g in range(2):
            P.mm(psH[:, g * 256:(g + 1) * 256], t["bm"][:, g * 128:(g + 1) * 128], t["xdd"][:, g * 4:(g + 1) * 4, :].rearrange("p a b -> p (a b)"))
        P.tt(H[:], H[:], bc(t["eend"].unsqueeze(2), [128, 8, 64]), ALU.mult)
        P.tt(H[:], H[:], psH[:].rearrange("p (a b) -> p a b", b=64), ALU.add)
        P.cp(Hb[:], H[:], e="act")
        y = wk.next()
        P.cp(y[:], psI[:], e="act")
        return lambda: finish(d, g0, t, y)

    def finish(d, g0, t, y):
        y3 = y[:].rearrange("p (a b) -> p a b", b=64)
        P.tt(y3, y3, bc(t["ecs"].unsqueeze(2), [128, 8, 64]), ALU.mult)
        P.tt(y[:], y[:], t["yi"][:], ALU.add)
        if d == 1:
            P.dma("pool", S["YSD"].ap()[g0:g0 + 128, :], y[:])
            return
        yb, z = wk.next(), wk.next()
        P.dma("sp", yb[:], S["YSD"].ap()[g0:g0 + 128, :])
        P.dma("sp", z[:], S["ZTM"].ap()[g0:g0 + 128, :])
        P.tt(y[:], y[:], yb[:], ALU.add)
        if "YSDSUM" in C.dbg:
            P.dma("pool", S["YSD"].ap()[g0:g0 + 128, :], y[:])
        P.tt(yb[:].rearrange("p (a b) -> p a b", b=64), t["xs"][:].rearrange("p (a b) -> p a b", b=64),
             bc(Dv[:].unsqueeze(2), [128, 8, 64]), ALU.mult)
        P.tt(y[:], y[:], yb[:], ALU.add)
        P.act(z[:], z[:], AF.Silu)
        P.tt(y[:], y[:], z[:], ALU.mult)
        s = smr.next()
        P.act(junk[:], y[:], AF.Square, accum_out=s[:, 0:1])
        P.act(s[:, 1:2], s[:, 0:1], AF.Sqrt, scale=1.0 / 512, bias=1e-6)
        P.op("dve", "reciprocal", out=s[:, 2:3], in_=s[:, 1:2])
        o = ob.next()
        P.stt(o[:], y[:], s[:, 2:3], nw[:], ALU.mult, ALU.mult)
        P.dma("pool", S["MIX"].ap()[g0:g0 + 128, 512:1024], o[:])

    yield
    for d in (1, 0):
        if d == 0:
            P.barrier()
        order = list(range(NCH)) if d == 0 else (list(range(NCC - 1, -1, -1)) + list(range(NCH - 1, NCC - 1, -1)))
        P.op("dve", "memset", ap=H[:], constant=0.0)
        P.op("dve", "memset", ap=Hb[:], constant=0.0)
        nxt = prep(d, order[0])
        fin = None
        for i, ci in enumerate(order):
            cur = nxt
            if i + 1 < len(order):
                nxt = prep(d, order[i + 1])
            f2 = state(d, ci, cur)
            if fin is not None:
                fin()
            fin = f2
            yield
        fin()


def load_weight_bf16(C, dst, src_view, nk, ncols):
    P = C.P
    ring = P.pool_of("wst", [128, ncols], F32, 2)
    for k in range(nk):
        st = ring.next()
        P.dma("sp", st[:], src_view[:, k, :])
        P.cp(dst[:, k, :], st[:], e="pool")


def xsrc(C, first, g0):
    if first:
        return C.I["ctx"].ap()[g0:g0 + 128, :] if g0 < TC else C.I["x"].ap()[g0 - TC:g0 - TC + 128, :]
    return C.S["xcur"].ap()[g0:g0 + 128, :]


def phase_outproj(C, layer, mixcols, wname, first, lat_only):
    P, I, S, T, TT = C.P, C.I, C.S, C.T, C.TT
    nk = mixcols // 128
    with P.phase():
        w = P.sb("wout", [128, nk, D], BF16)
        load_weight_bf16(C, w, I[wname].ap()[0].rearrange("(kc p) n -> p kc n", p=128), nk, D)
        GA = {}
        for ty in range(2):
            GA[ty] = P.sb("GA", [128, D])
            load_mod(C, layer, ty, 2, GA[ty][:])
        G2, SH2 = {}, {}
        g2 = P.sb("ng2", [128, D])
        P.dma("sp", g2[:], I["norm2_g"].ap()[layer].partition_broadcast(128))
        for ty in range(2):
            G2[ty] = P.sb("G2", [128, D])
            SH2[ty] = P.sb("SH2", [128, D])
            load_mod(C, layer, ty, 4, G2[ty][:])
            load_mod(C, layer, ty, 3, SH2[ty][:])
            P.stt(G2[ty][:], G2[ty][:], 1.0, g2[:], ALU.add, ALU.mult)
        njunk = P.sb("onjunk", [128, D], BF16)
        xnr = P.pool_of("oxn", [128, D], F32, 2)
        hr = P.pool_of("onh", [128, D], BF16, 2)
        hTr = P.pool_of("onhT", [128, 8, 128], BF16, 3)
        nst = P.pool_of("onst", [128, 4], F32, 3)
        mr = P.pool_of("omix", [128, mixcols], BF16, 2)
        mtr = P.pool_of("omixT", [128, nk, 128], BF16, 2)
        xr = P.pool_of("ox", [128, D], F32, 3)
        tr_ = P.pool_of("otmp", [128, 512], F32, 3)
        def emit_hT(h, g0):
            ps = P.ps()
            psb = ps[:].bitcast(BF16)
            for kc in range(8):
                P.tr(psb[:, kc * 128:(kc + 1) * 128], h[:, kc * 128:(kc + 1) * 128], C.identb[:])
            hT = hTr.next()
            P.cp(hT[:].rearrange("p a b -> p (a b)"), psb, e="act")
            P.dma("pool", hT_view(C, g0, 128), hT[:])

        pend = None
        for i in range(TT // 128):
            g0 = i * 128
            ty = 1 if g0 < TC else 0
            if lat_only and ty == 1:
                continue
            m = mr.next()
            P.dma("sp", m[:], S["MIX"].ap()[g0:g0 + 128, 0:mixcols])
            x = xr.next()
            P.dma("sp", x[:], xsrc(C, first, g0))
            mT = mtr.next()
            for b in range(nk // 8):
                ps = P.ps()
                psb = ps[:].bitcast(BF16)
                for kc in range(8):
                    P.tr(psb[:, kc * 128:(kc + 1) * 128], m[:, (b * 8 + kc) * 128:(b * 8 + kc + 1) * 128], C.identb[:])
                P.cp(mT[:, b * 8:(b + 1) * 8, :].rearrange("p a b -> p (a b)"), psb, e="act")
            pss = []
            for hf in range(2):
                ps = P.ps()
                for kc in range(nk):
                    P.mm(ps[:], mT[:, kc, :], w[:, kc, hf * 512:(hf + 1) * 512], start=(kc == 0), stop=(kc == nk - 1))
                pss.append(ps)
            if pend is not None:
                emit_hT(*pend)
            for hf in range(2):
                ps = pss[hf]
                tmp = tr_.next()
                P.tt(tmp[:], ps[:], GA[ty][:, hf * 512:(hf + 1) * 512], ALU.mult)
                P.tt(x[:, hf * 512:(hf + 1) * 512], x[:, hf * 512:(hf + 1) * 512], tmp[:], ALU.add)
            P.dma("pool", S["xcur"].ap()[g0:g0 + 128, :], x[:])
            sn = nst.next()
            P.act(njunk[:], x[:], AF.Square, accum_out=sn[:, 0:1])
            P.act(sn[:, 1:2], sn[:, 0:1], AF.Sqrt, scale=1.0 / D, bias=1e-6)
            P.op("dve", "reciprocal", out=sn[:, 2:3], in_=sn[:, 1:2])
            xn = xnr.next()
            P.stt(xn[:], x[:], sn[:, 2:3], G2[ty][:], ALU.mult, ALU.mult)
            h = hr.next()
            P.tt(h[:], xn[:], SH2[ty][:], ALU.add)
            pend = (h, g0)
        emit_hT(*pend)


GELU_C = 1.5957691216057308


def phase_ffn_up(C, layer, lat_only):
    P, I, S, T, TT = C.P, C.I, C.S, C.T, C.TT
    R = T // GW
    RP = min(64, max(R // 2, 1))
    parts = [] if lat_only else [("ctx", 0, 0)]
    parts += [("lat", r0, min(r0 + RP, R)) for r0 in range(0, R, RP)]
    ML = (RP + 2) * GW
    cwn, cbn = f"ffn_cw{layer}", f"ffn_cb{layer}"
    with P.phase():
        hp_ = P.sb("uhT", [128, 8, ML], BF16)
        gr = P.pool_of("grow", [128, ML + 2], F32, 2)
        for g_ in gr.items:
            P.op("pool", "memset", ap=g_[:], constant=0.0)
        acc = P.sb("gacc", [128, ML + 2])
        gb = P.sb("ggelu", [128, ML], BF16)
        vr = P.pool_of("vrow", [128, ML], BF16, 2)
        wfr = P.pool_of("uwf", [128, 8, 256], F32, 2)
        wbr = P.pool_of("uwb", [128, 8, 256], BF16, 2)
        wv = I["ffn_w_up"].ap()[layer].rearrange("(kc p) n -> p kc n", p=128)

        def loadw(j):
            wf, wb = wfr.next(), wbr.next()
            P.dma("sp", wf[:, :, 0:128], wv[:, :, j * 128:(j + 1) * 128])
            P.dma("sp", wf[:, :, 128:256], wv[:, :, 2816 + j * 128:2816 + (j + 1) * 128])
            P.cp(wb[:], wf[:], e="pool")
            return wb

        for (kind, R0, R1) in parts:
            if kind == "ctx":
                gA, gB = 0, TC
                oA, oB = 0, TC
            else:
                Rb0, Rb1 = max(R0 - 1, 0), min(R1 + 1, R)
                gA, gB = TC + Rb0 * GW, TC + Rb1 * GW
                oA, oB = TC + R0 * GW, TC + R1 * GW
            NL = gB - gA
            g = gA
            while g < gB:
                tile_end = TC if g < TC else TC + ((g - TC) // 512 + 1) * 512
                n = min(gB, tile_end) - g
                P.dma("sp", hp_[:, :, g - gA:g - gA + n], hT_view(C, g, n))
                g += n
            o0, oL = oA - gA, oB - oA

            def head(j, wb, vrow, grow):
                for t0 in range(0, NL, 512):
                    n = min(512, NL - t0)
                    psA, psB = P.ps(), P.ps()
                    for kc in range(8):
                        P.mm(psA[:, 0:n], wb[:, kc, 0:128], hp_[:, kc, t0:t0 + n], start=(kc == 0), stop=(kc == 7))
                    for kc in range(8):
                        P.mm(psB[:, 0:n], wb[:, kc, 128:256], hp_[:, kc, t0:t0 + n], start=(kc == 0), stop=(kc == 7))
                    P.cp(grow[:, 1 + t0:1 + t0 + n], psA[:, 0:n], e="act")
                    P.cp(vrow[:, t0:t0 + n], psB[:, 0:n], e="act")

            def conv(j, grow):
                cw = lambda tap: pv_(C, cwn, tap * 22 + j)
                P.act(acc[:, 1 + o0:1 + o0 + oL], grow[:, 1 + o0:1 + o0 + oL], AF.Identity, scale=cw(4), bias=pv_(C, cbn, j))
                if kind == "ctx":
                    P.stt(acc[:, 1:1 + TC], grow[:, 0:TC], cw(3), acc[:, 1:1 + TC], ALU.mult, ALU.add)
                    P.stt(acc[:, 1:1 + TC], grow[:, 2:2 + TC], cw(5), acc[:, 1:1 + TC], ALU.mult, ALU.add)
                else:
                    nb = Rb1 - Rb0
                    gin = grow[:, 1:1 + nb * GW].rearrange("p (r w) -> p r w", w=GW)
                    gac = acc[:, 1:1 + nb * GW].rearrange("p (r w) -> p r w", w=GW)
                    for di in (-1, 0, 1):
                        for dj in (-1, 0, 1):
                            if di == 0 and dj == 0:
                                continue
                            rl, rh = max(R0, -di) - Rb0, min(R1, R - di) - Rb0
                            c0, c1 = max(0, -dj), GW - max(0, dj)
                            P.stt(gac[:, rl:rh, c0:c1], gin[:, rl + di:rh + di, c0 + dj:c1 + dj], cw((di + 1) * 3 + dj + 1),
                                  gac[:, rl:rh, c0:c1], ALU.mult, ALU.add)

            def tail(j, vrow):
                P.act(gb[:, o0:o0 + oL], acc[:, 1 + o0:1 + o0 + oL], AF.Gelu_apprx_tanh)
                P.tt(vrow[:, o0:o0 + oL], gb[:, o0:o0 + oL], vrow[:, o0:o0 + oL], ALU.mult)
                P.dma("pool", S["ACTT"].ap()[:, j, oA:oB], vrow[:, o0:o0 + oL])

            wnext = loadw(0)
            pend = None
            for j in range(22):
                wb = wnext
                if j + 1 < 22:
                    wnext = loadw(j + 1)
                vrow, grow = vr.next(), gr.next()
                head(j, wb, vrow, grow)
                if pend is not None:
                    tail(*pend)
                conv(j, grow)
                pend = (j, vrow)
            tail(*pend)


def phase_ffn_down(C, layer, lat_only, final):
    P, I, S, T, TT = C.P, C.I, C.S, C.T, C.TT
    tiles = [t for t in token_tiles(T) if not (lat_only and t[0] < TC)]
    with P.phase():
        w = P.sb("wdn", [128, 22, D], BF16)
        load_weight_bf16(C, w, I["ffn_w_down"].ap()[layer].rearrange("(kc p) n -> p kc n", p=128), 22, D)
        GA = {}
        for ty in range(2):
            GA[ty] = P.sb("GA2", [128, D])
            load_mod(C, layer, ty, 5, GA[ty][:])
        if final:
            gf = P.sb("gfin", [128, D])
            bcast_row(C, gf[:], I["final_norm_g"].ap())
            junk = P.sb("fjunk", [128, D], BF16)
            st = P.pool_of("fst", [128, 4], F32, 3)
        ar = P.pool_of("dact", [128, 22, 512], BF16, 2)
        xr = P.pool_of("dx", [128, D], F32, 3)
        tr_ = P.pool_of("dtmp", [128, 512], F32, 3)
        for (g0, n) in tiles:
            a = ar.next()
            P.dma("sp", a[:, :, 0:n], S["ACTT"].ap()[:, :, g0:g0 + n])
            for sub in range(n // 128):
                gs = g0 + sub * 128
                ty = 1 if gs < TC else 0
                x = xr.next()
                P.dma("sp", x[:], S["xcur"].ap()[gs:gs + 128, :])
                for hf in range(2):
                    ps = P.ps()
                    for cc in range(22):
                        P.mm(ps[:], a[:, cc, sub * 128:(sub + 1) * 128], w[:, cc, hf * 512:(hf + 1) * 512],
                             start=(cc == 0), stop=(cc == 21))
                    tmp = tr_.next()
                    P.tt(tmp[:], ps[:], GA[ty][:, hf * 512:(hf + 1) * 512], ALU.mult)
                    P.tt(x[:, hf * 512:(hf + 1) * 512], x[:, hf * 512:(hf + 1) * 512], tmp[:], ALU.add)
                if final:
                    s = st.next()
                    P.act(junk[:], x[:], AF.Square, accum_out=s[:, 0:1])
                    P.act(s[:, 1:2], s[:, 0:1], AF.Sqrt, scale=1.0 / D, bias=1e-6)
                    P.op("dve", "reciprocal", out=s[:, 2:3], in_=s[:, 1:2])
                    P.stt(x[:], x[:], s[:, 2:3], gf[:], ALU.mult, ALU.mult)
                    P.dma("pool", C.out.ap()[gs - TC:gs - TC + 128, :], x[:])
                else:
                    P.dma("pool", S["xcur"].ap()[gs:gs + 128, :], x[:])


def phase_ret_proj(C):
    P, I, S, T, TT = C.P, C.I, C.S, C.T, C.TT
    tiles = token_tiles(T)
    with P.phase():
        w = P.sb("wret", [128, 8, 6144], BF16)
        wv = I["ret_w_in"].ap()[0].rearrange("(kc p) n -> p kc n", p=128)
        ring = P.pool_of("wst", [128, 8, 256], F32, 2)
        for nb in range(24):
            st = ring.next()
            P.dma("sp", st[:], wv[:, :, nb * 256:(nb + 1) * 256])
            P.cp(w[:, :, nb * 256:(nb + 1) * 256], st[:], e="pool")
        htr = P.pool_of("rht", [128, 8, 512], BF16, 2)
        qfr = P.pool_of("rqf", [128, 1024], F32, 3)
        qbr = P.pool_of("rqb", [128, 1024], BF16, 4)
        tmr = P.pool_of("rtm", [128, 8, 64], F32, 4)
        rpr = P.pool_of("rrp", [128, 128], F32, 2)
        vbr = P.pool_of("rvb", [128, 2048], BF16, 3)
        str_ = P.pool_of("rst", [128, 8, 128], BF16, 4)
        for (g0, n) in tiles:
            ht = htr.next()
            P.dma("sp", ht[:, :, 0:n], hT_view(C, g0, n))
            for sub in range(n // 128):
                gs = g0 + sub * 128
                lat = gs >= TC
                if lat:
                    rp = rpr.next()
                    P.dma("sp", rp[:], I["rope"].ap()[gs - TC:gs - TC + 128, :])
                    cos = bc(rp[:, 0:64].unsqueeze(1), [128, 8, 64])
                    sin = bc(rp[:, 64:128].unsqueeze(1), [128, 8, 64])

                def proj(nb0, dst, scale=None, func=None):
                    for b in range(dst.shape[1] // 512):
                        ps = P.ps()
                        for kc in range(8):
                            P.mm(ps[:], ht[:, kc, sub * 128:(sub + 1) * 128], w[:, kc, (nb0 + b) * 512:(nb0 + b + 1) * 512],
                                 start=(kc == 0), stop=(kc == 7))
                        o = dst[:, b * 512:(b + 1) * 512]
                        if func is not None:
                            P.act(o, ps[:], func)
                        elif scale is not None:
                            P.act(o, ps[:], AF.Copy, scale=scale)
                        else:
                            P.cp(o, ps[:], e="act")
                for (nb0, scale, tname, tmname) in ((0, None, "QT", None), (2, 128 ** -0.5, "KT1", "KTM1")):
                    qb = qbr.next()
                    if lat:
                        qf = qfr.next()
                        proj(nb0, qf[:], scale=scale)
                        q3 = qf[:].rearrange("p (h d) -> p h d", d=128)
                        o3 = qb[:].rearrange("p (h d) -> p h d", d=128)
                        x1, x2 = q3[:, :, 0:64], q3[:, :, 64:128]
                        t1, t2 = tmr.next(), tmr.next()
                        P.tt(t1[:], x1, cos, ALU.mult)
                        P.tt(t2[:], x2, sin, ALU.mult)
                        P.tt(o3[:, :, 0:64], t1[:], t2[:], ALU.subtract)
                        t3, t4 = tmr.next(), tmr.next()
                        P.tt(t3[:], x2, cos, ALU.mult)
                        P.tt(t4[:], x1, sin, ALU.mult)
                        P.tt(o3[:, :, 64:128], t3[:], t4[:], ALU.add)
                    else:
                        proj(nb0, qb[:], scale=scale)
                    ps = P.ps()
                    psb = ps[:].bitcast(BF16)
                    for h in range(8):
                        P.tr(psb[:, h * 128:(h + 1) * 128], qb[:, h * 128:(h + 1) * 128], C.identb[:])
                    st = str_.next()
                    P.cp(st[:].rearrange("p a b -> p (a b)"), psb, e="act")
                    P.dma("pool", S[tname].ap()[:, :, gs:gs + 128], st[:])
                    if tmname:
                        P.dma("pool", S[tmname].ap()[gs:gs + 128, :], qb[:])
                vb = vbr.next()
                proj(4, vb[:])
                P.dma("pool", S["VTM1"].ap()[gs:gs + 128, :], vb[:])
                if lat:
                    gb = vbr.next()
                    proj(8, gb[:], func=AF.Silu)
                    P.dma("pool", S["GS1"].ap()[gs:gs + 128, :], gb[:])


def phase_ret_scan(C):
    P, I, S, T, TT = C.P, C.I, C.S, C.T, C.TT
    NCH = TT // 128
    NCC = TC // 128
    LN2 = 0.6931471805599453
    with P.phase():
        Dm, GQ, GL = {}, {}, {}
        for d, dn in enumerate("fb"):
            lg = P.sb("lg", [128, 8])
            P.dma("sp", lg[:], I["ret_log2_" + dn].ap()[0].partition_broadcast(128))
            P.act(lg[:], lg[:], AF.Exp, scale=-LN2)
            P.ts(lg[:], lg[:], -1.0, ALU.mult, 1.0, ALU.add)
            P.act(lg[:], lg[:], AF.Ln)
            Dm[d] = P.sb("Dm", [128, 8, 128])
            GQ[d] = P.sb("GQ", [128, 8, 128])
            GL[d] = P.sb("GL", [128, 8])
            for h in range(8):
                P.act(Dm[d][:, h, :], cs_(C, "relf" if d == 0 else "relb"), AF.Exp, scale=lg[:, h:h + 1])
                P.tt(Dm[d][:, h, :], Dm[d][:, h, :], cs_(C, "le" if d == 0 else "ge"), ALU.mult)
                P.act(GQ[d][:, h, :], cs_(C, "lp1" if d == 0 else "rl"), AF.Exp, scale=lg[:, h:h + 1])
            P.act(GL[d][:], lg[:], AF.Exp, scale=128.0)
        St = P.sb("rS", [128, 8, 256])
        Sb = P.sb("rSb", [128, 8, 256], BF16)
        qr = P.pool_of("sq", [128, 8, 128], BF16, 2)
        kr = P.pool_of("sk", [128, 8, 128], BF16, 2)
        kmr = P.pool_of("skm", [128, 1024], BF16, 2)
        vr = P.pool_of("sv", [128, 2048], BF16, 2)
        wtr = P.pool_of("swt", [128, 8, 128], BF16, 2)
        qdr = P.pool_of("sqd", [128, 8, 128], BF16, 2)
        ker = P.pool_of("ske", [128, 8, 128], BF16, 2)
        yr = P.pool_of("sy", [128, 2048], F32, 2)
        ybr = P.pool_of("syb", [128, 2048], F32, 2)
        gr = P.pool_of("sg", [128, 2048], BF16, 2)
        ob = P.pool_of("so", [128, 2048], BF16, 2)
        sm = P.pool_of("ssm", [128, 32], F32, 3)

        def prep(d, ci):
            g0 = ci * 128
            q, k, km, v = qr.next(), kr.next(), kmr.next(), vr.next()
            P.dma("sp", q[:], S["QT"].ap()[:, :, g0:g0 + 128])
            P.dma("sp", k[:], S["KT1"].ap()[:, :, g0:g0 + 128])
            P.dma("sp", km[:], S["KTM1"].ap()[g0:g0 + 128, :])
            P.dma("sp", v[:], S["VTM1"].ap()[g0:g0 + 128, :])
            lend = 127 if d == 0 else 0
            t = dict(v=v)
            ke = ker.next()
            P.tt(ke[:], km[:].rearrange("p (h d) -> p h d", d=128), bc(Dm[d][:, :, lend:lend + 1], [128, 8, 128]), ALU.mult, e="pool")
            t["ke"] = ke
            if ci >= NCC:
                wt = wtr.next()
                for hf in range(2):
                    ps = P.ps()
                    for i in range(4):
                        h = hf * 4 + i
                        P.mm(ps[:, i * 128:(i + 1) * 128], k[:, h, :], q[:, h, :])
                    P.tt(wt[:, hf * 4:(hf + 1) * 4, :].rearrange("p a b -> p (a b)"), ps[:],
                         Dm[d][:, hf * 4:(hf + 1) * 4, :].rearrange("p a b -> p (a b)"), ALU.mult)
                qd = qdr.next()
                P.tt(qd[:], q[:], GQ[d][:], ALU.mult, e="pool")
                t.update(wt=wt, qd=qd)
            return t

        def state(d, ci, t):
            g0 = ci * 128
            v = t["v"]
            if ci >= NCC:
                y = yr.next()
                for b in range(4):
                    ps = P.ps()
                    for j in range(2):
                        h = 2 * b + j
                        P.mm(ps[:, j * 256:(j + 1) * 256], t["wt"][:, h, :], v[:, h * 256:(h + 1) * 256], start=True, stop=False)
                        P.mm(ps[:, j * 256:(j + 1) * 256], t["qd"][:, h, :], Sb[:, h, :], start=False, stop=True)
                    P.cp(y[:, b * 512:(b + 1) * 512], ps[:], e="act")
            for b in range(4):
                ps = P.ps()
                for j in range(2):
                    h = 2 * b + j
                    P.mm(ps[:, j * 256:(j + 1) * 256], t["ke"][:, h, :], v[:, h * 256:(h + 1) * 256])
                for j in range(2):
                    h = 2 * b + j
                    P.stt(St[:, h, :], St[:, h, :], GL[d][:, h:h + 1], ps[:, j * 256:(j + 1) * 256], ALU.mult, ALU.add)
            P.cp(Sb[:], St[:], e="act")
            if ci < NCC:
                return
            if d == 1:
                P.dma("pool", S["YRET"].ap()[g0:g0 + 128, :], y[:])
                return
            yb, gs = ybr.next(), gr.next()
            P.dma("sp", yb[:], S["YRET"].ap()[g0:g0 + 128, :])
            P.dma("sp", gs[:], S["GS1"].ap()[g0:g0 + 128, :])
            P.tt(y[:], y[:], yb[:], ALU.add)
            if "YRETSUM" in C.dbg:
                P.dma("pool", S["YRET"].ap()[g0:g0 + 128, :], y[:])
            P.tt(yb[:], y[:], y[:], ALU.mult)
            s = sm.next()
            P.op("dve", "tensor_reduce", out=s[:, 0:8], in_=yb[:].rearrange("p (a b) -> p a b", b=256), axis=AX.X, op=ALU.add)
            P.act(s[:, 8:16], s[:, 0:8], AF.Sqrt, scale=1.0 / 256, bias=1e-6)
            P.op("dve", "reciprocal", out=s[:, 16:24], in_=s[:, 8:16])
            y3 = y[:].rearrange("p (a b) -> p a b", b=256)
            P.tt(y3, y3, bc(s[:, 16:24].unsqueeze(2), [128, 8, 256]), ALU.mult)
            o = ob.next()
            P.tt(o[:], y[:], gs[:], ALU.mult)
            P.dma("pool", S["MIX"].ap()[g0:g0 + 128, :], o[:])

        for d in (1, 0):
            if d == 0:
                P.barrier()
            order = list(range(NCH)) if d == 0 else (list(range(NCC - 1, -1, -1)) + list(range(NCH - 1, NCC - 1, -1)))
            P.op("dve", "memset", ap=St[:], constant=0.0)
            P.op("dve", "memset", ap=Sb[:], constant=0.0)
            nxt = prep(d, order[0])
            for i, ci in enumerate(order):
                cur = nxt
                if i + 1 < len(order):
                    nxt = prep(d, order[i + 1])
                state(d, ci, cur)


def phase_scans0(C):
    P = C.P
    with P.phase():
        gens = [gen_rwkv_scan(C), gen_ssd_scan(C)]
        while gens:
            for g in list(gens):
                try:
                    next(g)
                except StopIteration:
                    gens.remove(g)
```

```python
import contextlib
import numpy as np
import concourse.bass as bass
import concourse.mybir as mybir
from concourse.bass_utils import run_bass_kernel_spmd

F32 = mybir.dt.float32
BF16 = mybir.dt.bfloat16
AF = mybir.ActivationFunctionType
ALU = mybir.AluOpType
AX = mybir.AxisListType

D = 1024
TC = 256
GW = 64
NDMA = 40
SELF_SKIP = ("pe", "sp", "pool")
READ_ARGS = ("in_", "in0", "in1", "lhsT", "rhs", "scalar1", "scalar2", "scalar", "bias", "scale",
             "data0", "data1", "initial", "identity", "mask", "on_true", "on_false")
WRITE_ARGS = ("out", "accum_out", "ap")


class PB:
    ENG = ("pe", "act", "dve", "pool", "sp")

    def __init__(self):
        self.nc = nc = bass.Bass("TRN2", target_bir_lowering=False)
        self.es = contextlib.ExitStack()
        self.eng = dict(pe=nc.tensor, act=nc.scalar, dve=nc.vector, pool=nc.gpsimd, sp=nc.sync)
        self.sems = {}
        for e in self.ENG:
            self.sems[e] = self.es.enter_context(nc.semaphore("sem_" + e))
        for k in range(NDMA):
            self.sems[("d", k)] = self.es.enter_context(nc.semaphore(f"semd{k}"))
        self.cnt = {e: 0 for e in self.ENG}
        self.duse = [0] * NDMA
        self.dq = {"sp": list(range(0, 28)), "pool": list(range(28, 36)), "act": list(range(36, 40))}
        self.dqi = {"sp": 0, "pool": 0, "act": 0}
        self.waited = {e: {} for e in self.ENG}
        self.lastw = {}
        self.rds = {}
        self.uid = 0
        self.stack = [self.es]
        self.psb = [self.es.enter_context(nc.psum_tensor(f"psb{i}", [128, 512], F32)) for i in range(8)]
        self.psi = 0
        self.ninstr = 0

    def sb(self, name, shape, dt=F32):
        self.uid += 1
        return self.stack[-1].enter_context(self.nc.sbuf_tensor(f"{name}_{self.uid}", list(shape), dt))

    def pool_of(self, name, shape, dt, n):
        return Ring([self.sb(name, shape, dt) for _ in range(n)])

    @contextlib.contextmanager
    def phase(self):
        st = contextlib.ExitStack()
        self.stack.append(st)
        try:
            yield
        finally:
            self.barrier()
            self.stack.pop()
            st.close()

    def ps(self):
        t = self.psb[self.psi]
        self.psi = (self.psi + 1) % 8
        return t

    def _wait(self, e, tok):
        sk, v = tok
        if sk == e and e in SELF_SKIP:
            return
        if self.waited[e].get(sk, 0) >= v:
            return
        self.eng[e].wait_ge(self.sems[sk], v)
        self.waited[e][sk] = v
        self.ninstr += 1

    def _deps(self, e, rd, wr):
        for k in rd:
            t = self.lastw.get(k)
            if t:
                self._wait(e, t)
            if k.startswith("psb"):
                for sk, v in self.rds.get(k, {}).items():
                    self._wait(e, (sk, v))
        for k in wr:
            t = self.lastw.get(k)
            if t:
                self._wait(e, t)
            for sk, v in self.rds.get(k, {}).items():
                self._wait(e, (sk, v))

    def _mark(self, tok, rd, wr):
        sk, v = tok
        for k in rd:
            d = self.rds.setdefault(k, {})
            d[sk] = max(d.get(sk, 0), v)
        for k in wr:
            self.lastw[k] = tok
            self.rds[k] = {}

    @staticmethod
    def _keys(kw, names):
        out = []
        for k in names:
            v = kw.get(k)
            if isinstance(v, bass.AP) and str(v.space) != "DRAM":
                out.append(v.tensor.name)
        return out

    def op(self, e, name, rd=None, wr=None, xrd=(), xwr=(), **kw):
        if rd is None:
            rd = self._keys(kw, READ_ARGS)
        if wr is None:
            wr = self._keys(kw, WRITE_ARGS)
        rd = list(rd) + list(xrd)
        wr = list(wr) + list(xwr)
        self._deps(e, rd, wr)
        ins = getattr(self.eng[e], name)(**kw)
        self.cnt[e] += 1
        ins.then_inc(self.sems[e], 1)
        self._mark((e, self.cnt[e]), rd, wr)
        self.ninstr += 1
        return ins

    def dma(self, q, out, in_, rd=None, wr=None, **kw):
        if rd is None:
            rd = self._keys({"in_": in_}, ("in_",))
        if wr is None:
            wr = self._keys({"out": out}, ("out",))
        self._deps(q, rd, wr)
        k = self.dq[q][self.dqi[q]]
        self.dqi[q] = (self.dqi[q] + 1) % len(self.dq[q])
        if self.duse[k] > 0:
            self._wait(q, (("d", k), 16 * self.duse[k]))
        self.eng[q].dma_start(out=out, in_=in_, **kw).then_inc(self.sems[("d", k)], 16)
        self.duse[k] += 1
        self._mark((("d", k), 16 * self.duse[k]), rd, wr)
        self.ninstr += 1

    def barrier(self):
        toks = [(e, self.cnt[e]) for e in self.ENG if self.cnt[e] > 0]
        toks += [(("d", k), 16 * self.duse[k]) for k in range(NDMA) if self.duse[k] > 0]
        for e in self.ENG:
            for t in toks:
                self._wait(e, t)

    def mm(self, out, lhsT, rhs, start=True, stop=True, **kw):
        return self.op("pe", "matmul", out=out, lhsT=lhsT, rhs=rhs, start=start, stop=stop, **kw)

    def tr(self, out, in_, ident):
        return self.op("pe", "transpose", out=out, in_=in_, identity=ident)

    def act(self, out, in_, func, e="act", **kw):
        return self.op(e, "activation", out=out, in_=in_, func=func, **kw)

    def tt(self, out, in0, in1, op, e="dve"):
        return self.op(e, "tensor_tensor", out=out, in0=in0, in1=in1, op=op)

    def ts(self, out, in0, s1, op0, s2=None, op1=None, e="dve", **kw):
        if op1 is None:
            return self.op(e, "tensor_scalar", out=out, in0=in0, scalar1=s1, scalar2=None, op0=op0, **kw)
        return self.op(e, "tensor_scalar", out=out, in0=in0, scalar1=s1, scalar2=s2, op0=op0, op1=op1, **kw)

    def stt(self, out, in0, scalar, in1, op0, op1):
        return self.op("dve", "scalar_tensor_tensor", out=out, in0=in0, scalar=scalar, in1=in1, op0=op0, op1=op1)

    def cp(self, out, in_, e="dve"):
        if e == "act":
            return self.op("act", "activation", out=out, in_=in_, func=AF.Copy)
        return self.op(e, "tensor_copy", out=out, in_=in_)


class Ring:
    def __init__(self, items):
        self.items = items
        self.i = 0

    def next(self):
        t = self.items[self.i]
        self.i = (self.i + 1) % len(self.items)
        return t


def bc(ap, shape):
    return ap.broadcast_to(list(shape))


CI = dict(ident=0, ones=128, blk64=256, le=384, lt=512, ge=640, gt=768, negf=896, negb=1024, cmask=1152,
          relf=1664, relb=1792, lp1=1920, rl=2048)
NCONST = 2048 + 128


def make_consts():
    c = np.zeros((128, NCONST), np.float32)
    i = np.arange(128)
    s, t = i[:, None], i[None, :]
    c[:, 0:128] = np.eye(128)
    c[:, 128:256] = 1.0
    c[:, 256:384] = (s // 64 == t // 64)
    c[:, 384:512] = (s <= t)
    c[:, 512:640] = (s < t)
    c[:, 640:768] = (s >= t)
    c[:, 768:896] = (s > t)
    c[:, 896:1024] = np.where(s <= t, 0.0, -30000.0)
    c[:, 1024:1152] = np.where(s >= t, 0.0, -30000.0)
    cm = np.ones(512, np.float32)
    cm[::128] = 0.0
    c[:, 1152:1664] = cm[None, :]
    c[:, 1664:1792] = np.maximum(t - s, 0)
    c[:, 1792:1920] = np.maximum(s - t, 0)
    c[:, 1920:2048] = t + 1 + 0 * s
    c[:, 2048:2176] = 128 - t + 0 * s
    return c


def make_rope(T):
    n = 32
    pos = np.arange(T)
    row = (pos // GW).astype(np.float32)
    col = (pos % GW).astype(np.float32)
    inv = (np.float32(10000.0) ** (-np.arange(n, dtype=np.float32) / n)).astype(np.float32)
    ang = np.concatenate([row[:, None] * inv, col[:, None] * inv], -1).astype(np.float32)
    return np.concatenate([np.cos(ang), np.sin(ang)], -1).astype(np.float32)


class Ctx:
    pass


def token_tiles(T, w=512):
    tiles = [(g, min(w, TC - g)) for g in range(0, TC, w)]
    tiles += [(TC + g, min(w, T - g)) for g in range(0, T, w)]
    return tiles


def build(T, dbg=()):
    P = PB()
    nc = P.nc
    C = Ctx()
    C.P, C.T, C.TT = P, T, T + TC
    TT = C.TT
    C.dbg = dbg

    def din(name, shape, dt=F32):
        return nc.dram_tensor(name, list(shape), dt, kind="ExternalInput")

    def dscr(name, shape, dt=F32):
        kind = "ExternalOutput" if name in dbg else "Internal"
        return nc.dram_tensor(name, list(shape), dt, kind=kind)

    I = C.I = {}
    I["x"] = din("x", [T, D])
    I["ctx"] = din("ctx", [TC, D])
    I["cT"] = din("cT", [128, 16])
    I["consts"] = din("consts", [128, NCONST])
    I["rope"] = din("rope", [T, 128])
    I["pvec"] = din("pvec", [128, NPV])
    for nm, shp in WSHAPES.items():
        I[nm] = din(nm, shp)
    C.out = nc.dram_tensor("out", [T, D], F32, kind="ExternalOutput")
    S = C.S = {}
    S["modbc"] = dscr("modbc", [2, 2, 6 * D])
    S["xcur"] = dscr("xcur", [TT, D])
    S["hT"] = dscr("hT", [1 + (T + 511) // 512, 128, 8, 512], BF16)
    S["F0"] = dscr("F0", [28 * 128, TT])
    for d in "fb":
        for nm in ("AT", "BT", "KT", "RT"):
            S[nm + d] = dscr(nm + d, [128, 4, TT], BF16)
        S["BTM" + d] = dscr("BTM" + d, [TT, 512], BF16)
        S["KTM" + d] = dscr("KTM" + d, [TT, 512], BF16)
        S["LAMC" + d] = dscr("LAMC" + d, [128, 4, TT // 128])
    S["VTM"] = dscr("VTM", [TT, 512], BF16)
    S["BONUS"] = dscr("BONUS", [TT, 512])
    S["GATE"] = dscr("GATE", [TT, 512])
    S["YRK"] = dscr("YRK", [TT, 512])
    S["MIX"] = dscr("MIX", [TT, 2048], BF16)
    S["XSTM"] = dscr("XSTM", [TT, 512])
    S["ZTM"] = dscr("ZTM", [TT, 512])
    S["BMTM"] = dscr("BMTM", [TT, 256], BF16)
    S["DTTM"] = dscr("DTTM", [TT, 32])
    S["BCT"] = dscr("BCT", [128, 4, TT], BF16)
    S["YSD"] = dscr("YSD", [TT, 512])
    S["ACTT"] = dscr("ACTT", [128, 22, TT], BF16)
    S["QT"] = dscr("QT", [128, 8, TT], BF16)
    S["KT1"] = dscr("KT1", [128, 8, TT], BF16)
    S["KTM1"] = dscr("KTM1", [TT, 1024], BF16)
    S["VTM1"] = dscr("VTM1", [TT, 2048], BF16)
    S["GS1"] = dscr("GS1", [TT, 2048], BF16)
    S["YRET"] = dscr("YRET", [TT, 2048])

    C.cst = P.sb("cst", [128, NCONST])
    P.dma("sp", C.cst[:], I["consts"].ap())
    C.pv = P.sb("pvec", [128, NPV])
    P.dma("sp", C.pv[:], I["pvec"].ap())
    C.identb = P.sb("identb", [128, 128], BF16)
    P.cp(C.identb[:], C.cst[:, 0:128])
    C.cmaskb = None

    phase_mod(C)
    for layer in range(2):
        C.layer = layer
        phase_norm(C, layer, 0, first=(layer == 0))
        if layer == 0:
            phase_proj0(C)
            if "STOP_B0" in dbg:
                break
            if "SKIP_RWKV" not in dbg:
                phase_rwkv_feat(C)
            if "STOP_C0" in dbg:
                break
            phase_ssd_prep(C)
            phase_scans0(C)
            if "STOP_E0" in dbg:
                break
            phase_outproj(C, 0, 1024, "ev_w_out", True, False)
            if "STOP_F0" in dbg:
                break
            if "STOP_G0a" in dbg:
                break
            phase_ffn_up(C, 0, False)
            if "STOP_G0b" in dbg:
                break
            phase_ffn_down(C, 0, False, False)
            if "STOP_G0" in dbg:
                break
        else:
            phase_ret_proj(C)
            if "STOP_R1" in dbg:
                break
            phase_ret_scan(C)
            if "STOP_S1" in dbg:
                break
            phase_outproj(C, 1, 2048, "ret_w_out", False, True)
            if "STOP_F1" in dbg:
                break
            phase_ffn_up(C, 1, True)
            phase_ffn_down(C, 1, True, True)
    P.barrier()
    P.es.close()
    return nc


WSHAPES = dict(
    mod_w=[2, D, 6 * D], mod_b=[2, 6 * D], norm1_g=[2, D], norm2_g=[2, D],
    ffn_w_up=[2, D, 5632], ffn_conv_w=[2, 3, 3, 2816], ffn_conv_b=[2, 2816], ffn_w_down=[2, 2816, D],
    ev_w_in=[1, D, 3472], ev_mu_prev=[1, 1920], ev_mu_next=[1, 1920],
    rk_w0_f=[1, 512], rk_w0_b=[1, 512], rk_w2_f=[1, 64, 512], rk_w2_b=[1, 64, 512],
    rk_a0_f=[1, 512], rk_a0_b=[1, 512], rk_a2_f=[1, 64, 512], rk_a2_b=[1, 64, 512],
    rk_g2=[1, 128, 512], rk_k_k=[1, 512], rk_k_a=[1, 512], rk_r_k=[1, 8, 64],
    rk_ln_w=[1, 512], rk_ln_b=[1, 512], ssd_conv_w=[1, 3, 1024], ssd_conv_b=[1, 1024],
    ssd_dt_bias_f=[1, 8], ssd_dt_bias_b=[1, 8], ssd_a_log_f=[1, 8], ssd_a_log_b=[1, 8],
    ssd_d=[1, 8], ssd_norm_w=[1, 512], ev_w_out=[1, D, D], ret_w_in=[1, D, 6144],
    ret_log2_f=[1, 8], ret_log2_b=[1, 8], ret_w_out=[1, 2048, D], final_norm_g=[D],
)


def hT_view(C, g0, n):
    if g0 < TC:
        ti, off = 0, g0
    else:
        ti, off = 1 + (g0 - TC) // 512, (g0 - TC) % 512
    return C.S["hT"].ap()[ti][:, :, off:off + n]


def cs_(C, name):
    o = CI[name]
    return C.cst[:, o:o + 128]


def phase_mod(C):
    P, I, S = C.P, C.I, C.S
    with P.phase():
        cT = P.sb("cT", [128, 16])
        P.dma("sp", cT[:], I["cT"].ap())
        cs = P.sb("csil", [128, 16])
        P.act(cs[:], cT[:], AF.Silu)
        L = P.sb("Lbc", [128, 8, 128])
        for j in range(8):
            P.ts(L[:, j, 0:64], cs_(C, "ones")[:, 0:64], cs[:, j:j + 1], ALU.mult)
            P.ts(L[:, j, 64:128], cs_(C, "ones")[:, 0:64], cs[:, 8 + j:9 + j], ALU.mult)
        wring = P.pool_of("modw", [128, 8, 512], F32, 2)
        bring = P.pool_of("modb", [128, 512], F32, 2)
        oring = P.pool_of("modo", [128, 512], F32, 2)
        for l in range(2):
            wv = I["mod_w"].ap()[l].rearrange("(kc p) n -> p kc n", p=128)
            for n in range(12):
                w = wring.next()
                P.dma("sp", w[:], wv[:, :, n * 512:(n + 1) * 512])
                b = bring.next()
                P.dma("sp", b[:], I["mod_b"].ap()[l, n * 512:(n + 1) * 512].partition_broadcast(128))
                ps = P.ps()
                for kc in range(8):
                    P.mm(ps[:], L[:, kc, :], w[:, kc, :], start=(kc == 0), stop=(kc == 7))
                o = oring.next()
                P.tt(o[:], ps[:], b[:], ALU.add)
                for ty in range(2):
                    P.dma("pool", S["modbc"].ap()[l, ty:ty + 1, n * 512:(n + 1) * 512], o[64 * ty:64 * ty + 1, :])


def load_mod(C, layer, ty, idx, out):
    C.P.dma("sp", out, C.S["modbc"].ap()[layer, ty, idx * D:(idx + 1) * D].partition_broadcast(128))


def phase_norm(C, layer, which, first=False, lat_only=False):
    P, I, S, T, TT = C.P, C.I, C.S, C.T, C.TT
    gname = "norm1_g" if which == 0 else "norm2_g"
    with P.phase():
        G, SH = {}, {}
        g = P.sb("ng", [128, D])
        P.dma("sp", g[:], I[gname].ap()[layer].partition_broadcast(128))
        for ty in range(2):
            G[ty] = P.sb("G", [128, D])
            SH[ty] = P.sb("SH", [128, D])
            load_mod(C, layer, ty, 3 * which + 1, G[ty][:])
            load_mod(C, layer, ty, 3 * which + 0, SH[ty][:])
            P.stt(G[ty][:], G[ty][:], 1.0, g[:], ALU.add, ALU.mult)
        xr = P.pool_of("nx", [128, D], F32, 3)
        junk = P.sb("njunk", [128, D], BF16)
        hr = P.pool_of("nh", [128, D], BF16, 2)
        hTr = P.pool_of("nhT", [128, 8, 128], BF16, 3)
        st = P.pool_of("nst", [128, 4], F32, 3)
        for i in range(TT // 128):
            g0 = i * 128
            ty = 1 if g0 < TC else 0
            if lat_only and ty == 1:
                continue
            if first:
                src = I["ctx"].ap()[g0:g0 + 128, :] if ty == 1 else I["x"].ap()[g0 - TC:g0 - TC + 128, :]
            else:
                src = S["xcur"].ap()[g0:g0 + 128, :]
            x = xr.next()
            P.dma("sp", x[:], src)
            s = st.next()
            P.act(junk[:], x[:], AF.Square, accum_out=s[:, 0:1])
            P.act(s[:, 1:2], s[:, 0:1], AF.Sqrt, scale=1.0 / D, bias=1e-6)
            P.op("dve", "reciprocal", out=s[:, 2:3], in_=s[:, 1:2])
            P.stt(x[:], x[:], s[:, 2:3], G[ty][:], ALU.mult, ALU.mult)
            h = hr.next()
            P.tt(h[:], x[:], SH[ty][:], ALU.add)
            ps = P.ps()
            psb = ps[:].bitcast(BF16)
            for kc in range(8):
                P.tr(psb[:, kc * 128:(kc + 1) * 128], h[:, kc * 128:(kc + 1) * 128], C.identb[:])
            hT = hTr.next()
            P.cp(hT[:].rearrange("p a b -> p (a b)"), psb, e="act")
            P.dma("pool", hT_view(C, g0, 128), hT[:])


PV = {}
_o = 0
for _n, _w in (("mu_prev", 15), ("mu_next", 15), ("conv_w", 24), ("conv_b", 8), ("k_k", 4), ("k_a", 4), ("r_k", 4),
               ("w0_f", 4), ("w0_b", 4), ("a0_f", 4), ("a0_b", 4), ("dt_bias", 1), ("a_log", 1), ("ssd_d", 1),
               ("ffn_cw0", 198), ("ffn_cb0", 22), ("ffn_cw1", 198), ("ffn_cb1", 22)):
    PV[_n] = (_o, _w)
    _o += _w
NPV = _o


def fm(v):
    v = np.asarray(v, np.float32).reshape(-1)
    return np.ascontiguousarray(v.reshape(-1, 128).T)


def make_pvec(inp):
    pv = np.zeros((128, NPV), np.float32)

    def put(name, arr):
        o, w = PV[name]
        assert arr.shape[1] == w, (name, arr.shape, w)
        pv[:arr.shape[0], o:o + w] = arr
    put("mu_prev", fm(inp["ev_mu_prev"][0]))
    put("mu_next", fm(inp["ev_mu_next"][0]))
    put("conv_w", np.concatenate([fm(inp["ssd_conv_w"][0, j]) for j in range(3)], 1))
    put("conv_b", fm(inp["ssd_conv_b"][0]))
    for n, k in (("k_k", "rk_k_k"), ("k_a", "rk_k_a"), ("r_k", "rk_r_k"), ("w0_f", "rk_w0_f"), ("w0_b", "rk_w0_b"),
                 ("a0_f", "rk_a0_f"), ("a0_b", "rk_a0_b")):
        put(n, fm(inp[k][0]))
    put("dt_bias", np.concatenate([inp["ssd_dt_bias_f"][0], inp["ssd_dt_bias_b"][0]])[:, None].astype(np.float32))
    put("a_log", np.concatenate([inp["ssd_a_log_f"][0], inp["ssd_a_log_b"][0]])[:, None].astype(np.float32))
    put("ssd_d", np.concatenate([inp["ssd_d"][0], inp["ssd_d"][0]])[:, None].astype(np.float32))
    for l in range(2):
        cw = inp["ffn_conv_w"][l].reshape(9, 2816)
        put(f"ffn_cw{l}", np.concatenate([fm(cw[j]) for j in range(9)], 1))
        put(f"ffn_cb{l}", fm(inp["ffn_conv_b"][l]))
    return pv


def pv_(C, name, j=0, n=1):
    o, w = PV[name]
    return C.pv[:, o + j:o + j + n]


def phase_proj0(C):
    P, I, S, T, TT = C.P, C.I, C.S, C.T, C.TT
    PL = min(4096, max(T // 2, 128))
    parts = [(0, TC, 0, TC)]
    for a in range(0, T, PL):
        oA, oB = TC + a, TC + min(a + PL, T)
        parts.append((max(oA - 1, TC), min(oB + 1, TC + T), oA, oB))
    ML = PL + 2
    with P.phase():
        c0 = P.sb("c0", [128, 15])
        P.tt(c0[:], pv_(C, "mu_prev", 0, 15), pv_(C, "mu_next", 0, 15), ALU.add)
        P.ts(c0[:], c0[:], -1.0, ALU.mult, 1.0, ALU.add)
        hp_ = P.sb("phT", [128, 8, ML], BF16)
        ur = P.pool_of("urow", [128, ML + 2], F32, 2)
        t1r = P.pool_of("t1row", [128, ML + 2], F32, 2)
        wfr = P.pool_of("wf", [128, 8, 128], F32, 2)
        wbr = P.pool_of("wb", [128, 8, 128], BF16, 2)
        wv = I["ev_w_in"].ap()[0].rearrange("(kc p) n -> p kc n", p=128)

        def loadw(j):
            ncol = 128 if j < 27 else 16
            wf, wb = wfr.next(), wbr.next()
            P.dma("sp", wf[:, :, 0:ncol], wv[:, :, j * 128:j * 128 + ncol])
            P.cp(wb[:, :, 0:ncol], wf[:, :, 0:ncol], e="pool")
            return wb

        for (gA, gB, oA, oB) in parts:
            NL, oL, o0 = gB - gA, oB - oA, oA - gA
            g = gA
            while g < gB:
                tile_end = TC if g < TC else TC + ((g - TC) // 512 + 1) * 512
                n = min(gB, tile_end) - g
                if n == 1:
                    P.dma("sp", hp_[:, :, g - gA:g - gA + n], hT_view(C, g, n), allow_slow_non_contiguous=True)
                else:
                    P.dma("sp", hp_[:, :, g - gA:g - gA + n], hT_view(C, g, n))
                g += n
            for u in ur.items:
                P.op("dve", "memset", ap=u[:, 0:1], constant=0.0)
                P.op("dve", "memset", ap=u[:, 1 + NL:2 + NL], constant=0.0)
            wnext = loadw(0)
            for j in range(28):
                ncol = 128 if j < 27 else 16
                wb = wnext
                if j + 1 < 28:
                    wnext = loadw(j + 1)
                u = ur.next()
                for t0 in range(0, NL, 512):
                    n = min(512, NL - t0)
                    ps = P.ps()
                    for kc in range(8):
                        P.mm(ps[0:ncol, 0:n], wb[:, kc, 0:ncol], hp_[:, kc, t0:t0 + n], start=(kc == 0), stop=(kc == 7))
                    P.cp(u[0:ncol, 1 + t0:1 + t0 + n], ps[0:ncol, 0:n], e="act")
                s0 = 1 + o0
                dst = S["F0"].ap()[j * 128:j * 128 + ncol, oA:oB]
                if j < 15:
                    t1 = t1r.next()
                    P.act(t1[:, s0:s0 + oL], u[:, s0:s0 + oL], AF.Identity, scale=c0[:, j:j + 1])
                    P.stt(t1[:, s0:s0 + oL], u[:, s0 - 1:s0 - 1 + oL], pv_(C, "mu_prev", j), t1[:, s0:s0 + oL], ALU.mult, ALU.add)
                    P.stt(t1[:, s0:s0 + oL], u[:, s0 + 1:s0 + 1 + oL], pv_(C, "mu_next", j), t1[:, s0:s0 + oL], ALU.mult, ALU.add)
                    P.dma("pool", dst, t1[:, s0:s0 + oL])
                elif 19 <= j < 27:
                    c = j - 19
                    t1 = t1r.next()
                    P.act(t1[:, s0:s0 + oL], u[:, s0:s0 + oL], AF.Identity, scale=pv_(C, "conv_w", 8 + c), bias=pv_(C, "conv_b", c))
                    P.stt(t1[:, s0:s0 + oL], u[:, s0 - 1:s0 - 1 + oL], pv_(C, "conv_w", c), t1[:, s0:s0 + oL], ALU.mult, ALU.add)
                    P.stt(t1[:, s0:s0 + oL], u[:, s0 + 1:s0 + 1 + oL], pv_(C, "conv_w", 16 + c), t1[:, s0:s0 + oL], ALU.mult, ALU.add)
                    P.act(t1[:, s0:s0 + oL], t1[:, s0:s0 + oL], AF.Silu)
                    P.dma("pool", dst, t1[:, s0:s0 + oL])
                else:
                    P.dma("pool", dst, u[0:ncol, s0:s0 + oL])


def make_in_maps(inp, T, ncores):
    consts = make_consts()
    rope = make_rope(T)
    pvec = make_pvec(inp)
    shared = {k: np.ascontiguousarray(np.asarray(inp[k], np.float32)) for k in WSHAPES}
    maps = []
    for b in range(ncores):
        m = dict(shared)
        m["x"] = np.ascontiguousarray(inp["x"][b], dtype=np.float32)
        m["ctx"] = np.ascontiguousarray(inp["ctx"][b], dtype=np.float32)
        m["cT"] = np.ascontiguousarray(np.concatenate([fm(inp["c"][b]), fm(inp["c_ctx"])], 1))
        m["consts"] = consts
        m["rope"] = rope
        m["pvec"] = pvec
        maps.append(m)
    return maps


def kernel(**inputs):
    B, T, _ = inputs["x"].shape
    nc = build(T)
    maps = make_in_maps(inputs, T, B)
    res = run_bass_kernel_spmd(nc, maps, core_ids=list(range(B)))
    return np.stack([np.asarray(r["out"], np.float32) for r in res.results], 0)


LDK = 0.6065306597126334


def f0rows(C, r0, nchunk, g0, n):
    return C.S["F0"].ap()[r0:r0 + 128 * nchunk, g0:g0 + n].rearrange("(c p) t -> p c t", p=128)


def phase_rwkv_feat(C):
    P, I, S, T, TT = C.P, C.I, C.S, C.T, C.TT
    W = 256
    tiles = token_tiles(T, W)
    identf = cs_(C, "ident")
    blk = cs_(C, "blk64")
    with P.phase():
        wtmp = P.sb("lw32", [128, 3, 512])
        P.dma("sp", wtmp[0:64, 0, :], I["rk_w2_f"].ap()[0])
        P.dma("sp", wtmp[64:128, 0, :], I["rk_w2_b"].ap()[0])
        P.dma("sp", wtmp[0:64, 1, :], I["rk_a2_f"].ap()[0])
        P.dma("sp", wtmp[64:128, 1, :], I["rk_a2_b"].ap()[0])
        P.dma("sp", wtmp[:, 2, :], I["rk_g2"].ap()[0])
        wl = P.sb("lwb", [128, 3, 512], BF16)
        P.cp(wl[:], wtmp[:])
        omka = P.sb("omka", [128, 4])
        P.ts(omka[:], pv_(C, "k_a", 0, 4), -1.0, ALU.mult, 1.0, ALU.add)
        cmask = C.cst[:, CI["cmask"]:CI["cmask"] + W]
        Rr = P.pool_of("fR", [128, 4, W], F32, 2)
        Kr = P.pool_of("fK", [128, 4, W], F32, 2)
        Vr = P.pool_of("fV", [128, 4, W], F32, 2)
        KKr = P.pool_of("fKK", [128, 4, W], F32, 2)
        WAr = P.pool_of("fWA", [128, 2, W], F32, 2)
        GDr = P.pool_of("fGD", [128, W], F32, 2)
        tmp = P.pool_of("ftmp", [128, 4, W], F32, 12)
        t1r = P.pool_of("ft1", [128, W], F32, 4)
        b16 = P.pool_of("fb16", [128, 4, W], BF16, 10)
        s16 = P.pool_of("fs16", [128, W], BF16, 6)
        stf = P.pool_of("fstf", [128, 512], F32, 4)
        stb = P.pool_of("fstb", [128, 512], BF16, 6)
        lamr = P.pool_of("flam", [128, 4, W // 128], F32, 4)
        for (g0, n) in tiles:
            nch = n // 128
            R, K, V, KK, WA, GD = Rr.next(), Kr.next(), Vr.next(), KKr.next(), WAr.next(), GDr.next()
            P.dma("sp", R[:, :, 0:n], f0rows(C, 0, 4, g0, n))
            P.dma("sp", K[:, :, 0:n], f0rows(C, 512, 4, g0, n))
            P.dma("sp", V[:, :, 0:n], f0rows(C, 1024, 4, g0, n))
            P.dma("sp", WA[:, :, 0:n], f0rows(C, 1536, 2, g0, n))
            P.dma("sp", GD[:, 0:n], S["F0"].ap()[1792:1920, g0:g0 + n])
            thw, adb, sgd = s16.next(), s16.next(), s16.next()
            P.act(thw[:, 0:n], WA[:, 0, 0:n], AF.Tanh)
            P.cp(adb[:, 0:n], WA[:, 1, 0:n], e="act")
            P.act(sgd[:, 0:n], GD[:, 0:n], AF.Sigmoid)
            kq, bon, gt, sq4, rk4 = tmp.next(), tmp.next(), tmp.next(), tmp.next(), tmp.next()
            for c in range(4):
                P.ts(kq[:, c, 0:n], K[:, c, 0:n], pv_(C, "k_k", c), ALU.mult)
            P.act(sq4[:, :, 0:n], kq[:, :, 0:n], AF.Square)
            pss = []
            for c in range(4):
                ps = P.ps()
                P.mm(ps[:, 0:n], blk, sq4[:, c, 0:n])
                pss.append(ps)
            for c in range(4):
                P.stt(rk4[:, c, 0:n], R[:, c, 0:n], pv_(C, "r_k", c), K[:, c, 0:n], ALU.mult, ALU.mult)
            for c in range(4):
                P.act(sq4[:, c, 0:n], pss[c][:, 0:n], AF.Sqrt)
            pss = []
            for c in range(4):
                ps = P.ps()
                P.mm(ps[:, 0:n], blk, rk4[:, c, 0:n])
                pss.append(ps)
            P.ts(sq4[:, :, 0:n], sq4[:, :, 0:n], 1e-12, ALU.max)
            P.op("dve", "reciprocal", out=sq4[:, :, 0:n], in_=sq4[:, :, 0:n])
            P.tt(KK[:, :, 0:n], kq[:, :, 0:n], sq4[:, :, 0:n], ALU.mult)
            for c in range(4):
                P.tt(bon[:, c, 0:n], pss[c][:, 0:n], V[:, c, 0:n], ALU.mult)
            pss = []
            for c in range(4):
                ps = P.ps()
                P.mm(ps[:, 0:n], wl[:, 2, c * 128:(c + 1) * 128], sgd[:, 0:n])
                pss.append(ps)
            for c in range(4):
                P.cp(gt[:, c, 0:n], pss[c][:, 0:n], e="act")
            vb = b16.next()
            P.cp(vb[:, :, 0:n], V[:, :, 0:n], e="act")
            for k in range(nch):
                gk = g0 + k * 128
                for (src, dst) in ((bon, S["BONUS"]), (gt, S["GATE"])):
                    ps = P.ps()
                    for c in range(4):
                        P.tr(ps[:, c * 128:(c + 1) * 128], src[:, c, k * 128:(k + 1) * 128], identf)
                    st = stf.next()
                    P.cp(st[:], ps[:], e="act")
                    P.dma("pool", dst.ap()[gk:gk + 128, :], st[:])
                ps = P.ps()
                psb = ps[:].bitcast(BF16)
                for c in range(4):
                    P.tr(psb[:, c * 128:(c + 1) * 128], vb[:, c, k * 128:(k + 1) * 128], C.identb[:])
                st = stb.next()
                P.cp(st[:], psb[:, 0:512], e="act")
                P.dma("pool", S["VTM"].ap()[gk:gk + 128, :], st[:])
            for d, dn in enumerate("fb"):
                dp = slice(0, 64) if d == 0 else slice(64, 128)
                sig, icl = tmp.next(), tmp.next()
                for (wi, src, dst, bn) in ((0, thw, sig, "w0_"), (1, adb, icl, "a0_")):
                    pss = []
                    for c in range(4):
                        ps = P.ps()
                        P.mm(ps[:, 0:n], wl[dp, wi, c * 128:(c + 1) * 128], src[dp, 0:n])
                        pss.append(ps)
                    for c in range(4):
                        P.act(dst[:, c, 0:n], pss[c][:, 0:n], AF.Sigmoid, bias=pv_(C, bn + dn, c))
                cs = tmp.next()
                for c in range(4):
                    P.op("dve", "tensor_tensor_scan", out=cs[:, c, 0:n], data0=cmask[:, 0:n], data1=sig[:, c, 0:n],
                         initial=0.0, op0=ALU.mult, op1=ALU.add)
                v4 = lambda t: t[:, :, 0:n].rearrange("p c (k t) -> p c k t", t=128)
                tot = v4(cs)[:, :, :, 127:128]
                lam = lamr.next()
                P.act(lam[:, :, 0:nch], v4(cs)[:, :, :, 127], AF.Exp, scale=-LDK)
                P.dma("pool", S["LAMC" + dn].ap()[:, :, g0 // 128:g0 // 128 + nch], lam[:, :, 0:nch])
                if d == 1:
                    cs2 = tmp.next()
                    P.tt(cs2[:, :, 0:n], sig[:, :, 0:n], cs[:, :, 0:n], ALU.subtract)
                    P.tt(v4(cs2), v4(cs2), bc(tot, [128, 4, nch, 128]), ALU.add)
                    cs = cs2
                ex = tmp.next()
                P.tt(ex[:, :, 0:n], cs[:, :, 0:n], sig[:, :, 0:n], ALU.subtract)
                P.act(ex[:, :, 0:n], ex[:, :, 0:n], AF.Exp, scale=-LDK)
                ef, ei = tmp.next(), tmp.next()
                P.act(ef[:, :, 0:n], cs[:, :, 0:n], AF.Exp, scale=-LDK)
                P.act(ei[:, :, 0:n], cs[:, :, 0:n], AF.Exp, scale=LDK)
                kd = tmp.next()
                for c in range(4):
                    P.ts(kd[:, c, 0:n], icl[:, c, 0:n], pv_(C, "k_a", c), ALU.mult, omka[:, c:c + 1], ALU.add)
                P.tt(kd[:, :, 0:n], kd[:, :, 0:n], K[:, :, 0:n], ALU.mult)
                at, bt, kt, rt = b16.next(), b16.next(), b16.next(), b16.next()
                P.stt(at[:, :, 0:n], KK[:, :, 0:n], -1.0, ex[:, :, 0:n], ALU.mult, ALU.mult)
                P.tt(icl[:, :, 0:n], icl[:, :, 0:n], KK[:, :, 0:n], ALU.mult)
                P.tt(bt[:, :, 0:n], icl[:, :, 0:n], ei[:, :, 0:n], ALU.mult)
                P.tt(kt[:, :, 0:n], kd[:, :, 0:n], ei[:, :, 0:n], ALU.mult)
                P.tt(rt[:, :, 0:n], R[:, :, 0:n], ef[:, :, 0:n], ALU.mult)
                for (src, nm) in ((at, "AT"), (bt, "BT"), (kt, "KT"), (rt, "RT")):
                    P.dma("pool", S[nm + dn].ap()[:, :, g0:g0 + n], src[:, :, 0:n])
                for k in range(nch):
                    gk = g0 + k * 128
                    for (src, nm) in ((bt, "BTM"), (kt, "KTM")):
                        ps = P.ps()
                        psb = ps[:].bitcast(BF16)
                        for c in range(4):
                            P.tr(psb[:, c * 128:(c + 1) * 128], src[:, c, k * 128:(k + 1) * 128], C.identb[:])
                        st = stb.next()
                        P.cp(st[:], psb[:, 0:512], e="act")
                        P.dma("pool", S[nm + dn].ap()[gk:gk + 128, :], st[:])


def bcast_row(C, out, vec_ap):
    C.P.dma("sp", out, vec_ap.partition_broadcast(128))


def gen_rwkv_scan(C):
    P, I, S, T, TT = C.P, C.I, C.S, C.T, C.TT
    NCH = TT // 128
    NCC = TC // 128
    hp = lambda h: slice(0, 64) if h % 2 == 0 else slice(64, 128)
    mk = {}
    for nm in ("le", "lt", "ge", "gt"):
        mk[nm] = P.sb("mk" + nm, [128, 4, 128])
        for i in range(4):
            P.cp(mk[nm][:, i, :], cs_(C, nm))
    id4 = P.sb("id4", [128, 4, 128], F32)
    for i in range(4):
        P.cp(id4[:, i, :], cs_(C, "ident"))
    lnw = P.sb("lnw", [128, 512])
    lnb = P.sb("lnb", [128, 512])
    bcast_row(C, lnw[:], I["rk_ln_w"].ap()[0])
    bcast_row(C, lnb[:], I["rk_ln_b"].ap()[0])
    ST = P.sb("ST", [128, 4, 64])
    STz = P.sb("STz", [128, 8, 64], BF16)
    lam = P.sb("lamall", [128, 4, NCH])
    fr = {nm: P.pool_of("s" + nm, [128, 4, 128], BF16, 2) for nm in ("at", "bt", "kt", "rt")}
    tr_ = {nm: P.pool_of("s" + nm, [128, 512], BF16, 2) for nm in ("btm", "ktm", "v")}
    Pr = P.pool_of("sP", [128, 4, 128], F32, 4)
    PTr = P.pool_of("sPT", [128, 4, 128], F32, 4)
    TTr = P.pool_of("sTT", [128, 4, 128], F32, 4)
    TTfin = P.pool_of("sTTf", [128, 4, 128], F32, 4)
    Mr = {nm: P.pool_of("s" + nm, [128, 4, 128], BF16, 4) for nm in ("lak", "nrb", "nrk")}
    zr = P.pool_of("sz", [128, 512], F32, 2)
    ur = P.pool_of("su", [128, 512], BF16, 2)
    yr = P.pool_of("sy", [128, 512], F32, 5)
    wk = P.pool_of("swk", [128, 512], F32, 5)
    sm = P.pool_of("ssm", [128, 32], F32, 3)
    ob = P.pool_of("sob", [128, 512], BF16, 2)

    def prep(d, ci):
        dn = "fb"[d]
        g0 = ci * 128
        t = {}
        for nm, key in (("at", "AT"), ("bt", "BT"), ("kt", "KT"), ("rt", "RT")):
            t[nm] = fr[nm].next()
            P.dma("sp", t[nm][:], S[key + dn].ap()[:, :, g0:g0 + 128])
        for nm, key in (("btm", "BTM" + dn), ("ktm", "KTM" + dn), ("v", "VTM")):
            t[nm] = tr_[nm].next()
            P.dma("sp", t[nm][:], S[key].ap()[g0:g0 + 128, :])
        m_l, m_lT, m_n = (("gt", "lt", "le") if d == 0 else ("lt", "gt", "ge"))
        t["TT"], t["lak"], t["nrb"], t["nrk"] = [], [], [], []

        def prod(half, lhs, rhs, mask, ring):
            ps = P.ps()
            for i in range(4):
                P.mm(ps[:, i * 128:(i + 1) * 128], t[lhs][hp(half), i, :], t[rhs][hp(half), i, :])
            o = ring.next()
            P.tt(o[:].rearrange("p a b -> p (a b)"), ps[:], mk[mask][:].rearrange("p a b -> p (a b)"), ALU.mult)
            return o
        Pm, PTm, TTm = [None, None], [None, None], [None, None]
        for half in range(2):
            Pm[half] = prod(half, "at", "bt", m_l, Pr)
            PTm[half] = prod(half, "bt", "at", m_lT, PTr)
            TTm[half] = TTr.next()
            P.tt(TTm[half][:], PTm[half][:], id4[:], ALU.add)
            t["lak"].append(prod(half, "kt", "at", m_lT, Mr["lak"]))
            t["nrb"].append(prod(half, "bt", "rt", m_n, Mr["nrb"]))
            t["nrk"].append(prod(half, "kt", "rt", m_n, Mr["nrk"]))
        for lvl in range(1, 7):
            Pn = [None, None]
            for half in range(2):
                ps = P.ps()
                for i in range(4):
                    P.mm(ps[:, i * 128:(i + 1) * 128], PTm[half][:, i, :], Pm[half][:, i, :])
                Pn[half] = Pr.next()
                P.cp(Pn[half][:].rearrange("p a b -> p (a b)"), ps[:], e="act")
            for half in range(2):
                ps = P.ps()
                for i in range(4):
                    P.mm(ps[:, i * 128:(i + 1) * 128], Pn[half][:, i, :], TTm[half][:, i, :])
                TTn = TTr.next() if lvl < 6 else TTfin.next()
                P.tt(TTn[:].rearrange("p a b -> p (a b)"), ps[:], TTm[half][:].rearrange("p a b -> p (a b)"), ALU.add)
                TTm[half] = TTn
                if lvl < 6:
                    ps = P.ps()
                    for i in range(4):
                        P.tr(ps[:, i * 128:(i + 1) * 128], Pn[half][:, i, :], cs_(C, "ident"))
                    PTn = PTr.next()
                    P.cp(PTn[:].rearrange("p a b -> p (a b)"), ps[:], e="act")
                    PTm[half] = PTn
                Pm[half] = Pn[half]
        t["TT"] = TTm
        return t
        for half in range(2):
            def prod(lhs, rhs, mask, ring, addI=False):
                ps = P.ps()
                for i in range(4):
                    P.mm(ps[:, i * 128:(i + 1) * 128], t[lhs][hp(half), i, :], t[rhs][hp(half), i, :])
                o = ring.next()
                P.tt(o[:].rearrange("p a b -> p (a b)"), ps[:], mk[mask][:].rearrange("p a b -> p (a b)"), ALU.mult)
                return o
            Pm = prod("at", "bt", m_l, Pr)
            PTm = prod("bt", "at", m_lT, PTr)
            TTm = TTr.next()
            P.tt(TTm[:], PTm[:], id4[:], ALU.add)
            t["lak"].append(prod("kt", "at", m_lT, Mr["lak"]))
            t["nrb"].append(prod("bt", "rt", m_n, Mr["nrb"]))
            t["nrk"].append(prod("kt", "rt", m_n, Mr["nrk"]))
            if "PREP_PROD" in C.dbg:
                continue
            for lvl in range(1, 7):
                ps = P.ps()
                for i in range(4):
                    P.mm(ps[:, i * 128:(i + 1) * 128], PTm[:, i, :], Pm[:, i, :])
                Pn = Pr.next()
                P.cp(Pn[:].rearrange("p a b -> p (a b)"), ps[:], e="act")
                if lvl < 6:
                    ps = P.ps()
                    for i in range(4):
                        P.mm(ps[:, i * 128:(i + 1) * 128], Pm[:, i, :], PTm[:, i, :])
                    PTn = PTr.next()
                    P.cp(PTn[:].rearrange("p a b -> p (a b)"), ps[:], e="act")
                else:
                    PTn = None
                ps = P.ps()
                for i in range(4):
                    P.mm(ps[:, i * 128:(i + 1) * 128], Pn[:, i, :], TTm[:, i, :])
                TTn = TTr.next() if lvl < 6 else TTfin.next()
                P.tt(TTn[:].rearrange("p a b -> p (a b)"), ps[:], TTm[:].rearrange("p a b -> p (a b)"), ALU.add)
                Pm, PTm, TTm = Pn, PTn, TTn
            t["TT"].append(TTm)
        return t

    def state(d, ci, t):
        g0 = ci * 128
        hc = lambda h: slice(h * 64, (h + 1) * 64)
        psZ = P.ps()
        for h in range(8):
            P.mm(psZ[:, hc(h)], t["at"][:, h // 2, :], STz[:, h, :], start=True, stop=False)
            P.mm(psZ[:, hc(h)], t["lak"][h % 2][:, h // 2, :], t["v"][:, hc(h)], start=False, stop=True)
        zb = zr.next()
        P.cp(zb[:], psZ[:], e="act")
        psU = P.ps()
        for h in range(8):
            P.mm(psU[:, hc(h)], t["TT"][h % 2][:, h // 2, :], zb[:, hc(h)])
        ub = ur.next()
        P.cp(ub[:], psU[:], e="act")
        psS = P.ps()
        for p in range(4):
            pc = slice(p * 128, (p + 1) * 128)
            P.mm(psS[:, pc], t["btm"][:, pc], ub[:, pc], start=True, stop=False)
            P.mm(psS[:, pc], t["ktm"][:, pc], t["v"][:, pc], start=False, stop=True)
        psY = P.ps()
        for h in range(8):
            P.mm(psY[:, hc(h)], t["rt"][:, h // 2, :], STz[:, h, :], start=True, stop=False)
            P.mm(psY[:, hc(h)], t["nrb"][h % 2][:, h // 2, :], ub[:, hc(h)], start=False, stop=False)
            P.mm(psY[:, hc(h)], t["nrk"][h % 2][:, h // 2, :], t["v"][:, hc(h)], start=False, stop=True)
        for par in range(2):
            rows = slice(par * 64, par * 64 + 64)
            blk_ = psS[rows, :].rearrange("p (a b) -> p a b", b=128)[:, :, par * 64:(par + 1) * 64]
            P.tt(ST[rows, :, :], ST[rows, :, :], blk_, ALU.add)
        P.tt(ST[:], ST[:], bc(lam[:, :, ci:ci + 1], [128, 4, 64]), ALU.mult)
        for par in range(2):
            rows = slice(par * 64, par * 64 + 64)
            P.cp(STz[rows, par::2, :], ST[rows, :, :], e="act")
        y = yr.next()
        P.cp(y[:], psY[:], e="act")
        return lambda: finish(d, g0, y)

    def finish(d, g0, y):
        if d == 1:
            P.dma("pool", S["YRK"].ap()[g0:g0 + 128, :], y[:])
            return
        yb = yr.next()
        P.dma("sp", yb[:], S["YRK"].ap()[g0:g0 + 128, :])
        bon, gat = wk.next(), wk.next()
        P.dma("sp", bon[:], S["BONUS"].ap()[g0:g0 + 128, :])
        P.dma("sp", gat[:], S["GATE"].ap()[g0:g0 + 128, :])
        P.tt(y[:], y[:], yb[:], ALU.add)
        if "YRKSUM" in C.dbg:
            P.dma("pool", S["YRK"].ap()[g0:g0 + 128, :], y[:])
        s = sm.next()
        y3 = y[:].rearrange("p (a b) -> p a b", b=64)
        P.op("dve", "tensor_reduce", out=s[:, 0:8], in_=y3, axis=AX.X, op=ALU.add)
        P.ts(s[:, 0:8], s[:, 0:8], 1.0 / 64, ALU.mult)
        P.tt(y3, y3, bc(s[:, 0:8].unsqueeze(2), [128, 8, 64]), ALU.subtract)
        sq = wk.next()
        P.tt(sq[:], y[:], y[:], ALU.mult)
        P.op("dve", "tensor_reduce", out=s[:, 8:16], in_=sq[:].rearrange("p (a b) -> p a b", b=64), axis=AX.X, op=ALU.add)
        P.act(s[:, 16:24], s[:, 8:16], AF.Sqrt, scale=1.0 / 64, bias=64e-5)
        P.op("dve", "reciprocal", out=s[:, 24:32], in_=s[:, 16:24])
        P.tt(y3, y3, bc(s[:, 24:32].unsqueeze(2), [128, 8, 64]), ALU.mult)
        P.tt(y[:], y[:], lnw[:], ALU.mult)
        P.tt(y[:], y[:], lnb[:], ALU.add)
        P.tt(y[:], y[:], bon[:], ALU.add)
        o = ob.next()
        P.tt(o[:], y[:], gat[:], ALU.mult)
        P.dma("pool", S["MIX"].ap()[g0:g0 + 128, 0:512], o[:])

    yield
    for d in (1, 0):
        if d == 0:
            P.barrier()
        order = list(range(NCH)) if d == 0 else (list(range(NCC - 1, -1, -1)) + list(range(NCH - 1, NCC - 1, -1)))
        P.dma("sp", lam[:], S["LAMC" + "fb"[d]].ap())
        P.op("dve", "memset", ap=ST[:], constant=0.0)
        P.op("dve", "memset", ap=STz[:], constant=0.0)
        nxt = prep(d, order[0])
        fin = None
        for i, ci in enumerate(order):
            cur = nxt
            if i + 1 < len(order):
                nxt = prep(d, order[i + 1])
            f2 = state(d, ci, cur)
            if fin is not None:
                fin()
            fin = f2
            yield
        fin()


def phase_ssd_prep(C):
    P, I, S, T, TT = C.P, C.I, C.S, C.T, C.TT
    W = 512
    tiles = token_tiles(T, W)
    identf = cs_(C, "ident")
    with P.phase():
        negA = P.sb("negA", [16, 1])
        P.act(negA[:], pv_(C, "a_log")[0:16, :], AF.Exp)
        P.ts(negA[:], negA[:], -1.0, ALU.mult)
        Zr = P.pool_of("pZ", [128, 4, W], F32, 2)
        Xr = P.pool_of("pX", [128, 4, W], F32, 2)
        Br = P.pool_of("pB", [128, 4, W], F32, 2)
        Dr = P.pool_of("pD", [16, W], F32, 2)
        D2 = P.pool_of("pD2", [16, 2, W], F32, 2)
        bcb = P.pool_of("pbc", [128, 4, W], BF16, 2)
        stf = P.pool_of("pstf", [128, 512], F32, 4)
        stb = P.pool_of("pstb", [128, 256], BF16, 3)
        std = P.pool_of("pstd", [128, 32], F32, 3)
        for (g0, n) in tiles:
            nch = n // 128
            Z, X, B, Dt, dd = Zr.next(), Xr.next(), Br.next(), Dr.next(), D2.next()
            P.dma("sp", Z[:, :, 0:n], f0rows(C, 1920, 4, g0, n))
            P.dma("sp", X[:, :, 0:n], f0rows(C, 2432, 4, g0, n))
            P.dma("sp", B[:, :, 0:n], f0rows(C, 2944, 4, g0, n))
            P.dma("sp", Dt[:, 0:n], S["F0"].ap()[3456:3472, g0:g0 + n])
            P.act(Dt[:, 0:n], Dt[:, 0:n], AF.Exp, bias=pv_(C, "dt_bias")[0:16, :])
            P.act(dd[:, 0, 0:n], Dt[:, 0:n], AF.Ln, bias=1.0)
            P.ts(dd[:, 1, 0:n], dd[:, 0, 0:n], negA[:, 0:1], ALU.mult)
            bb = bcb.next()
            P.cp(bb[:, :, 0:n], B[:, :, 0:n], e="act")
            P.dma("pool", S["BCT"].ap()[:, :, g0:g0 + n], bb[:, :, 0:n])
            for k in range(nch):
                gk = g0 + k * 128
                ks = slice(k * 128, (k + 1) * 128)
                for (src, dst) in ((X, "XSTM"), (Z, "ZTM")):
                    ps = P.ps()
                    for c in range(4):
                        P.tr(ps[:, c * 128:(c + 1) * 128], src[:, c, ks], identf)
                    st = stf.next()
                    P.cp(st[:], ps[:], e="act")
                    P.dma("pool", S[dst].ap()[gk:gk + 128, :], st[:])
                ps = P.ps()
                psb = ps[:].bitcast(BF16)
                for c in range(2):
                    P.tr(psb[:, c * 128:(c + 1) * 128], bb[:, c, ks], C.identb[:])
                st = stb.next()
                P.cp(st[:], psb[:, 0:256], e="act")
                P.dma("pool", S["BMTM"].ap()[gk:gk + 128, :], st[:])
                ps = P.ps()
                for j in range(2):
                    P.tr(ps[:, j * 16:(j + 1) * 16], dd[:, j, ks], identf[0:16, 0:16])
                st = std.next()
                P.cp(st[:], ps[:, 0:32], e="act")
                P.dma("pool", S["DTTM"].ap()[gk:gk + 128, :], st[:])


def gen_ssd_scan(C):
    P, I, S, T, TT = C.P, C.I, C.S, C.T, C.TT
    NCH = TT // 128
    NCC = TC // 128
    id8 = P.sb("id8", [128, 8, 128])
    neg8 = {d: P.sb("neg8", [128, 8, 128]) for d in (0, 1)}
    for i in range(8):
        P.cp(id8[:, i, :], cs_(C, "ident"))
        P.cp(neg8[0][:, i, :], cs_(C, "negf"), e="act")
        P.cp(neg8[1][:, i, :], cs_(C, "negb"), e="act")
    ones = cs_(C, "ones")
    Dv = P.sb("Dv", [128, 8])
    P.dma("sp", Dv[:], I["ssd_d"].ap()[0].partition_broadcast(128))
    nw = P.sb("nw", [128, 512])
    bcast_row(C, nw[:], I["ssd_norm_w"].ap()[0])
    H = P.sb("H", [128, 8, 64])
    Hb = P.sb("Hb", [128, 8, 64], BF16)
    xsr = P.pool_of("dxs", [128, 512], F32, 4)
    bmr = P.pool_of("dbm", [128, 256], BF16, 3)
    bcr = P.pool_of("dbc", [128, 4, 128], BF16, 3)
    dtr = P.pool_of("ddt", [128, 32], F32, 3)
    smr = P.pool_of("dsm", [128, 48], F32, 6)
    dgr = P.pool_of("ddg", [128, 8, 128], F32, 2)
    segr = P.pool_of("dseg", [128, 8, 128], F32, 2)
    cbr = P.pool_of("dcb", [128, 2, 128], F32, 2)
    wtr = P.pool_of("dwt", [128, 8, 128], BF16, 2)
    xdr = P.pool_of("dxd", [128, 8, 64], BF16, 3)
    xer = P.pool_of("dxe", [128, 8, 64], BF16, 3)
    wk = P.pool_of("dwk", [128, 512], F32, 11)
    ob = P.pool_of("dob", [128, 512], BF16, 2)
    junk = P.sb("djunk", [128, 512], BF16)

    def prep(d, ci):
        g0 = ci * 128
        do = 8 * d
        lend = 127 if d == 0 else 0
        t = {}
        xs, bm, bct, dtt, sm = xsr.next(), bmr.next(), bcr.next(), dtr.next(), smr.next()
        P.dma("sp", xs[:], S["XSTM"].ap()[g0:g0 + 128, :])
        P.dma("sp", bm[:], S["BMTM"].ap()[g0:g0 + 128, :])
        P.dma("sp", bct[:], S["BCT"].ap()[:, :, g0:g0 + 128])
        P.dma("sp", dtt[:], S["DTTM"].ap()[g0:g0 + 128, :])
        dt, dtA = dtt[:, do:do + 8], dtt[:, 16 + do:16 + do + 8]
        if "SS0" in C.dbg:
            return t
        ps = P.ps()
        P.mm(ps[:, 0:8], cs_(C, "le" if d == 0 else "ge"), dtA)
        cs = sm[:, 0:8]
        P.cp(cs, ps[:, 0:8])
        if "SS0b" in C.dbg:
            return t
        dg = dgr.next()
        P.tt(dg[:], id8[:], bc(cs.unsqueeze(2), [128, 8, 128]), ALU.mult)
        if "SS0c" in C.dbg:
            return t
        seg = segr.next()
        P.tt(seg[:], neg8[d][:], bc(cs.unsqueeze(2), [128, 8, 128]), ALU.subtract)
        eend = sm[:, 16:24]
        if "SS1" in C.dbg:
            return t
        for hf in range(2):
            psr = P.ps()
            P.mm(psr[:], ones, dg[:, hf * 4:(hf + 1) * 4, :].rearrange("p a b -> p (a b)"))
            P.tt(seg[:, hf * 4:(hf + 1) * 4, :].rearrange("p a b -> p (a b)"), seg[:, hf * 4:(hf + 1) * 4, :].rearrange("p a b -> p (a b)"), psr[:], ALU.add)
            P.act(eend[:, hf * 4:(hf + 1) * 4], psr[:].rearrange("p (a b) -> p a b", b=128)[:, :, lend], AF.Exp)
        P.act(seg[:].rearrange("p a b -> p (a b)"), seg[:].rearrange("p a b -> p (a b)"), AF.Exp)
        if "SS2" in C.dbg:
            return t
        ecs = sm[:, 8:16]
        P.act(ecs, cs, AF.Exp)
        psc = P.ps()
        for g in range(2):
            P.mm(psc[:, g * 128:(g + 1) * 128], bct[:, g, :], bct[:, 2 + g, :])
        cb = cbr.next()
        P.cp(cb[:].rearrange("p a b -> p (a b)"), psc[:, 0:256], e="act")
        wt = wtr.next()
        for g in range(2):
            P.tt(wt[:, g * 4:(g + 1) * 4, :], seg[:, g * 4:(g + 1) * 4, :], bc(cb[:, g:g + 1, :], [128, 4, 128]), ALU.mult)
        if "SS3" in C.dbg:
            return t
        xdt = xdr.next()
        xs3 = xs[:].rearrange("p (a b) -> p a b", b=64)
        P.tt(xdt[:], xs3, bc(dt.unsqueeze(2), [128, 8, 64]), ALU.mult)
        sc = sm[:, 24:32]
        P.tt(sc, dt, seg[:, :, lend], ALU.mult)
        xdd = xer.next()
        P.tt(xdd[:], xs3, bc(sc.unsqueeze(2), [128, 8, 64]), ALU.mult)
        psY = P.ps()
        for h in range(8):
            P.mm(psY[:, h * 64:(h + 1) * 64], wt[:, h, :], xdt[:, h, :])
        yi = wk.next()
        P.cp(yi[:], psY[:], e="act")
        t.update(xs=xs, bm=bm, bct=bct, ecs=ecs, eend=eend, xdd=xdd, yi=yi)
        return t

    def state(d, ci, t):
        g0 = ci * 128
        psI = P.ps()
        for g in range(2):
            P.mm(psI[:, g * 256:(g + 1) * 256], t["bct"][:, 2 + g, :], Hb[:, g * 4:(g + 1) * 4, :].rearrange("p a b -> p (a b)"))
        psH = P.ps()
        for g in range(2):
            P.mm(psH[:, g * 256:(g + 1) * 256], t["bm"][:, g * 128:(g + 1) * 128], t["xdd"][:, g * 4:(g + 1) * 4, :].rearrange("p a b -> p (a b)"))
        P.tt(H[:], H[:], bc(t["eend"].unsqueeze(2), [128, 8, 64]), ALU.mult)
        P.tt(H[:], H[:], psH[:].rearrange("p (a b) -> p a b", b=64), ALU.add)
        P.cp(Hb[:], H[:], e="act")
        y = wk.next()
        P.cp(y[:], psI[:], e="act")
        return lambda: finish(d, g0, t, y)

    def finish(d, g0, t, y):
        y3 = y[:].rearrange("p (a b) -> p a b", b=64)
        P.tt(y3, y3, bc(t["ecs"].unsqueeze(2), [128, 8, 64]), ALU.mult)
        P.tt(y[:], y[:], t["yi"][:], ALU.add)
        if d == 1:
            P.dma("pool", S["YSD"].ap()[g0:g0 + 128, :], y[:])
            return
        yb, z = wk.next(), wk.next()
        P.dma("sp", yb[:], S["YSD"].ap()[g0:g0 + 128, :])
        P.dma("sp", z[:], S["ZTM"].ap()[g0:g0 + 128, :])
        P.tt(y[:], y[:], yb[:], ALU.add)
        if "YSDSUM" in C.dbg:
            P.dma("pool", S["YSD"].ap()[g0:g0 + 128, :], y[:])
        P.tt(yb[:].rearrange("p (a b) -> p a b", b=64), t["xs"][:].rearrange("p (a b) -> p a b", b=64),
             bc(Dv[:].unsqueeze(2), [128, 8, 64]), ALU.mult)
        P.tt(y[:], y[:], yb[:], ALU.add)
        P.act(z[:], z[:], AF.Silu)
        P.tt(y[:], y[:], z[:], ALU.mult)
        s = smr.next()
        P.act(junk[:], y[:], AF.Square, accum_out=s[:, 0:1])
        P.act(s[:, 1:2], s[:, 0:1], AF.Sqrt, scale=1.0 / 512, bias=1e-6)
        P.op("dve", "reciprocal", out=s[:, 2:3], in_=s[:, 1:2])
        o = ob.next()
        P.stt(o[:], y[:], s[:, 2:3], nw[:], ALU.mult, ALU.mult)
        P.dma("pool", S["MIX"].ap()[g0:g0 + 128, 512:1024], o[:])

    yield
    for d in (1, 0):
        if d == 0:
            P.barrier()
        order = list(range(NCH)) if d == 0 else (list(range(NCC - 1, -1, -1)) + list(range(NCH - 1, NCC - 1, -1)))
        P.op("dve", "memset", ap=H[:], constant=0.0)
        P.op("dve", "memset", ap=Hb[:], constant=0.0)
        nxt = prep(d, order[0])
        fin = None
        for i, ci in enumerate(order):
            cur = nxt
            if i + 1 < len(order):
                nxt = prep(d, order[i + 1])
            f2 = state(d, ci, cur)
            if fin is not None:
                fin()
            fin = f2
            yield
        fin()


def load_weight_bf16(C, dst, src_view, nk, ncols):
    P = C.P
    ring = P.pool_of("wst", [128, ncols], F32, 2)
    for k in range(nk):
        st = ring.next()
        P.dma("sp", st[:], src_view[:, k, :])
        P.cp(dst[:, k, :], st[:], e="pool")


def xsrc(C, first, g0):
    if first:
        return C.I["ctx"].ap()[g0:g0 + 128, :] if g0 < TC else C.I["x"].ap()[g0 - TC:g0 - TC + 128, :]
    return C.S["xcur"].ap()[g0:g0 + 128, :]


def phase_outproj(C, layer, mixcols, wname, first, lat_only):
    P, I, S, T, TT = C.P, C.I, C.S, C.T, C.TT
    nk = mixcols // 128
    with P.phase():
        w = P.sb("wout", [128, nk, D], BF16)
        load_weight_bf16(C, w, I[wname].ap()[0].rearrange("(kc p) n -> p kc n", p=128), nk, D)
        GA = {}
        for ty in range(2):
            GA[ty] = P.sb("GA", [128, D])
            load_mod(C, layer, ty, 2, GA[ty][:])
        G2, SH2 = {}, {}
        g2 = P.sb("ng2", [128, D])
        P.dma("sp", g2[:], I["norm2_g"].ap()[layer].partition_broadcast(128))
        for ty in range(2):
            G2[ty] = P.sb("G2", [128, D])
            SH2[ty] = P.sb("SH2", [128, D])
            load_mod(C, layer, ty, 4, G2[ty][:])
            load_mod(C, layer, ty, 3, SH2[ty][:])
            P.stt(G2[ty][:], G2[ty][:], 1.0, g2[:], ALU.add, ALU.mult)
        njunk = P.sb("onjunk", [128, D], BF16)
        xnr = P.pool_of("oxn", [128, D], F32, 2)
        hr = P.pool_of("onh", [128, D], BF16, 2)
        hTr = P.pool_of("onhT", [128, 8, 128], BF16, 3)
        nst = P.pool_of("onst", [128, 4], F32, 3)
        mr = P.pool_of("omix", [128, mixcols], BF16, 2)
        mtr = P.pool_of("omixT", [128, nk, 128], BF16, 2)
        xr = P.pool_of("ox", [128, D], F32, 3)
        tr_ = P.pool_of("otmp", [128, 512], F32, 3)
        def emit_hT(h, g0):
            ps = P.ps()
            psb = ps[:].bitcast(BF16)
            for kc in range(8):
                P.tr(psb[:, kc * 128:(kc + 1) * 128], h[:, kc * 128:(kc + 1) * 128], C.identb[:])
            hT = hTr.next()
            P.cp(hT[:].rearrange("p a b -> p (a b)"), psb, e="act")
            P.dma("pool", hT_view(C, g0, 128), hT[:])

        pend = None
        for i in range(TT // 128):
            g0 = i * 128
            ty = 1 if g0 < TC else 0
            if lat_only and ty == 1:
                continue
            m = mr.next()
            P.dma("sp", m[:], S["MIX"].ap()[g0:g0 + 128, 0:mixcols])
            x = xr.next()
            P.dma("sp", x[:], xsrc(C, first, g0))
            mT = mtr.next()
            for b in range(nk // 8):
                ps = P.ps()
                psb = ps[:].bitcast(BF16)
                for kc in range(8):
                    P.tr(psb[:, kc * 128:(kc + 1) * 128], m[:, (b * 8 + kc) * 128:(b * 8 + kc + 1) * 128], C.identb[:])
                P.cp(mT[:, b * 8:(b + 1) * 8, :].rearrange("p a b -> p (a b)"), psb, e="act")
            pss = []
            for hf in range(2):
                ps = P.ps()
                for kc in range(nk):
                    P.mm(ps[:], mT[:, kc, :], w[:, kc, hf * 512:(hf + 1) * 512], start=(kc == 0), stop=(kc == nk - 1))
                pss.append(ps)
            if pend is not None:
                emit_hT(*pend)
            for hf in range(2):
                ps = pss[hf]
                tmp = tr_.next()
                P.tt(tmp[:], ps[:], GA[ty][:, hf * 512:(hf + 1) * 512], ALU.mult)
                P.tt(x[:, hf * 512:(hf + 1) * 512], x[:, hf * 512:(hf + 1) * 512], tmp[:], ALU.add)
            P.dma("pool", S["xcur"].ap()[g0:g0 + 128, :], x[:])
            sn = nst.next()
            P.act(njunk[:], x[:], AF.Square, accum_out=sn[:, 0:1])
            P.act(sn[:, 1:2], sn[:, 0:1], AF.Sqrt, scale=1.0 / D, bias=1e-6)
            P.op("dve", "reciprocal", out=sn[:, 2:3], in_=sn[:, 1:2])
            xn = xnr.next()
            P.stt(xn[:], x[:], sn[:, 2:3], G2[ty][:], ALU.mult, ALU.mult)
            h = hr.next()
            P.tt(h[:], xn[:], SH2[ty][:], ALU.add)
            pend = (h, g0)
        emit_hT(*pend)


GELU_C = 1.5957691216057308


def phase_ffn_up(C, layer, lat_only):
    P, I, S, T, TT = C.P, C.I, C.S, C.T, C.TT
    R = T // GW
    RP = min(64, max(R // 2, 1))
    parts = [] if lat_only else [("ctx", 0, 0)]
    parts += [("lat", r0, min(r0 + RP, R)) for r0 in range(0, R, RP)]
    ML = (RP + 2) * GW
    cwn, cbn = f"ffn_cw{layer}", f"ffn_cb{layer}"
    with P.phase():
        hp_ = P.sb("uhT", [128, 8, ML], BF16)
        gr = P.pool_of("grow", [128, ML + 2], F32, 2)
        for g_ in gr.items:
            P.op("pool", "memset", ap=g_[:], constant=0.0)
        acc = P.sb("gacc", [128, ML + 2])
        gb = P.sb("ggelu", [128, ML], BF16)
        vr = P.pool_of("vrow", [128, ML], BF16, 2)
        wfr = P.pool_of("uwf", [128, 8, 256], F32, 2)
        wbr = P.pool_of("uwb", [128, 8, 256], BF16, 2)
        wv = I["ffn_w_up"].ap()[layer].rearrange("(kc p) n -> p kc n", p=128)

        def loadw(j):
            wf, wb = wfr.next(), wbr.next()
            P.dma("sp", wf[:, :, 0:128], wv[:, :, j * 128:(j + 1) * 128])
            P.dma("sp", wf[:, :, 128:256], wv[:, :, 2816 + j * 128:2816 + (j + 1) * 128])
            P.cp(wb[:], wf[:], e="pool")
            return wb

        for (kind, R0, R1) in parts:
            if kind == "ctx":
                gA, gB = 0, TC
                oA, oB = 0, TC
            else:
                Rb0, Rb1 = max(R0 - 1, 0), min(R1 + 1, R)
                gA, gB = TC + Rb0 * GW, TC + Rb1 * GW
                oA, oB = TC + R0 * GW, TC + R1 * GW
            NL = gB - gA
            g = gA
            while g < gB:
                tile_end = TC if g < TC else TC + ((g - TC) // 512 + 1) * 512
                n = min(gB, tile_end) - g
                P.dma("sp", hp_[:, :, g - gA:g - gA + n], hT_view(C, g, n))
                g += n
            o0, oL = oA - gA, oB - oA

            def head(j, wb, vrow, grow):
                for t0 in range(0, NL, 512):
                    n = min(512, NL - t0)
                    psA, psB = P.ps(), P.ps()
                    for kc in range(8):
                        P.mm(psA[:, 0:n], wb[:, kc, 0:128], hp_[:, kc, t0:t0 + n], start=(kc == 0), stop=(kc == 7))
                    for kc in range(8):
                        P.mm(psB[:, 0:n], wb[:, kc, 128:256], hp_[:, kc, t0:t0 + n], start=(kc == 0), stop=(kc == 7))
                    P.cp(grow[:, 1 + t0:1 + t0 + n], psA[:, 0:n], e="act")
                    P.cp(vrow[:, t0:t0 + n], psB[:, 0:n], e="act")

            def conv(j, grow):
                cw = lambda tap: pv_(C, cwn, tap * 22 + j)
                P.act(acc[:, 1 + o0:1 + o0 + oL], grow[:, 1 + o0:1 + o0 + oL], AF.Identity, scale=cw(4), bias=pv_(C, cbn, j))
                if kind == "ctx":
                    P.stt(acc[:, 1:1 + TC], grow[:, 0:TC], cw(3), acc[:, 1:1 + TC], ALU.mult, ALU.add)
                    P.stt(acc[:, 1:1 + TC], grow[:, 2:2 + TC], cw(5), acc[:, 1:1 + TC], ALU.mult, ALU.add)
                else:
                    nb = Rb1 - Rb0
                    gin = grow[:, 1:1 + nb * GW].rearrange("p (r w) -> p r w", w=GW)
                    gac = acc[:, 1:1 + nb * GW].rearrange("p (r w) -> p r w", w=GW)
                    for di in (-1, 0, 1):
                        for dj in (-1, 0, 1):
                            if di == 0 and dj == 0:
                                continue
                            rl, rh = max(R0, -di) - Rb0, min(R1, R - di) - Rb0
                            c0, c1 = max(0, -dj), GW - max(0, dj)
                            P.stt(gac[:, rl:rh, c0:c1], gin[:, rl + di:rh + di, c0 + dj:c1 + dj], cw((di + 1) * 3 + dj + 1),
                                  gac[:, rl:rh, c0:c1], ALU.mult, ALU.add)

            def tail(j, vrow):
                P.act(gb[:, o0:o0 + oL], acc[:, 1 + o0:1 + o0 + oL], AF.Gelu_apprx_tanh)
                P.tt(vrow[:, o0:o0 + oL], gb[:, o0:o0 + oL], vrow[:, o0:o0 + oL], ALU.mult)
                P.dma("pool", S["ACTT"].ap()[:, j, oA:oB], vrow[:, o0:o0 + oL])

            wnext = loadw(0)
            pend = None
            for j in range(22):
                wb = wnext
                if j + 1 < 22:
                    wnext = loadw(j + 1)
                vrow, grow = vr.next(), gr.next()
                head(j, wb, vrow, grow)
                if pend is not None:
                    tail(*pend)
                conv(j, grow)
                pend = (j, vrow)
            tail(*pend)


def phase_ffn_down(C, layer, lat_only, final):
    P, I, S, T, TT = C.P, C.I, C.S, C.T, C.TT
    tiles = [t for t in token_tiles(T) if not (lat_only and t[0] < TC)]
    with P.phase():
        w = P.sb("wdn", [128, 22, D], BF16)
        load_weight_bf16(C, w, I["ffn_w_down"].ap()[layer].rearrange("(kc p) n -> p kc n", p=128), 22, D)
        GA = {}
        for ty in range(2):
            GA[ty] = P.sb("GA2", [128, D])
            load_mod(C, layer, ty, 5, GA[ty][:])
        if final:
            gf = P.sb("gfin", [128, D])
            bcast_row(C, gf[:], I["final_norm_g"].ap())
            junk = P.sb("fjunk", [128, D], BF16)
            st = P.pool_of("fst", [128, 4], F32, 3)
        ar = P.pool_of("dact", [128, 22, 512], BF16, 2)
        xr = P.pool_of("dx", [128, D], F32, 3)
        tr_ = P.pool_of("dtmp", [128, 512], F32, 3)
        for (g0, n) in tiles:
            a = ar.next()
            P.dma("sp", a[:, :, 0:n], S["ACTT"].ap()[:, :, g0:g0 + n])
            for sub in range(n // 128):
                gs = g0 + sub * 128
                ty = 1 if gs < TC else 0
                x = xr.next()
                P.dma("sp", x[:], S["xcur"].ap()[gs:gs + 128, :])
                for hf in range(2):
                    ps = P.ps()
                    for cc in range(22):
                        P.mm(ps[:], a[:, cc, sub * 128:(sub + 1) * 128], w[:, cc, hf * 512:(hf + 1) * 512],
                             start=(cc == 0), stop=(cc == 21))
                    tmp = tr_.next()
                    P.tt(tmp[:], ps[:], GA[ty][:, hf * 512:(hf + 1) * 512], ALU.mult)
                    P.tt(x[:, hf * 512:(hf + 1) * 512], x[:, hf * 512:(hf + 1) * 512], tmp[:], ALU.add)
                if final:
                    s = st.next()
                    P.act(junk[:], x[:], AF.Square, accum_out=s[:, 0:1])
                    P.act(s[:, 1:2], s[:, 0:1], AF.Sqrt, scale=1.0 / D, bias=1e-6)
                    P.op("dve", "reciprocal", out=s[:, 2:3], in_=s[:, 1:2])
                    P.stt(x[:], x[:], s[:, 2:3], gf[:], ALU.mult, ALU.mult)
                    P.dma("pool", C.out.ap()[gs - TC:gs - TC + 128, :], x[:])
                else:
                    P.dma("pool", S["xcur"].ap()[gs:gs + 128, :], x[:])


def phase_ret_proj(C):
    P, I, S, T, TT = C.P, C.I, C.S, C.T, C.TT
    tiles = token_tiles(T)
    with P.phase():
        w = P.sb("wret", [128, 8, 6144], BF16)
        wv = I["ret_w_in"].ap()[0].rearrange("(kc p) n -> p kc n", p=128)
        ring = P.pool_of("wst", [128, 8, 256], F32, 2)
        for nb in range(24):
            st = ring.next()
            P.dma("sp", st[:], wv[:, :, nb * 256:(nb + 1) * 256])
            P.cp(w[:, :, nb * 256:(nb + 1) * 256], st[:], e="pool")
        htr = P.pool_of("rht", [128, 8, 512], BF16, 2)
        qfr = P.pool_of("rqf", [128, 1024], F32, 3)
        qbr = P.pool_of("rqb", [128, 1024], BF16, 4)
        tmr = P.pool_of("rtm", [128, 8, 64], F32, 4)
        rpr = P.pool_of("rrp", [128, 128], F32, 2)
        vbr = P.pool_of("rvb", [128, 2048], BF16, 3)
        str_ = P.pool_of("rst", [128, 8, 128], BF16, 4)
        for (g0, n) in tiles:
            ht = htr.next()
            P.dma("sp", ht[:, :, 0:n], hT_view(C, g0, n))
            for sub in range(n // 128):
                gs = g0 + sub * 128
                lat = gs >= TC
                if lat:
                    rp = rpr.next()
                    P.dma("sp", rp[:], I["rope"].ap()[gs - TC:gs - TC + 128, :])
                    cos = bc(rp[:, 0:64].unsqueeze(1), [128, 8, 64])
                    sin = bc(rp[:, 64:128].unsqueeze(1), [128, 8, 64])

                def proj(nb0, dst, scale=None, func=None):
                    for b in range(dst.shape[1] // 512):
                        ps = P.ps()
                        for kc in range(8):
                            P.mm(ps[:], ht[:, kc, sub * 128:(sub + 1) * 128], w[:, kc, (nb0 + b) * 512:(nb0 + b + 1) * 512],
                                 start=(kc == 0), stop=(kc == 7))
                        o = dst[:, b * 512:(b + 1) * 512]
                        if func is not None:
                            P.act(o, ps[:], func)
                        elif scale is not None:
                            P.act(o, ps[:], AF.Copy, scale=scale)
                        else:
                            P.cp(o, ps[:], e="act")
                pending = []
                for (nb0, scale, tname, tmname) in ((0, None, "QT", None), (2, 128 ** -0.5, "KT1", "KTM1")):
                    qb = qbr.next()
                    if lat:
                        qf = qfr.next()
                        proj(nb0, qf[:], scale=scale)
                        q3 = qf[:].rearrange("p (h d) -> p h d", d=128)
                        o3 = qb[:].rearrange("p (h d) -> p h d", d=128)
                        x1, x2 = q3[:, :, 0:64], q3[:, :, 64:128]
                        t1, t2 = tmr.next(), tmr.next()
                        P.tt(t1[:], x1, cos, ALU.mult)
                        P.tt(t2[:], x2, sin, ALU.mult)
                        P.tt(o3[:, :, 0:64], t1[:], t2[:], ALU.subtract)
                        t3, t4 = tmr.next(), tmr.next()
                        P.tt(t3[:], x2, cos, ALU.mult)
                        P.tt(t4[:], x1, sin, ALU.mult)
                        P.tt(o3[:, :, 64:128], t3[:], t4[:], ALU.add)
                    else:
                        proj(nb0, qb[:], scale=scale)
                    pending.append((qb, tname, tmname))
                vb = vbr.next()
                proj(4, vb[:])
                P.dma("pool", S["VTM1"].ap()[gs:gs + 128, :], vb[:])
                if lat:
                    gb = vbr.next()
                    proj(8, gb[:], func=AF.Silu)
                    P.dma("pool", S["GS1"].ap()[gs:gs + 128, :], gb[:])
                for (qb, tname, tmname) in pending:
                    ps = P.ps()
                    psb = ps[:].bitcast(BF16)
                    for h in range(8):
                        P.tr(psb[:, h * 128:(h + 1) * 128], qb[:, h * 128:(h + 1) * 128], C.identb[:])
                    st = str_.next()
                    P.cp(st[:].rearrange("p a b -> p (a b)"), psb, e="act")
                    P.dma("pool", S[tname].ap()[:, :, gs:gs + 128], st[:])
                    if tmname:
                        P.dma("pool", S[tmname].ap()[gs:gs + 128, :], qb[:])


def phase_ret_scan(C):
    P, I, S, T, TT = C.P, C.I, C.S, C.T, C.TT
    NCH = TT // 128
    NCC = TC // 128
    LN2 = 0.6931471805599453
    with P.phase():
        Dm, GQ, GL = {}, {}, {}
        for d, dn in enumerate("fb"):
            lg = P.sb("lg", [128, 8])
            P.dma("sp", lg[:], I["ret_log2_" + dn].ap()[0].partition_broadcast(128))
            P.act(lg[:], lg[:], AF.Exp, scale=-LN2)
            P.ts(lg[:], lg[:], -1.0, ALU.mult, 1.0, ALU.add)
            P.act(lg[:], lg[:], AF.Ln)
            Dm[d] = P.sb("Dm", [128, 8, 128])
            GQ[d] = P.sb("GQ", [128, 8, 128])
            GL[d] = P.sb("GL", [128, 8])
            for h in range(8):
                P.act(Dm[d][:, h, :], cs_(C, "relf" if d == 0 else "relb"), AF.Exp, scale=lg[:, h:h + 1])
                P.tt(Dm[d][:, h, :], Dm[d][:, h, :], cs_(C, "le" if d == 0 else "ge"), ALU.mult)
                P.act(GQ[d][:, h, :], cs_(C, "lp1" if d == 0 else "rl"), AF.Exp, scale=lg[:, h:h + 1])
            P.act(GL[d][:], lg[:], AF.Exp, scale=128.0)
        St = P.sb("rS", [128, 8, 256])
        Sb = P.sb("rSb", [128, 8, 256], BF16)
        qr = P.pool_of("sq", [128, 8, 128], BF16, 2)
        kr = P.pool_of("sk", [128, 8, 128], BF16, 2)
        kmr = P.pool_of("skm", [128, 1024], BF16, 2)
        vr = P.pool_of("sv", [128, 2048], BF16, 2)
        wtr = P.pool_of("swt", [128, 8, 128], BF16, 2)
        qdr = P.pool_of("sqd", [128, 8, 128], BF16, 2)
        ker = P.pool_of("ske", [128, 8, 128], BF16, 2)
        yr = P.pool_of("sy", [128, 2048], F32, 2)
        ybr = P.pool_of("syb", [128, 2048], F32, 2)
        gr = P.pool_of("sg", [128, 2048], BF16, 2)
        ob = P.pool_of("so", [128, 2048], BF16, 2)
        sm = P.pool_of("ssm", [128, 32], F32, 3)

        def prep(d, ci):
            g0 = ci * 128
            q, k, km, v = qr.next(), kr.next(), kmr.next(), vr.next()
            P.dma("sp", q[:], S["QT"].ap()[:, :, g0:g0 + 128])
            P.dma("sp", k[:], S["KT1"].ap()[:, :, g0:g0 + 128])
            P.dma("sp", km[:], S["KTM1"].ap()[g0:g0 + 128, :])
            P.dma("sp", v[:], S["VTM1"].ap()[g0:g0 + 128, :])
            lend = 127 if d == 0 else 0
            t = dict(v=v)
            ke = ker.next()
            P.tt(ke[:], km[:].rearrange("p (h d) -> p h d", d=128), bc(Dm[d][:, :, lend:lend + 1], [128, 8, 128]), ALU.mult, e="pool")
            t["ke"] = ke
            if ci >= NCC:
                wt = wtr.next()
                for hf in range(2):
                    ps = P.ps()
                    for i in range(4):
                        h = hf * 4 + i
                        P.mm(ps[:, i * 128:(i + 1) * 128], k[:, h, :], q[:, h, :])
                    P.tt(wt[:, hf * 4:(hf + 1) * 4, :].rearrange("p a b -> p (a b)"), ps[:],
                         Dm[d][:, hf * 4:(hf + 1) * 4, :].rearrange("p a b -> p (a b)"), ALU.mult)
                qd = qdr.next()
                P.tt(qd[:], q[:], GQ[d][:], ALU.mult, e="pool")
                t.update(wt=wt, qd=qd)
            return t

        def state(d, ci, t):
            g0 = ci * 128
            v = t["v"]
            if ci >= NCC:
                y = yr.next()
                for b in range(4):
                    ps = P.ps()
                    for j in range(2):
                        h = 2 * b + j
                        P.mm(ps[:, j * 256:(j + 1) * 256], t["wt"][:, h, :], v[:, h * 256:(h + 1) * 256], start=True, stop=False)
                        P.mm(ps[:, j * 256:(j + 1) * 256], t["qd"][:, h, :], Sb[:, h, :], start=False, stop=True)
                    P.cp(y[:, b * 512:(b + 1) * 512], ps[:], e="act")
            for b in range(4):
                ps = P.ps()
                for j in range(2):
                    h = 2 * b + j
                    P.mm(ps[:, j * 256:(j + 1) * 256], t["ke"][:, h, :], v[:, h * 256:(h + 1) * 256])
                for j in range(2):
                    h = 2 * b + j
                    P.stt(St[:, h, :], St[:, h, :], GL[d][:, h:h + 1], ps[:, j * 256:(j + 1) * 256], ALU.mult, ALU.add)
            P.cp(Sb[:], St[:], e="act")
            if ci < NCC:
                return
            if d == 1:
                P.dma("pool", S["YRET"].ap()[g0:g0 + 128, :], y[:])
                return
            yb, gs = ybr.next(), gr.next()
            P.dma("sp", yb[:], S["YRET"].ap()[g0:g0 + 128, :])
            P.dma("sp", gs[:], S["GS1"].ap()[g0:g0 + 128, :])
            P.tt(y[:], y[:], yb[:], ALU.add)
            if "YRETSUM" in C.dbg:
                P.dma("pool", S["YRET"].ap()[g0:g0 + 128, :], y[:])
            P.tt(yb[:], y[:], y[:], ALU.mult)
            s = sm.next()
            P.op("dve", "tensor_reduce", out=s[:, 0:8], in_=yb[:].rearrange("p (a b) -> p a b", b=256), axis=AX.X, op=ALU.add)
            P.act(s[:, 8:16], s[:, 0:8], AF.Sqrt, scale=1.0 / 256, bias=1e-6)
            P.op("dve", "reciprocal", out=s[:, 16:24], in_=s[:, 8:16])
            y3 = y[:].rearrange("p (a b) -> p a b", b=256)
            P.tt(y3, y3, bc(s[:, 16:24].unsqueeze(2), [128, 8, 256]), ALU.mult)
            o = ob.next()
            P.tt(o[:], y[:], gs[:], ALU.mult)
            P.dma("pool", S["MIX"].ap()[g0:g0 + 128, :], o[:])

        for d in (1, 0):
            if d == 0:
                P.barrier()
            order = list(range(NCH)) if d == 0 else (list(range(NCC - 1, -1, -1)) + list(range(NCH - 1, NCC - 1, -1)))
            P.op("dve", "memset", ap=St[:], constant=0.0)
            P.op("dve", "memset", ap=Sb[:], constant=0.0)
            nxt = prep(d, order[0])
            for i, ci in enumerate(order):
                cur = nxt
                if i + 1 < len(order):
                    nxt = prep(d, order[i + 1])
                state(d, ci, cur)


def phase_scans0(C):
    P = C.P
    with P.phase():
        gens = [gen_rwkv_scan(C), gen_ssd_scan(C)]
        while gens:
            for g in list(gens):
                try:
                    next(g)
                except StopIteration:
                    gens.remove(g)
```
